# Optimizing a Trainium2 kernel written in Bass

```python
import jax, jax.numpy as jnp
from jax import lax
import numpy as np

D_MODEL = 1024
BATCH = 8
SEQ = 2048
DEPTH = 4

N_A = DEPTH // 2
N_B = DEPTH - N_A
N_HEADS = 16
HEAD_DIM = D_MODEL // N_HEADS
D_FF = 2816
CONV_WIDTH = 31
Q_BLOCK = 128
RMS_EPS = 1e-6
LN_EPS = 1e-5
HALF_STEP = 0.5

kernel_name = "conformer_conv_stickbreaking_yoco"


def rms_norm(x, g):
    xf = x.astype(jnp.float32)
    y = xf * lax.rsqrt(jnp.mean(xf * xf, axis=-1, keepdims=True) + RMS_EPS)
    return (y * g.astype(jnp.float32)).astype(x.dtype)


def layer_norm(x, g, b):
    xf = x.astype(jnp.float32)
    mu = jnp.mean(xf, axis=-1, keepdims=True)
    xc = xf - mu
    var = jnp.mean(xc * xc, axis=-1, keepdims=True)
    y = xc * lax.rsqrt(var + LN_EPS) * g.astype(jnp.float32) + b.astype(jnp.float32)
    return y.astype(x.dtype)


def swiglu_ffn(x, w_gate, w_up, w_down):
    return (jax.nn.silu(x @ w_gate) * (x @ w_up)) @ w_down


def conformer_conv(x, w_pw1, b_pw1, w_dw, b_dw, ln_g, ln_b, w_pw2, b_pw2):
    h = x @ w_pw1 + b_pw1
    a, gate = jnp.split(h, 2, axis=-1)
    h = a * jax.nn.sigmoid(gate)
    h = lax.conv_general_dilated(
        h, w_dw[:, None, :].astype(h.dtype),
        window_strides=(1,), padding=[(CONV_WIDTH - 1, 0)],
        dimension_numbers=('NWC', 'WIO', 'NWC'),
        feature_group_count=D_MODEL) + b_dw
    h = jax.nn.silu(layer_norm(h, ln_g, ln_b))
    return h @ w_pw2 + b_pw2


def split_heads(t):
    b, s, _ = t.shape
    return t.reshape(b, s, N_HEADS, HEAD_DIM).transpose(0, 2, 1, 3)


def stick_breaking_attention(q, k, v):
    s_len = q.shape[2]
    scale = HEAD_DIM ** -0.5
    outs = []
    for blk in range(s_len // Q_BLOCK):
        start = blk * Q_BLOCK
        end = start + Q_BLOCK
        qb = q[:, :, start:end]
        kb = k[:, :, :end]
        vb = v[:, :, :end]
        z = jnp.einsum('bhqd,bhkd->bhqk', qb, kb, preferred_element_type=jnp.float32) * scale
        t_pos = jnp.arange(start, end)[:, None]
        s_pos = jnp.arange(end)[None, :]
        causal = s_pos < t_pos
        log_beta = jax.nn.log_sigmoid(z)
        log_one_minus = jnp.where(causal, jax.nn.log_sigmoid(-z), 0.0)
        log_remain = lax.cumsum(log_one_minus, axis=3, reverse=True) - log_one_minus
        weights = jnp.where(causal, jnp.exp(log_beta + log_remain), 0.0)
        outs.append(jnp.einsum('bhqk,bhkd->bhqd', weights.astype(vb.dtype), vb))
    return jnp.concatenate(outs, axis=2)


def setup_inputs(seed: int = 0) -> dict:
    key = jax.random.key(seed)
    ks = jax.random.split(key, 32)
    f32 = jnp.float32

    def w(k, shape, fan_in):
        return jax.random.normal(k, shape, f32) * (fan_in ** -0.5)

    def gain(k, shape):
        return 1.0 + 0.02 * jax.random.normal(k, shape, f32)

    def bias(k, shape):
        return 0.01 * jax.random.normal(k, shape, f32)

    D, F = D_MODEL, D_FF
    return {
        "x": jax.random.normal(ks[0], (BATCH, SEQ, D), f32),
        "ffn1_norm": gain(ks[1], (DEPTH, D)),
        "ffn1_w_gate": w(ks[2], (DEPTH, D, F), D),
        "ffn1_w_up": w(ks[3], (DEPTH, D, F), D),
        "ffn1_w_down": w(ks[4], (DEPTH, F, D), F),
        "mix_norm": gain(ks[5], (DEPTH, D)),
        "ffn2_norm": gain(ks[6], (DEPTH, D)),
        "ffn2_w_gate": w(ks[7], (DEPTH, D, F), D),
        "ffn2_w_up": w(ks[8], (DEPTH, D, F), D),
        "ffn2_w_down": w(ks[9], (DEPTH, F, D), F),
        "conv_w_pw1": w(ks[10], (N_A, D, 2 * D), D),
        "conv_b_pw1": bias(ks[11], (N_A, 2 * D)),
        "conv_w_dw": w(ks[12], (N_A, CONV_WIDTH, D), CONV_WIDTH),
        "conv_b_dw": bias(ks[13], (N_A, D)),
        "conv_ln_g": gain(ks[14], (N_A, D)),
        "conv_ln_b": bias(ks[15], (N_A, D)),
        "conv_w_pw2": w(ks[16], (N_A, D, D), D),
        "conv_b_pw2": bias(ks[17], (N_A, D)),
        "kv_norm": gain(ks[18], (D,)),
        "w_kv": w(ks[19], (D, 2 * D), D),
        "attn_w_q": w(ks[20], (N_B, D, D), D),
        "attn_w_o": w(ks[21], (N_B, D, D), D),
        "final_norm": gain(ks[22], (D,)),
    }


def reference(x, ffn1_norm, ffn1_w_gate, ffn1_w_up, ffn1_w_down, mix_norm,
              ffn2_norm, ffn2_w_gate, ffn2_w_up, ffn2_w_down,
              conv_w_pw1, conv_b_pw1, conv_w_dw, conv_b_dw, conv_ln_g, conv_ln_b,
              conv_w_pw2, conv_b_pw2, kv_norm, w_kv, attn_w_q, attn_w_o, final_norm):
    b, s, d = x.shape
    h = x
    k_shared = None
    v_shared = None
    for layer in range(DEPTH):
        h = h + HALF_STEP * swiglu_ffn(rms_norm(h, ffn1_norm[layer]),
                                       ffn1_w_gate[layer], ffn1_w_up[layer], ffn1_w_down[layer])
        u = rms_norm(h, mix_norm[layer])
        if layer < N_A:
            i = layer
            h = h + conformer_conv(u, conv_w_pw1[i], conv_b_pw1[i], conv_w_dw[i], conv_b_dw[i],
                                   conv_ln_g[i], conv_ln_b[i], conv_w_pw2[i], conv_b_pw2[i])
        else:
            i = layer - N_A
            q = split_heads(u @ attn_w_q[i])
            o = stick_breaking_attention(q, k_shared, v_shared)
            o = o.transpose(0, 2, 1, 3).reshape(b, s, d)
            h = h + o @ attn_w_o[i]
        h = h + HALF_STEP * swiglu_ffn(rms_norm(h, ffn2_norm[layer]),
                                       ffn2_w_gate[layer], ffn2_w_up[layer], ffn2_w_down[layer])
        if layer == N_A - 1:
            kv = rms_norm(h, kv_norm) @ w_kv
            k_flat, v_flat = jnp.split(kv, 2, axis=-1)
            k_shared = split_heads(k_flat)
            v_shared = split_heads(v_flat)
    return rms_norm(h, final_norm)
```

```python
import numpy as np
from contextlib import ExitStack

import concourse.bass as bass
import concourse.mybir as mybir
from concourse.bass_utils import run_bass_kernel_spmd

F32 = mybir.dt.float32
BF16 = mybir.dt.bfloat16
AF = mybir.ActivationFunctionType
ALU = mybir.AluOpType

D = 1024
S = 2048
F = 2816
NDC = 8
NFC = 22
TG = 512
NTG = 4
NTT = 16
CW = 31
NCORES = 8
SLICES = [(0, 6), (6, 6), (12, 5), (17, 5)]
AW_BYTES = 108032
POOL_CONV_CHUNKS = ()
SEQ_ATTN = False
AENG = "dve"

V_FFN1 = 0
V_MIX = 32
V_FFN2 = 64
V_KV = 96
V_FIN = 104
V_CONV = 112
NV = 112 + 2 * 296


class _Op:
    __slots__ = ("eng", "fn", "idx", "waits", "signal", "dma", "sig", "isdma")


class _Stream:
    def __init__(self, name):
        self.name = name
        self.count = 0
        self.sem = None


class Prog:
    ENGS = ("pe", "act", "dve", "pool", "sp")

    def __init__(self):
        self.ops = {e: [] for e in self.ENGS}
        self.lastw = {}
        self.readers = {}
        self.seen = {e: {} for e in self.ENGS}
        self.streams = {}
        self.nwaits = 0

    def stream(self, name):
        st = self.streams.get(name)
        if st is None:
            st = _Stream(name)
            self.streams[name] = st
        return st

    def add(self, eng, fn, reads=(), writes=(), dma=None):
        op = _Op()
        op.eng = eng
        op.fn = fn
        op.idx = len(self.ops[eng])
        op.waits = []
        op.signal = False
        op.dma = None
        op.sig = None
        op.isdma = dma is not None
        for k in reads:
            self._need(op, self.lastw.get(k), True)
        for k in writes:
            self._need(op, self.lastw.get(k), False)
            rd = self.readers.get(k)
            if rd:
                for r in rd.values():
                    self._need(op, r, False)
        if dma is not None:
            st = self.stream(dma)
            st.count += 1
            op.dma = st
        agent = eng if dma is None else "dma:" + dma
        for k in reads:
            self.readers.setdefault(k, {})[agent] = op
        for k in writes:
            self.lastw[k] = op
            self.readers[k] = {}
        self.ops[eng].append(op)
        return op

    def _need(self, X, P_, raw):
        if P_ is None or P_ is X:
            return
        e = X.eng
        if P_.dma is not None:
            st = P_.dma
            val = 16 * st.count
            key = ("s", st.name)
            if self.seen[e].get(key, 0) >= val:
                return
            X.waits.append((st, val))
            self.seen[e][key] = val
            self.nwaits += 1
            return
        if P_.eng == e and not X.isdma:
            if e == "pe" or e == "sp":
                return
            if not raw:
                return
            if e != "pool" and X.idx - P_.idx >= 4:
                return
        if self.seen[e].get(P_.eng, -1) >= P_.idx:
            return
        P_.signal = True
        X.waits.append(P_)
        self.seen[e][P_.eng] = P_.idx
        self.nwaits += 1

    def barrier(self):
        lasts = [self.ops[e][-1] for e in self.ENGS if e != "sp" and self.ops[e]]
        hub = _Op()
        hub.eng = "sp"
        hub.fn = lambda eng: eng.nop()
        hub.idx = len(self.ops["sp"])
        hub.waits = []
        hub.signal = False
        hub.dma = None
        hub.sig = None
        hub.isdma = False
        for P_ in lasts:
            self._need(hub, P_, True)
        for st in self.streams.values():
            if st.count > 0:
                key = ("s", st.name)
                val = 16 * st.count
                if self.seen["sp"].get(key, 0) < val:
                    hub.waits.append((st, val))
                    self.seen["sp"][key] = val
        self.ops["sp"].append(hub)
        for e in self.ENGS:
            if e == "sp":
                continue
            op = _Op()
            op.eng = e
            op.fn = lambda eng: eng.nop()
            op.idx = len(self.ops[e])
            op.waits = []
            op.signal = False
            op.dma = None
            op.sig = None
            op.isdma = False
            self._need(op, hub, True)
            self.ops[e].append(op)

    def finalize(self, nc, stack, nsig=8):
        for st in self.streams.values():
            st.sem = stack.enter_context(nc.semaphore("d_" + st.name))
        self.sig_sems = {}
        for e in self.ENGS:
            sems = [stack.enter_context(nc.semaphore("g_%s_%d" % (e, i))) for i in range(nsig)]
            self.sig_sems[e] = sems
            c = 0
            for op in self.ops[e]:
                if op.signal and op.dma is None:
                    op.sig = (sems[c % nsig], c // nsig + 1)
                    c += 1

    def emit(self, e, eng, final_streams=()):
        for op in self.ops[e]:
            for w in op.waits:
                if isinstance(w, tuple):
                    eng.wait_ge(w[0].sem, w[1])
                else:
                    eng.wait_ge(w.sig[0], w.sig[1])
            ins = op.fn(eng)
            if op.dma is not None:
                ins.then_inc(op.dma.sem, 16)
            elif op.signal:
                ins.then_inc(op.sig[0], 1)
        for name in final_streams:
            st = self.streams[name]
            eng.wait_ge(st.sem, 16 * st.count)


class Ctx:
    pass


def _build(nph=12):
    nc = bass.Bass("TRN2", target_bir_lowering=False)
    stack = ExitStack()
    C = Ctx()
    C.nc = nc
    pr = Prog()
    C.pr = pr

    def din(name, shape):
        return nc.dram_tensor(name, list(shape), F32, kind="ExternalInput").ap()

    C.x = din("x", (S, D))
    shapes = {"ffn1_w_gate": (4, D, F), "ffn1_w_up": (4, D, F), "ffn1_w_down": (4, F, D),
              "ffn2_w_gate": (4, D, F), "ffn2_w_up": (4, D, F), "ffn2_w_down": (4, F, D),
              "conv_w_pw1": (2, D, 2 * D), "conv_w_pw2": (2, D, D), "w_kv": (D, 2 * D),
              "attn_w_q": (2, D, D), "attn_w_o": (2, D, D)}

    class LazyW(dict):
        def __missing__(self, nm):
            self[nm] = din(nm, shapes[nm])
            return self[nm]
    C.w = LazyW()
    C.vecs_d = din("vecs", (128, NV))
    C.ident_d = din("ident", (128, 128))
    C.cb_d = din("cbf", (128, 512))
    C.out = nc.dram_tensor("out", [S, D], F32, kind="ExternalOutput").ap()
    skind = "ExternalOutput" if nph < 0 else "Internal"
    C.kscr = nc.dram_tensor("kscr", [128, 16384], BF16, kind=skind).ap()
    C.vscr = nc.dram_tensor("vscr", [128, 16384], BF16, kind=skind).ap()

    C.dbg = None
    if nph == -3:
        C.dbg = (nc.dram_tensor("dbg_bf", [128, 16384 + 3072], BF16, kind="ExternalOutput").ap(),
                 nc.dram_tensor("dbg_f32", [128, 512], F32, kind="ExternalOutput").ap(),
                 nc.dram_tensor("dbg_kv", [128, 32768], BF16, kind="ExternalOutput").ap())
    C.H = stack.enter_context(nc.sbuf_tensor("H", [128, NDC, S], F32))
    C.Xr = stack.enter_context(nc.sbuf_tensor("Xr", [128, 16384], BF16))
    C.AW = stack.enter_context(nc.sbuf_tensor("AW", [128, AW_BYTES // 2], BF16))
    C.vecs = stack.enter_context(nc.sbuf_tensor("vecs_sb", [128, NV], F32))
    C.ident = stack.enter_context(nc.sbuf_tensor("ident_sb", [128, 128], F32))
    C.cb = stack.enter_context(nc.sbuf_tensor("cb_sb", [128, 512], BF16))
    C.PS = [stack.enter_context(nc.psum_tensor("ps%d" % i, [128, 512], F32)) for i in range(8)]

    def carve(base, off, shape, dtype):
        n = 1
        for s_ in shape:
            n *= s_
        nb = n * (4 if dtype == F32 else 2)
        assert off % 4 == 0
        v = base[:, off // 2:(off + nb) // 2]
        if dtype == F32:
            v = v.bitcast(F32)
        if len(shape) == 2:
            v = v.rearrange("p (a b) -> p a b", a=shape[0])
        return v

    C.aw = lambda off, shape, dtype: carve(C.AW, off, shape, dtype)
    C.xr = lambda off, shape, dtype: carve(C.Xr, off, shape, dtype)
    C.onesM = C.cb[:, 0:128]
    C.ones1 = C.cb[:, 128:256]
    C.tril = C.cb[:, 256:384]
    C.triu = C.cb[:, 384:512]

    def vcol(c):
        return C.vecs[:, c:c + 1]
    C.vcol = vcol

    pr.add("sp", lambda e: e.dma_start(out=C.vecs[:], in_=C.vecs_d), writes=["vecs"], dma="c_vecs")
    pr.add("sp", lambda e: e.dma_start(out=C.ident[:], in_=C.ident_d), writes=["ident"], dma="c_ident")
    pr.add("pool", lambda e: e.dma_start(out=C.cb[:], in_=C.cb_d), writes=["cb"], dma="c_cb")

    phase_load(C)
    pr.barrier()
    n = 0
    if nph < 0:
        phase_kv(C)
        pr.barrier()
        if nph < -1:
            phase_attn(C, 0)
            pr.barrier()
    for layer in range(4):
        if n >= nph:
            break
        phase_ffn(C, layer, 1)
        pr.barrier()
        n += 1
        if n >= nph:
            break
        if layer < 2:
            phase_conv(C, layer)
        else:
            phase_attn(C, layer - 2)
        pr.barrier()
        n += 1
        if n >= nph:
            break
        phase_ffn(C, layer, 2)
        pr.barrier()
        n += 1
        if layer == 1 and n < nph:
            phase_kv(C)
            pr.barrier()
    phase_final(C)

    pr.finalize(nc, stack)
    with nc.Block() as block:
        @block.sync
        def _(e):
            pr.emit("sp", e, final_streams=("os0", "os1"))

        @block.tensor
        def _(e):
            pr.emit("pe", e)

        @block.scalar
        def _(e):
            pr.emit("act", e)

        @block.vector
        def _(e):
            pr.emit("dve", e)

        @block.gpsimd
        def _(e):
            pr.emit("pool", e)
    stack.close()
    nc._used_w = sorted(C.w.keys())
    return nc


def hk(dc, tg):
    return "h%d_%d" % (dc, tg)


def norm_tg(C, tg, gcol, out_ap, out_key, tmp, sq_eng="act", st_bank=6, out_engs=("dve",)):
    pr = C.pr
    cols = slice(tg * TG, (tg + 1) * TG)
    ps = C.PS[st_bank]
    psk = "ps%d" % st_bank
    for dc in range(NDC):
        sq, sqk = tmp["sq%d" % (dc % 2)]
        hin = C.H[:, dc, cols]
        if sq_eng == "act":
            pr.add("act", lambda e, sq=sq, hin=hin: e.activation(out=sq, in_=hin, func=AF.Square),
                   reads=[hk(dc, tg)], writes=sqk)
        else:
            pr.add(sq_eng, lambda e, sq=sq, hin=hin: e.tensor_tensor(out=sq, in0=hin, in1=hin, op=ALU.mult),
                   reads=[hk(dc, tg)], writes=sqk)
        pr.add("pe", lambda e, sq=sq, dc=dc: e.matmul(ps[:], lhsT=C.onesM, rhs=sq, start=(dc == 0), stop=(dc == NDC - 1)),
               reads=sqk + ["cb"], writes=[psk])
    lnt, lntk = tmp["lnt"]
    rs, rsk = tmp["rs"]
    pr.add("act", lambda e: e.activation(out=lnt, in_=ps[:], func=AF.Ln, bias=1e-6, scale=1.0),
           writes=[psk] + lntk)
    pr.add("act", lambda e: e.activation(out=rs, in_=lnt, func=AF.Exp, scale=-0.5),
           reads=lntk, writes=rsk)
    for dc in range(NDC):
        eng = out_engs[dc % len(out_engs)]
        o = out_ap(dc)
        hin = C.H[:, dc, cols]
        pr.add(eng, lambda e, o=o, hin=hin, dc=dc: e.scalar_tensor_tensor(
            out=o, in0=hin, scalar=C.vcol(gcol + dc), in1=rs, op0=ALU.mult, op1=ALU.mult),
            reads=[hk(dc, tg), "vecs"] + rsk, writes=out_key(dc))


def phase_load(C):
    pr = C.pr
    XS = [C.aw(0, (1024,), F32), C.aw(4096, (1024,), F32)]
    for tt in range(NTT):
        b = tt % 2
        xs = XS[b]
        pr.add("sp", lambda e, xs=xs, tt=tt: e.dma_start(out=xs, in_=C.x[tt * 128:(tt + 1) * 128, :]),
               writes=["xs%d" % b], dma="xs%d" % b)
        tg = tt // 4
        for half in range(2):
            bi = (2 * tt + half) % 4
            ps = C.PS[bi]
            for q in range(4):
                dc = half * 4 + q
                pr.add("pe", lambda e, ps=ps, q=q, xs=xs, dc=dc: e.transpose(
                    ps[:, q * 128:(q + 1) * 128], xs[:, dc * 128:(dc + 1) * 128], C.ident[:]),
                    reads=["xs%d" % b, "ident"], writes=["ps%d" % bi])
            dst = C.H[:, half * 4:(half + 1) * 4, tt * 128:(tt + 1) * 128]
            src = ps[:].rearrange("p (q t) -> p q t", q=4)
            pr.add("dve", lambda e, dst=dst, src=src: e.tensor_copy(out=dst, in_=src),
                   writes=["ps%d" % bi] + [hk(half * 4 + q, tg) for q in range(4)])


def phase_final(C):
    pr = C.pr
    YF = C.aw(0, (NDC, TG), F32)
    OS = [C.aw(32768, (1024,), F32), C.aw(36864, (1024,), F32)]
    tmp = {"sq0": (C.aw(88064, (TG,), BF16), ["sq0"]), "sq1": (C.aw(89088, (TG,), BF16), ["sq1"]),
           "lnt": (C.aw(90112, (TG,), F32), ["lnt"]), "rs": (C.aw(92160, (TG,), F32), ["rs"])}
    for tg in range(NTG):
        norm_tg(C, tg, V_FIN, lambda dc: YF[:, dc, :], lambda dc: ["yf%d" % dc], tmp)
        for tl in range(4):
            tt = tg * 4 + tl
            ob = tt % 2
            for half in range(2):
                bi = (2 * tt + half) % 4
                ps = C.PS[bi]
                for q in range(4):
                    dc = half * 4 + q
                    pr.add("pe", lambda e, ps=ps, q=q, dc=dc, tl=tl: e.transpose(
                        ps[:, q * 128:(q + 1) * 128], YF[:, dc, tl * 128:(tl + 1) * 128], C.ident[:]),
                        reads=["yf%d" % dc, "ident"], writes=["ps%d" % bi])
                dst = OS[ob][:, half * 512:(half + 1) * 512]
                eng = "dve" if half == 0 else "act"
                if eng == "dve":
                    pr.add("dve", lambda e, dst=dst, ps=ps: e.tensor_copy(out=dst, in_=ps[:]),
                           writes=["ps%d" % bi, "os%d_%d" % (ob, half)])
                else:
                    pr.add("act", lambda e, dst=dst, ps=ps: e.activation(out=dst, in_=ps[:], func=AF.Copy),
                           writes=["ps%d" % bi, "os%d_%d" % (ob, half)])
            pr.add("sp", lambda e, ob=ob, tt=tt: e.dma_start(out=C.out[tt * 128:(tt + 1) * 128, :], in_=OS[ob]),
                   reads=["os%d_0" % ob, "os%d_1" % ob], dma="os%d" % ob)


def phase_ffn(C, layer, which):
    pr = C.pr
    pre = "ffn%d_" % which
    wg = C.w[pre + "w_gate"][layer]
    wu = C.w[pre + "w_up"][layer]
    wd = C.w[pre + "w_down"][layer]
    gcol = (V_FFN1 if which == 1 else V_FFN2) + layer * 8
    X = C.Xr[:, :].rearrange("p (a b) -> p a b", a=NDC)
    SLOT = 36864
    WG = [C.aw(s * SLOT, (NDC, 768), BF16) for s in range(2)]
    WU = [C.aw(s * SLOT + 12288, (NDC, 768), BF16) for s in range(2)]
    WD = [C.aw(s * SLOT + 24576, (6, 1024), BF16) for s in range(2)]
    HID = [C.aw(73728, (6, TG), BF16), C.aw(79872, (6, TG), BF16)]
    SG = [C.aw(86016, (TG,), BF16), C.aw(87040, (TG,), BF16)]
    tmp = {"sq0": (C.aw(88064, (TG,), BF16), ["sq0"]), "sq1": (C.aw(89088, (TG,), BF16), ["sq1"]),
           "lnt": (C.aw(90112, (TG,), F32), ["lnt"]), "rs": (C.aw(92160, (TG,), F32), ["rs"])}
    wgv = wg.rearrange("(dc p) f -> p dc f", p=128)
    wuv = wu.rearrange("(dc p) f -> p dc f", p=128)

    def load_slice(s):
        f0, n = SLICES[s]
        slot = s % 2
        k = "slot%d" % slot
        pr.add("pool", lambda e: e.dma_start(out=WG[slot][:, :, 0:n * 128], in_=wgv[:, :, f0 * 128:(f0 + n) * 128]),
               writes=[k], dma="w%d" % slot)
        pr.add("pool", lambda e: e.dma_start(out=WU[slot][:, :, 0:n * 128], in_=wuv[:, :, f0 * 128:(f0 + n) * 128]),
               writes=[k], dma="w%d" % slot)
        pr.add("pool", lambda e: e.dma_start(
            out=WD[slot][:, 0:n, :], in_=wd[f0 * 128:(f0 + n) * 128, :].rearrange("(fc p) d -> p fc d", p=128)),
            writes=[k], dma="w%d" % slot)

    load_slice(0)
    load_slice(1)
    for tg in range(NTG):
        cols = slice(tg * TG, (tg + 1) * TG)
        norm_tg(C, tg, gcol, lambda dc: X[:, dc, cols], lambda dc: ["x%d_%d" % (dc, tg)], tmp)

    units = [(s, tg) for s in range(len(SLICES)) for tg in range(NTG)]
    cnt = [0, 0]

    def gu(ui):
        s, tg = units[ui]
        f0, n = SLICES[s]
        slot = s % 2
        hid = HID[ui % 2]
        cols = slice(tg * TG, (tg + 1) * TG)
        for j in range(n):
            b = cnt[0] % 2
            cnt[0] += 1
            pg = C.PS[b]
            pu = C.PS[2 + b]
            for dc in range(NDC):
                pr.add("pe", lambda e, pg=pg, j=j, dc=dc: e.matmul(
                    pg[:], lhsT=WG[slot][:, dc, j * 128:(j + 1) * 128], rhs=X[:, dc, cols],
                    start=(dc == 0), stop=(dc == NDC - 1)),
                    reads=["slot%d" % slot, "x%d_%d" % (dc, tg)], writes=["ps%d" % b])
            for dc in range(NDC):
                pr.add("pe", lambda e, pu=pu, j=j, dc=dc: e.matmul(
                    pu[:], lhsT=WU[slot][:, dc, j * 128:(j + 1) * 128], rhs=X[:, dc, cols],
                    start=(dc == 0), stop=(dc == NDC - 1)),
                    reads=["slot%d" % slot, "x%d_%d" % (dc, tg)], writes=["ps%d" % (2 + b)])
            sg = SG[b]
            pr.add("act", lambda e, sg=sg, pg=pg: e.activation(out=sg, in_=pg[:], func=AF.Silu),
                   writes=["ps%d" % b, "sg%d" % b])
            pr.add("dve", lambda e, hid=hid, j=j, pu=pu, sg=sg: e.tensor_tensor(
                out=hid[:, j, :], in0=pu[:], in1=sg, op=ALU.mult),
                reads=["sg%d" % b], writes=["ps%d" % (2 + b), "hid%d_%d" % (ui % 2, j)])

    def down(ui):
        s, tg = units[ui]
        f0, n = SLICES[s]
        slot = s % 2
        hid = HID[ui % 2]
        cols = slice(tg * TG, (tg + 1) * TG)
        for dc in range(NDC):
            b = 4 + cnt[1] % 2
            cnt[1] += 1
            pd = C.PS[b]
            for j in range(n):
                pr.add("pe", lambda e, pd=pd, j=j, dc=dc: e.matmul(
                    pd[:], lhsT=WD[slot][:, j, dc * 128:(dc + 1) * 128], rhs=hid[:, j, :],
                    start=(j == 0), stop=(j == n - 1)),
                    reads=["slot%d" % slot, "hid%d_%d" % (ui % 2, j)], writes=["ps%d" % b])
            hh = C.H[:, dc, cols]
            pr.add("dve", lambda e, pd=pd, hh=hh: e.scalar_tensor_tensor(
                out=hh, in0=pd[:], scalar=0.5, in1=hh, op0=ALU.mult, op1=ALU.add),
                reads=[hk(dc, tg)], writes=["ps%d" % b, hk(dc, tg)])

    gu(0)
    for ui in range(len(units)):
        if ui + 1 < len(units):
            gu(ui + 1)
        down(ui)
        s, tg = units[ui]
        if tg == NTG - 1 and s + 2 < len(SLICES):
            load_slice(s + 2)


def phase_conv(C, i):
    pr = C.pr
    w1 = C.w["conv_w_pw1"][i].rearrange("(dc p) f -> p dc f", p=128)
    w2 = C.w["conv_w_pw2"][i].rearrange("(dc p) f -> p dc f", p=128)
    vb = V_CONV + i * 296
    c_b1a, c_b1g, c_bdw, c_lng, c_lnb, c_b2, c_wdw = vb, vb + 8, vb + 16, vb + 24, vb + 32, vb + 40, vb + 48
    gcol = V_MIX + i * 8
    W1 = C.aw(0, (NDC, 2048), BF16)
    W2 = C.aw(32768, (NDC, 1024), BF16)
    G = C.aw(49152, (NDC, 542), F32)
    Y = C.aw(66496, (NDC, TG), F32)
    YSQ = C.aw(82880, (NDC, TG), BF16)
    SIG = [C.aw(91072, (TG,), F32), C.aw(93120, (TG,), F32)]
    LNT = C.aw(95168, (TG,), F32)
    RS = C.aw(97216, (TG,), F32)
    MEAN = C.aw(99264, (TG,), F32)
    MSQ = C.aw(101312, (TG,), F32)
    NMR = C.aw(103360, (TG,), F32)
    UT = C.xr(0, (NDC, TG), BF16)
    SS = C.xr(8192, (NDC, TG), BF16)
    YB = C.xr(16384, (NDC, TG), BF16)
    tmp = {"sq0": (C.xr(24576, (TG,), BF16), ["sq0"]), "sq1": (C.xr(25600, (TG,), BF16), ["sq1"]),
           "lnt": (LNT, ["lnt"]), "rs": (RS, ["rs"])}
    for hf in range(4):
        pr.add("pool", lambda e, hf=hf: e.dma_start(out=W1[:, 2 * hf:2 * hf + 2, :], in_=w1[:, 2 * hf:2 * hf + 2, :]),
               writes=["w1"], dma="cw1")
    for hf in range(2):
        pr.add("pool", lambda e, hf=hf: e.dma_start(out=W2[:, 4 * hf:4 * hf + 4, :], in_=w2[:, 4 * hf:4 * hf + 4, :]),
               writes=["w2"], dma="cw2")
    pr.add("pool", lambda e: e.memset(G[:, :, 0:30], 0.0), writes=["g%d" % c for c in range(NDC)])

    def ceng(c):
        return "pool" if c in POOL_CONV_CHUNKS else "dve"

    for tg in range(NTG):
        cols = slice(tg * TG, (tg + 1) * TG)
        norm_tg(C, tg, gcol, lambda dc: UT[:, dc, :], lambda dc: ["ut%d" % dc], tmp)
        for c in range(NDC):
            b = c % 2
            pa = C.PS[b]
            pg = C.PS[2 + b]
            for dc in range(NDC):
                pr.add("pe", lambda e, pa=pa, c=c, dc=dc: e.matmul(
                    pa[:], lhsT=W1[:, dc, c * 128:(c + 1) * 128], rhs=UT[:, dc, :],
                    start=(dc == 0), stop=(dc == NDC - 1)),
                    reads=["w1", "ut%d" % dc], writes=["ps%d" % b])
            for dc in range(NDC):
                pr.add("pe", lambda e, pg=pg, c=c, dc=dc: e.matmul(
                    pg[:], lhsT=W1[:, dc, 1024 + c * 128:1024 + (c + 1) * 128], rhs=UT[:, dc, :],
                    start=(dc == 0), stop=(dc == NDC - 1)),
                    reads=["w1", "ut%d" % dc], writes=["ps%d" % (2 + b)])
            sig = SIG[b]
            pr.add("act", lambda e, sig=sig, pg=pg, c=c: e.activation(
                out=sig, in_=pg[:], func=AF.Sigmoid, bias=C.vcol(c_b1g + c), scale=1.0),
                reads=["vecs"], writes=["ps%d" % (2 + b), "sig%d" % b])
            pr.add("dve", lambda e, pa=pa, sig=sig, c=c: e.scalar_tensor_tensor(
                out=G[:, c, 30:542], in0=pa[:], scalar=C.vcol(c_b1a + c), in1=sig, op0=ALU.add, op1=ALU.mult),
                reads=["sig%d" % b, "vecs"], writes=["ps%d" % b, "g%d" % c])
        for k in range(CW):
            for c in range(NDC):
                eng = ceng(c)
                if k == 0:
                    pr.add(eng, lambda e, c=c: e.tensor_scalar(
                        out=Y[:, c, :], in0=G[:, c, 0:512], scalar1=C.vcol(c_wdw + c * 31), scalar2=C.vcol(c_bdw + c),
                        op0=ALU.mult, op1=ALU.add),
                        reads=["g%d" % c, "vecs"], writes=["y%d" % c])
                else:
                    pr.add(eng, lambda e, c=c, k=k: e.scalar_tensor_tensor(
                        out=Y[:, c, :], in0=G[:, c, k:k + 512], scalar=C.vcol(c_wdw + c * 31 + k), in1=Y[:, c, :],
                        op0=ALU.mult, op1=ALU.add),
                        reads=["g%d" % c, "vecs", "y%d" % c], writes=["y%d" % c])
        for c in range(NDC):
            pr.add(ceng(c), lambda e, c=c: e.tensor_copy(out=G[:, c, 0:30], in_=G[:, c, 512:542]),
                   reads=["g%d" % c], writes=["g%d" % c])
        for c in range(NDC):
            pr.add("act", lambda e, c=c: e.activation(out=YB[:, c, :], in_=Y[:, c, :], func=AF.Copy),
                   reads=["y%d" % c], writes=["yb%d" % c])
            pr.add("act", lambda e, c=c: e.activation(out=YSQ[:, c, :], in_=Y[:, c, :], func=AF.Square),
                   reads=["y%d" % c], writes=["ysq%d" % c])
        for c in range(NDC):
            pr.add("pe", lambda e, c=c: e.matmul(C.PS[6][:], lhsT=C.onesM, rhs=YB[:, c, :],
                                                 start=(c == 0), stop=(c == NDC - 1)),
                   reads=["yb%d" % c, "cb"], writes=["ps6"])
        for c in range(NDC):
            pr.add("pe", lambda e, c=c: e.matmul(C.PS[7][:], lhsT=C.onesM, rhs=YSQ[:, c, :],
                                                 start=(c == 0), stop=(c == NDC - 1)),
                   reads=["ysq%d" % c, "cb"], writes=["ps7"])
        pr.add("dve", lambda e: e.tensor_copy(out=MEAN, in_=C.PS[6][:]), writes=["ps6", "mean"])
        pr.add("dve", lambda e: e.tensor_tensor(out=MSQ, in0=MEAN, in1=MEAN, op=ALU.mult),
               reads=["mean"], writes=["msq"])
        pr.add("dve", lambda e: e.tensor_tensor(out=MSQ, in0=C.PS[7][:], in1=MSQ, op=ALU.subtract),
               reads=["msq"], writes=["ps7", "msq"])
        pr.add("act", lambda e: e.activation(out=LNT, in_=MSQ, func=AF.Ln, bias=1e-5, scale=1.0),
               reads=["msq"], writes=["lnt"])
        pr.add("act", lambda e: e.activation(out=RS, in_=LNT, func=AF.Exp, scale=-0.5),
               reads=["lnt"], writes=["rs"])
        pr.add("dve", lambda e: e.scalar_tensor_tensor(out=NMR, in0=MEAN, scalar=-1.0, in1=RS, op0=ALU.mult, op1=ALU.mult),
               reads=["mean", "rs"], writes=["nmr"])
        for c in range(NDC):
            pr.add("dve", lambda e, c=c: e.tensor_tensor(out=Y[:, c, :], in0=Y[:, c, :], in1=RS, op=ALU.mult),
                   reads=["y%d" % c, "rs"], writes=["y%d" % c])
        for c in range(NDC):
            pr.add("dve", lambda e, c=c: e.tensor_tensor(out=Y[:, c, :], in0=Y[:, c, :], in1=NMR, op=ALU.add),
                   reads=["y%d" % c, "nmr"], writes=["y%d" % c])
        for c in range(NDC):
            pr.add("act", lambda e, c=c: e.activation(
                out=SS[:, c, :], in_=Y[:, c, :], func=AF.Silu, bias=C.vcol(c_lnb + c), scale=C.vcol(c_lng + c)),
                reads=["y%d" % c, "vecs"], writes=["s%d" % c])
        for dc in range(NDC):
            b = 4 + dc % 2
            pd = C.PS[b]
            for c in range(NDC):
                pr.add("pe", lambda e, pd=pd, c=c, dc=dc: e.matmul(
                    pd[:], lhsT=W2[:, c, dc * 128:(dc + 1) * 128], rhs=SS[:, c, :],
                    start=(c == 0), stop=(c == NDC - 1)),
                    reads=["w2", "s%d" % c], writes=["ps%d" % b])
            hh = C.H[:, dc, cols]
            pr.add("dve", lambda e, pd=pd, hh=hh, dc=dc: e.scalar_tensor_tensor(
                out=hh, in0=pd[:], scalar=C.vcol(c_b2 + dc), in1=hh, op0=ALU.add, op1=ALU.add),
                reads=[hk(dc, tg), "vecs"], writes=["ps%d" % b, hk(dc, tg)])


def phase_kv(C):
    pr = C.pr
    wkv = C.w["w_kv"].rearrange("(dc p) f -> p dc f", p=128)
    KT = C.aw(0, (NDC, S), BF16)
    V = C.aw(32768, (NTT, D), BF16)
    WKV = C.aw(65536, (NDC, 2048), BF16)
    tmp = {"lnt": (C.aw(98304, (TG,), F32), ["lnt"]), "rs": (C.aw(100352, (TG,), F32), ["rs"]),
           "sq0": (C.aw(102400, (TG,), BF16), ["sq0"]), "sq1": (C.aw(103424, (TG,), BF16), ["sq1"])}
    UT = C.xr(0, (NDC, TG), BF16)
    for hf in range(4):
        pr.add("pool", lambda e, hf=hf: e.dma_start(out=WKV[:, 2 * hf:2 * hf + 2, :], in_=wkv[:, 2 * hf:2 * hf + 2, :]),
               writes=["wkv"], dma="wkv")
    cnt = 0
    for tg in range(NTG):
        cols = slice(tg * TG, (tg + 1) * TG)
        norm_tg(C, tg, V_KV, lambda dc: UT[:, dc, :], lambda dc: ["ut%d" % dc], tmp)
        for c in range(NDC):
            b = cnt % 4
            cnt += 1
            ps = C.PS[b]
            for dc in range(NDC):
                pr.add("pe", lambda e, ps=ps, c=c, dc=dc: e.matmul(
                    ps[:], lhsT=WKV[:, dc, c * 128:(c + 1) * 128], rhs=UT[:, dc, :],
                    start=(dc == 0), stop=(dc == NDC - 1)),
                    reads=["wkv", "ut%d" % dc], writes=["ps%d" % b])
            kdst = KT[:, c, cols]
            pr.add("act", lambda e, ps=ps, kdst=kdst: e.activation(out=kdst, in_=ps[:], func=AF.Copy),
                   writes=["ps%d" % b, "K"])
        for tl in range(4):
            tt = tg * 4 + tl
            for half in range(2):
                b = cnt % 4
                cnt += 1
                ps = C.PS[b]
                for dc in range(NDC):
                    pr.add("pe", lambda e, ps=ps, dc=dc, tl=tl, half=half: e.matmul(
                        ps[:], lhsT=UT[:, dc, tl * 128:(tl + 1) * 128],
                        rhs=WKV[:, dc, 1024 + half * 512:1024 + (half + 1) * 512],
                        start=(dc == 0), stop=(dc == NDC - 1)),
                        reads=["wkv", "ut%d" % dc], writes=["ps%d" % b])
                pr.add("dve", lambda e, ps=ps, tt=tt, half=half: e.tensor_copy(
                    out=V[:, tt, half * 512:(half + 1) * 512], in_=ps[:]),
                    writes=["ps%d" % b, "V"])
    KTf = C.AW[:, 0:16384]
    Vf = C.AW[:, 16384:32768]
    for q in range(4):
        pr.add("sp", lambda e, q=q: e.dma_start(out=C.kscr[:, q * 4096:(q + 1) * 4096], in_=KTf[:, q * 4096:(q + 1) * 4096]),
               reads=["K"], dma="kvs")
    for q in range(4):
        pr.add("sp", lambda e, q=q: e.dma_start(out=C.vscr[:, q * 4096:(q + 1) * 4096], in_=Vf[:, q * 4096:(q + 1) * 4096]),
               reads=["V"], dma="kvs")


def phase_attn(C, i):
    pr = C.pr
    wq = C.w["attn_w_q"][i].rearrange("(dc p) f -> p dc f", p=128)
    wo = C.w["attn_w_o"][i].rearrange("(dc p) f -> p dc f", p=128)
    gcol = V_MIX + (2 + i) * 8
    KT = C.aw(0, (NDC, S), BF16)
    V = C.aw(32768, (NTT, D), BF16)
    WQ = C.aw(65536, (NDC, 1024), BF16)
    WO = C.aw(81920, (NDC, 1024), BF16)
    Eb = [C.aw(98304, (TG,), F32), C.aw(100352, (TG,), F32)]
    Lb = [C.aw(102400, (TG,), BF16), C.aw(103424, (TG,), BF16)]
    Wb = [C.aw(104448, (TG,), BF16), C.aw(105472, (TG,), BF16)]
    SSt = C.aw(106496, (TG,), BF16)
    UT = C.xr(0, (NDC, TG), BF16)
    QE = C.xr(8192, (NDC, TG), BF16)
    QO = C.xr(16384, (NDC, TG), BF16)
    OT = C.xr(24576, (NDC, TG), BF16)
    XRb = [C.xr(0, (TG,), F32), C.xr(2048, (TG,), F32)]
    xrk = [["ut0", "ut1"], ["ut2", "ut3"]]
    tmp = {"sq0": (C.xr(16384, (TG,), BF16), ["qo0"]), "sq1": (C.xr(17408, (TG,), BF16), ["qo1"]),
           "lnt": (C.xr(18432, (TG,), F32), ["qo2", "qo3"]), "rs": (C.xr(20480, (TG,), F32), ["qo4", "qo5"])}
    KTf = C.AW[:, 0:16384]
    Vf = C.AW[:, 16384:32768]
    for q in range(4):
        pr.add("sp", lambda e, q=q: e.dma_start(out=KTf[:, q * 4096:(q + 1) * 4096], in_=C.kscr[:, q * 4096:(q + 1) * 4096]),
               writes=["K"], dma="kld")
    for q in range(4):
        pr.add("sp", lambda e, q=q: e.dma_start(out=Vf[:, q * 4096:(q + 1) * 4096], in_=C.vscr[:, q * 4096:(q + 1) * 4096]),
               writes=["V"], dma="vld")
    for hf in range(2):
        pr.add("pool", lambda e, hf=hf: e.dma_start(out=WQ[:, 4 * hf:4 * hf + 4, :], in_=wq[:, 4 * hf:4 * hf + 4, :]),
               writes=["wq"], dma="wq")
    for hf in range(2):
        pr.add("pool", lambda e, hf=hf: e.dma_start(out=WO[:, 4 * hf:4 * hf + 4, :], in_=wo[:, 4 * hf:4 * hf + 4, :]),
               writes=["wo"], dma="wo")

    qcnt = 0
    for tg in range(NTG):
        cols = slice(tg * TG, (tg + 1) * TG)
        norm_tg(C, tg, gcol, lambda dc: UT[:, dc, :], lambda dc: ["ut%d" % dc], tmp, sq_eng="dve", st_bank=3)
        pr.add(AENG, lambda e: e.memset(QE[64:128, :, :], 0.0), writes=["qe%d" % c for c in range(NDC)])
        pr.add(AENG, lambda e: e.memset(QO[0:64, :, :], 0.0), writes=["qo%d" % c for c in range(NDC)])
        for c in range(NDC):
            b = qcnt % 3
            qcnt += 1
            ps = C.PS[b]
            for dc in range(NDC):
                pr.add("pe", lambda e, ps=ps, c=c, dc=dc: e.matmul(
                    ps[:], lhsT=WQ[:, dc, c * 128:(c + 1) * 128], rhs=UT[:, dc, :],
                    start=(dc == 0), stop=(dc == NDC - 1)),
                    reads=["wq", "ut%d" % dc], writes=["ps%d" % b])
            pr.add("dve", lambda e, ps=ps, c=c: e.tensor_scalar(
                out=QE[0:64, c, :], in0=ps[0:64, :], scalar1=0.125, scalar2=None, op0=ALU.mult),
                writes=["ps%d" % b, "qe%d" % c])
            pr.add("dve", lambda e, ps=ps, c=c: e.tensor_scalar(
                out=QO[64:128, c, :], in0=ps[64:128, :], scalar1=0.125, scalar2=None, op0=ALU.mult),
                writes=["ps%d" % b, "qo%d" % c])
        kmax = 4 * tg + 3
        items = []
        for c in range(NDC):
            for j2 in range(2):
                for kb in range(kmax, -1, -1):
                    items.append((c, j2, kb))
        n = len(items)

        def info(it):
            c, j2, kb = items[it]
            c0 = (kb - 4 * tg) * 128 if kb >= 4 * tg else 0
            return c, j2, kb, c0

        def st_z(it):
            c, j2, kb, c0 = info(it)
            b = it % 2
            pz = C.PS[b]
            Q = QE if j2 == 0 else QO
            qk = ("qe%d" if j2 == 0 else "qo%d") % c
            pr.add("pe", lambda e: e.matmul(pz[:, c0:TG], lhsT=KT[:, c, kb * 128:(kb + 1) * 128], rhs=Q[:, c, c0:TG],
                                            start=True, stop=True),
                   reads=["K", qk], writes=["ps%d" % b])
            L = Lb[it % 2]
            E = Eb[it % 2]
            pr.add("act", lambda e: e.activation(out=E[:, c0:TG], in_=pz[:, c0:TG], func=AF.Exp),
                   writes=["ps%d" % b, "E%d" % (it % 2)])
            pr.add("act", lambda e: e.activation(out=L[:, c0:TG], in_=E[:, c0:TG], func=AF.Ln, bias=1.0, scale=1.0),
                   reads=["E%d" % (it % 2)], writes=["L%d" % (it % 2)])
            if kb >= 4 * tg:
                pr.add(AENG, lambda e: e.tensor_tensor(out=L[:, c0:c0 + 128], in0=L[:, c0:c0 + 128], in1=C.triu, op=ALU.mult),
                       reads=["L%d" % (it % 2), "cb"], writes=["L%d" % (it % 2)])

        def st_b(it):
            c, j2, kb, c0 = info(it)
            b = 2 + it % 2
            pb = C.PS[b]
            L = Lb[it % 2]
            E = Eb[it % 2]
            XR = XRb[it % 2]
            first = (kb == kmax)
            if first:
                pr.add(AENG, lambda e: e.memset(SSt, 0.0), writes=["ss"])
            pr.add("pe", lambda e: e.matmul(pb[:, c0:TG], lhsT=C.tril, rhs=L[:, c0:TG], start=True, stop=first),
                   reads=["L%d" % (it % 2), "cb"], writes=["ps%d" % b])
            if not first:
                pr.add("pe", lambda e: e.matmul(pb[:, c0:TG], lhsT=C.ones1, rhs=SSt[:, c0:TG], start=False, stop=True),
                       reads=["ss", "cb"], writes=["ps%d" % b])
            if kb > 0:
                pr.add(AENG, lambda e: e.tensor_tensor(out=SSt[:, c0:TG], in0=SSt[:, c0:TG], in1=L[:, c0:TG], op=ALU.add),
                       reads=["ss", "L%d" % (it % 2)], writes=["ss"])
            Wt = Wb[it % 2]
            wk = "wt%d" % (it % 2)
            pr.add("act", lambda e: e.activation(out=XR[:, c0:TG], in_=pb[:, c0:TG], func=AF.Exp, scale=-1.0),
                   writes=["ps%d" % b] + xrk[it % 2])
            pr.add("dve", lambda e: e.tensor_tensor(out=Wt[:, c0:TG], in0=E[:, c0:TG], in1=XR[:, c0:TG], op=ALU.mult),
                   reads=["E%d" % (it % 2)] + xrk[it % 2], writes=[wk])
            if kb >= 4 * tg:
                pr.add(AENG, lambda e: e.tensor_tensor(out=Wt[:, c0:c0 + 128], in0=Wt[:, c0:c0 + 128], in1=C.triu, op=ALU.mult),
                       reads=[wk, "cb"], writes=[wk])
                if c0 > 0:
                    pr.add(AENG, lambda e: e.memset(Wt[:, 0:c0], 0.0), writes=[wk])

        def st_pv(it, tg_=tg):
            c, j2, kb, c0 = info(it)
            hp = slice(j2 * 64, (j2 + 1) * 64)
            b = 4 + 2 * (c % 2) + j2
            po = C.PS[b]
            Wt = Wb[it % 2]
            pv_first = (kb == 4 * tg_ + 3)
            pr.add("pe", lambda e: e.matmul(po[:], lhsT=V[:, kb, c * 128:(c + 1) * 128], rhs=Wt[:, :],
                                            start=pv_first, stop=(kb == 0)),
                   reads=["V", "wt%d" % (it % 2)], writes=["ps%d" % b])
            if kb == 0:
                pr.add("dve", lambda e: e.tensor_copy(out=OT[hp, c, :], in_=po[hp, :]),
                       writes=["ps%d" % b, "ot%d" % c])

        if SEQ_ATTN:
            for step in range(n):
                st_z(step)
                st_b(step)
                st_pv(step)
        else:
            for step in range(n + 2):
                if step < n:
                    st_z(step)
                if 0 <= step - 1 < n:
                    st_b(step - 1)
                if 0 <= step - 2 < n:
                    st_pv(step - 2)
        if C.dbg is not None and tg == 0:
            pr.barrier()
            pr.add("sp", lambda e: e.dma_start(out=C.dbg[0][:, 0:16384], in_=C.Xr[:, 0:16384]), dma="dbg")
            pr.add("sp", lambda e: e.dma_start(out=C.dbg[0][:, 16384:16384 + 3072], in_=C.AW[:, 100352 // 2:106496 // 2]),
                   dma="dbg")
            pr.add("sp", lambda e: e.dma_start(out=C.dbg[1][:, :], in_=Eb[0]), dma="dbg")
            pr.add("sp", lambda e: e.dma_start(out=C.dbg[2][:, :], in_=C.AW[:, 0:32768]), dma="dbg")
            pr.barrier()
        for dc in range(NDC):
            b = qcnt % 3
            qcnt += 1
            ps = C.PS[b]
            for c in range(NDC):
                pr.add("pe", lambda e, ps=ps, c=c, dc=dc: e.matmul(
                    ps[:], lhsT=WO[:, c, dc * 128:(dc + 1) * 128], rhs=OT[:, c, :],
                    start=(c == 0), stop=(c == NDC - 1)),
                    reads=["wo", "ot%d" % c], writes=["ps%d" % b])
            hh = C.H[:, dc, cols]
            pr.add("dve", lambda e, ps=ps, hh=hh: e.tensor_tensor(out=hh, in0=ps[:], in1=hh, op=ALU.add),
                   reads=[hk(dc, tg)], writes=["ps%d" % b, hk(dc, tg)])


def _fm(v):
    return np.ascontiguousarray(np.asarray(v, dtype=np.float32).reshape(8, 128).T)


def _host_tables(inp):
    vecs = np.zeros((128, NV), dtype=np.float32)
    for l in range(4):
        vecs[:, V_FFN1 + l * 8:V_FFN1 + l * 8 + 8] = _fm(inp["ffn1_norm"][l])
        vecs[:, V_MIX + l * 8:V_MIX + l * 8 + 8] = _fm(inp["mix_norm"][l])
        vecs[:, V_FFN2 + l * 8:V_FFN2 + l * 8 + 8] = _fm(inp["ffn2_norm"][l])
    vecs[:, V_KV:V_KV + 8] = _fm(inp["kv_norm"])
    vecs[:, V_FIN:V_FIN + 8] = _fm(inp["final_norm"])
    for i in range(2):
        vb = V_CONV + i * 296
        b1 = np.asarray(inp["conv_b_pw1"][i], dtype=np.float32)
        vecs[:, vb:vb + 8] = _fm(b1[:D])
        vecs[:, vb + 8:vb + 16] = _fm(b1[D:])
        vecs[:, vb + 16:vb + 24] = _fm(inp["conv_b_dw"][i])
        vecs[:, vb + 24:vb + 32] = _fm(inp["conv_ln_g"][i])
        vecs[:, vb + 32:vb + 40] = _fm(inp["conv_ln_b"][i])
        vecs[:, vb + 40:vb + 48] = _fm(inp["conv_b_pw2"][i])
        wdw = np.asarray(inp["conv_w_dw"][i], dtype=np.float32)
        vecs[:, vb + 48:vb + 296] = wdw.T.reshape(8, 128, CW).transpose(1, 0, 2).reshape(128, 8 * CW)
    ident = np.eye(128, dtype=np.float32)
    cb = np.zeros((128, 512), dtype=np.float32)
    cb[:, 0:128] = 1.0 / 1024.0
    cb[:, 128:256] = 1.0
    r = np.arange(128)
    cb[:, 256:384] = (r[:, None] >= r[None, :]).astype(np.float32)
    cb[:, 384:512] = (r[None, :] > r[:, None]).astype(np.float32)
    return vecs, ident, cb


_NC_CACHE = {}
NPH = 12


def kernel(**inputs):
    inp = {k: np.asarray(v) for k, v in inputs.items()}
    vecs, ident, cb = _host_tables(inp)
    if NPH not in _NC_CACHE:
        _NC_CACHE[NPH] = _build(NPH)
    nc = _NC_CACHE[NPH]
    shared = {"vecs": vecs, "ident": ident, "cbf": cb}
    for nm in nc._used_w:
        shared[nm] = np.ascontiguousarray(inp[nm], dtype=np.float32)
    x = np.ascontiguousarray(inp["x"], dtype=np.float32)
    in_maps = []
    for b in range(NCORES):
        m = dict(shared)
        m["x"] = x[b]
        in_maps.append(m)
    res = run_bass_kernel_spmd(nc, in_maps, core_ids=list(range(NCORES)))
    out = np.stack([np.asarray(res.results[b]["out"], dtype=np.float32) for b in range(NCORES)], axis=0)
    if NPH < 0:
        kernel.dbg = res.results
    return out
```

```python
import numpy as np
from contextlib import ExitStack

import concourse.bass as bass
import concourse.mybir as mybir
from concourse.bass_utils import run_bass_kernel_spmd

F32 = mybir.dt.float32
BF16 = mybir.dt.bfloat16
AF = mybir.ActivationFunctionType
ALU = mybir.AluOpType

D = 1024
S = 2048
F = 2816
NDC = 8
NFC = 22
TG = 512
NTG = 4
NTT = 16
CW = 31
NCORES = 8
SLICES = [(0, 6), (6, 6), (12, 5), (17, 5)]
AW_BYTES = 108032
POOL_CONV_CHUNKS = ()
SEQ_ATTN = False
TRACE = False
NJUNK = 0
AENG = "dve"

V_FFN1 = 0
V_MIX = 32
V_FFN2 = 64
V_KV = 96
V_FIN = 104
V_CONV = 112
NV = 112 + 2 * 296


class _Op:
    __slots__ = ("eng", "fn", "idx", "waits", "signal", "dma", "sig", "isdma")


class _Stream:
    def __init__(self, name):
        self.name = name
        self.count = 0
        self.sem = None


class Prog:
    ENGS = ("pe", "act", "dve", "pool", "sp")

    def __init__(self):
        self.ops = {e: [] for e in self.ENGS}
        self.lastw = {}
        self.readers = {}
        self.seen = {e: {} for e in self.ENGS}
        self.streams = {}
        self.nwaits = 0

    def stream(self, name):
        st = self.streams.get(name)
        if st is None:
            st = _Stream(name)
            self.streams[name] = st
        return st

    def add(self, eng, fn, reads=(), writes=(), dma=None):
        op = _Op()
        op.eng = eng
        op.fn = fn
        op.idx = len(self.ops[eng])
        op.waits = []
        op.signal = False
        op.dma = None
        op.sig = None
        op.isdma = dma is not None
        for k in reads:
            self._need(op, self.lastw.get(k), True)
        for k in writes:
            self._need(op, self.lastw.get(k), False)
            rd = self.readers.get(k)
            if rd:
                for r in rd.values():
                    self._need(op, r, False)
        if dma is not None:
            st = self.stream(dma)
            st.count += 1
            op.dma = st
        agent = eng if dma is None else "dma:" + dma
        for k in reads:
            self.readers.setdefault(k, {})[agent] = op
        for k in writes:
            self.lastw[k] = op
            self.readers[k] = {}
        self.ops[eng].append(op)
        return op

    def _need(self, X, P_, raw):
        if P_ is None or P_ is X:
            return
        e = X.eng
        if P_.dma is not None:
            st = P_.dma
            val = 16 * st.count
            key = ("s", st.name)
            if self.seen[e].get(key, 0) >= val:
                return
            X.waits.append((st, val))
            self.seen[e][key] = val
            self.nwaits += 1
            return
        if P_.eng == e and not X.isdma:
            if e == "pe" or e == "sp":
                return
            if not raw:
                return
            if e != "pool" and X.idx - P_.idx >= 4:
                return
        if self.seen[e].get(P_.eng, -1) >= P_.idx:
            return
        P_.signal = True
        X.waits.append(P_)
        self.seen[e][P_.eng] = P_.idx
        self.nwaits += 1

    def barrier(self):
        lasts = [self.ops[e][-1] for e in self.ENGS if e != "sp" and self.ops[e]]
        hub = _Op()
        hub.eng = "sp"
        hub.fn = lambda eng: eng.nop()
        hub.idx = len(self.ops["sp"])
        hub.waits = []
        hub.signal = False
        hub.dma = None
        hub.sig = None
        hub.isdma = False
        for P_ in lasts:
            self._need(hub, P_, True)
        for st in self.streams.values():
            if st.count > 0:
                key = ("s", st.name)
                val = 16 * st.count
                if self.seen["sp"].get(key, 0) < val:
                    hub.waits.append((st, val))
                    self.seen["sp"][key] = val
        self.ops["sp"].append(hub)
        for e in self.ENGS:
            if e == "sp":
                continue
            op = _Op()
            op.eng = e
            op.fn = lambda eng: eng.nop()
            op.idx = len(self.ops[e])
            op.waits = []
            op.signal = False
            op.dma = None
            op.sig = None
            op.isdma = False
            self._need(op, hub, True)
            self.ops[e].append(op)

    def finalize(self, nc, stack, nsig=8):
        for st in self.streams.values():
            st.sem = stack.enter_context(nc.semaphore("d_" + st.name))
        self.sig_sems = {}
        for e in self.ENGS:
            sems = [stack.enter_context(nc.semaphore("g_%s_%d" % (e, i))) for i in range(nsig)]
            self.sig_sems[e] = sems
            c = 0
            for op in self.ops[e]:
                if op.signal and op.dma is None:
                    op.sig = (sems[c % nsig], c // nsig + 1)
                    c += 1

    def emit(self, e, eng, final_streams=()):
        for op in self.ops[e]:
            for w in op.waits:
                if isinstance(w, tuple):
                    eng.wait_ge(w[0].sem, w[1])
                else:
                    eng.wait_ge(w.sig[0], w.sig[1])
            ins = op.fn(eng)
            if op.dma is not None:
                ins.then_inc(op.dma.sem, 16)
            elif op.signal:
                ins.then_inc(op.sig[0], 1)
        for name in final_streams:
            st = self.streams[name]
            eng.wait_ge(st.sem, 16 * st.count)


class Ctx:
    pass


def _build(nph=12):
    nc = bass.Bass("TRN2", target_bir_lowering=False)
    stack = ExitStack()
    C = Ctx()
    C.nc = nc
    pr = Prog()
    C.pr = pr

    def din(name, shape):
        return nc.dram_tensor(name, list(shape), F32, kind="ExternalInput").ap()

    C.x = din("x", (S, D))
    shapes = {"ffn1_w_gate": (4, D, F), "ffn1_w_up": (4, D, F), "ffn1_w_down": (4, F, D),
              "ffn2_w_gate": (4, D, F), "ffn2_w_up": (4, D, F), "ffn2_w_down": (4, F, D),
              "conv_w_pw1": (2, D, 2 * D), "conv_w_pw2": (2, D, D), "w_kv": (D, 2 * D),
              "attn_w_q": (2, D, D), "attn_w_o": (2, D, D)}

    class LazyW(dict):
        def __missing__(self, nm):
            self[nm] = din(nm, shapes[nm])
            return self[nm]
    C.w = LazyW()
    C.vecs_d = din("vecs", (128, NV))
    C.ident_d = din("ident", (128, 128))
    C.cb_d = din("cbf", (128, 640))
    C.out = nc.dram_tensor("out", [S, D], F32, kind="ExternalOutput").ap()
    skind = "ExternalOutput" if nph < 0 else "Internal"
    C.kscr = nc.dram_tensor("kscr", [128, 16384], BF16, kind=skind).ap()
    C.vscr = nc.dram_tensor("vscr", [128, 16384], BF16, kind=skind).ap()

    C.dbg = None
    if nph == -3:
        C.dbg = (nc.dram_tensor("dbg_bf", [128, 16384 + 3072], BF16, kind="ExternalOutput").ap(),
                 nc.dram_tensor("dbg_f32", [128, 512], F32, kind="ExternalOutput").ap(),
                 nc.dram_tensor("dbg_kv", [128, 32768], BF16, kind="ExternalOutput").ap())
    C.H = stack.enter_context(nc.sbuf_tensor("H", [128, NDC, S], F32))
    C.Xr = stack.enter_context(nc.sbuf_tensor("Xr", [128, 16384], BF16))
    C.AW = stack.enter_context(nc.sbuf_tensor("AW", [128, AW_BYTES // 2], BF16))
    C.vecs = stack.enter_context(nc.sbuf_tensor("vecs_sb", [128, NV], F32))
    C.ident = stack.enter_context(nc.sbuf_tensor("ident_sb", [128, 128], F32))
    C.cb = stack.enter_context(nc.sbuf_tensor("cb_sb", [128, 640], BF16))
    C.PS = [stack.enter_context(nc.psum_tensor("ps%d" % i, [128, 512], F32)) for i in range(8)]

    def carve(base, off, shape, dtype):
        n = 1
        for s_ in shape:
            n *= s_
        nb = n * (4 if dtype == F32 else 2)
        assert off % 4 == 0
        v = base[:, off // 2:(off + nb) // 2]
        if dtype == F32:
            v = v.bitcast(F32)
        if len(shape) == 2:
            v = v.rearrange("p (a b) -> p a b", a=shape[0])
        return v

    C.aw = lambda off, shape, dtype: carve(C.AW, off, shape, dtype)
    C.xr = lambda off, shape, dtype: carve(C.Xr, off, shape, dtype)
    C.onesM = C.cb[:, 0:128]
    C.ones1 = C.cb[:, 128:256]
    C.tril = C.cb[:, 256:384]
    C.triu = C.cb[:, 384:512]
    C.identb = C.cb[:, 512:640]

    def vcol(c):
        return C.vecs[:, c:c + 1]
    C.vcol = vcol

    pr.add("sp", lambda e: e.dma_start(out=C.vecs[:], in_=C.vecs_d), writes=["vecs"], dma="c_vecs")
    pr.add("sp", lambda e: e.dma_start(out=C.ident[:], in_=C.ident_d), writes=["ident"], dma="c_ident")
    pr.add("pool", lambda e: e.dma_start(out=C.cb[:], in_=C.cb_d), writes=["cb"], dma="c_cb")

    phase_load(C)
    pr.barrier()
    n = 0
    if nph < 0:
        phase_kv(C)
        pr.barrier()
        if nph < -1:
            phase_attn(C, 0)
            pr.barrier()
    for layer in range(4):
        if n >= nph:
            break
        phase_ffn(C, layer, 1)
        pr.barrier()
        n += 1
        if n >= nph:
            break
        if layer < 2:
            phase_conv(C, layer)
        else:
            phase_attn(C, layer - 2)
        pr.barrier()
        n += 1
        if n >= nph:
            break
        phase_ffn(C, layer, 2)
        pr.barrier()
        n += 1
        if layer == 1 and n < nph:
            phase_kv(C)
            pr.barrier()
    phase_final(C)

    pr.finalize(nc, stack)
    with nc.Block() as block:
        @block.sync
        def _(e):
            pr.emit("sp", e, final_streams=("os0", "os1"))

        @block.tensor
        def _(e):
            pr.emit("pe", e)

        @block.scalar
        def _(e):
            pr.emit("act", e)

        @block.vector
        def _(e):
            pr.emit("dve", e)

        @block.gpsimd
        def _(e):
            pr.emit("pool", e)
    stack.close()
    nc._used_w = sorted(C.w.keys())
    return nc


def hk(dc, tg):
    return "h%d_%d" % (dc, tg)


def norm_tg(C, tg, gcol, out_ap, out_key, tmp, sq_eng="act", st_bank=6, out_engs=("dve",)):
    pr = C.pr
    cols = slice(tg * TG, (tg + 1) * TG)
    ps = C.PS[st_bank]
    psk = "ps%d" % st_bank
    for dc in range(NDC):
        sq, sqk = tmp["sq%d" % (dc % 2)]
        hin = C.H[:, dc, cols]
        if sq_eng == "act":
            pr.add("act", lambda e, sq=sq, hin=hin: e.activation(out=sq, in_=hin, func=AF.Square),
                   reads=[hk(dc, tg)], writes=sqk)
        else:
            pr.add(sq_eng, lambda e, sq=sq, hin=hin: e.tensor_tensor(out=sq, in0=hin, in1=hin, op=ALU.mult),
                   reads=[hk(dc, tg)], writes=sqk)
        pr.add("pe", lambda e, sq=sq, dc=dc: e.matmul(ps[:], lhsT=C.onesM, rhs=sq, start=(dc == 0), stop=(dc == NDC - 1)),
               reads=sqk + ["cb"], writes=[psk])
    lnt, lntk = tmp["lnt"]
    rs, rsk = tmp["rs"]
    pr.add("act", lambda e: e.activation(out=lnt, in_=ps[:], func=AF.Ln, bias=1e-6, scale=1.0),
           writes=[psk] + lntk)
    pr.add("act", lambda e: e.activation(out=rs, in_=lnt, func=AF.Exp, scale=-0.5),
           reads=lntk, writes=rsk)
    for dc in range(NDC):
        eng = out_engs[dc % len(out_engs)]
        o = out_ap(dc)
        hin = C.H[:, dc, cols]
        pr.add(eng, lambda e, o=o, hin=hin, dc=dc: e.scalar_tensor_tensor(
            out=o, in0=hin, scalar=C.vcol(gcol + dc), in1=rs, op0=ALU.mult, op1=ALU.mult),
            reads=[hk(dc, tg), "vecs"] + rsk, writes=out_key(dc))


def phase_load(C):
    pr = C.pr
    XS = [C.aw(0, (1024,), F32), C.aw(4096, (1024,), F32)]
    for tt in range(NTT):
        b = tt % 2
        xs = XS[b]
        pr.add("sp", lambda e, xs=xs, tt=tt: e.dma_start(out=xs, in_=C.x[tt * 128:(tt + 1) * 128, :]),
               writes=["xs%d" % b], dma="xs%d" % b)
        tg = tt // 4
        for half in range(2):
            bi = (2 * tt + half) % 4
            ps = C.PS[bi]
            for q in range(4):
                dc = half * 4 + q
                pr.add("pe", lambda e, ps=ps, q=q, xs=xs, dc=dc: e.transpose(
                    ps[:, q * 128:(q + 1) * 128], xs[:, dc * 128:(dc + 1) * 128], C.ident[:]),
                    reads=["xs%d" % b, "ident"], writes=["ps%d" % bi])
            dst = C.H[:, half * 4:(half + 1) * 4, tt * 128:(tt + 1) * 128]
            src = ps[:].rearrange("p (q t) -> p q t", q=4)
            pr.add("dve", lambda e, dst=dst, src=src: e.tensor_copy(out=dst, in_=src),
                   writes=["ps%d" % bi] + [hk(half * 4 + q, tg) for q in range(4)])


def phase_final(C):
    pr = C.pr
    YF = C.aw(0, (NDC, TG), F32)
    OS = [C.aw(32768, (1024,), F32), C.aw(36864, (1024,), F32)]
    tmp = {"sq0": (C.aw(88064, (TG,), BF16), ["sq0"]), "sq1": (C.aw(89088, (TG,), BF16), ["sq1"]),
           "lnt": (C.aw(90112, (TG,), F32), ["lnt"]), "rs": (C.aw(92160, (TG,), F32), ["rs"])}
    for tg in range(NTG):
        norm_tg(C, tg, V_FIN, lambda dc: YF[:, dc, :], lambda dc: ["yf%d" % dc], tmp)
        for tl in range(4):
            tt = tg * 4 + tl
            ob = tt % 2
            for half in range(2):
                bi = (2 * tt + half) % 4
                ps = C.PS[bi]
                for q in range(4):
                    dc = half * 4 + q
                    pr.add("pe", lambda e, ps=ps, q=q, dc=dc, tl=tl: e.transpose(
                        ps[:, q * 128:(q + 1) * 128], YF[:, dc, tl * 128:(tl + 1) * 128], C.ident[:]),
                        reads=["yf%d" % dc, "ident"], writes=["ps%d" % bi])
                dst = OS[ob][:, half * 512:(half + 1) * 512]
                eng = "dve" if half == 0 else "act"
                if eng == "dve":
                    pr.add("dve", lambda e, dst=dst, ps=ps: e.tensor_copy(out=dst, in_=ps[:]),
                           writes=["ps%d" % bi, "os%d_%d" % (ob, half)])
                else:
                    pr.add("act", lambda e, dst=dst, ps=ps: e.activation(out=dst, in_=ps[:], func=AF.Copy),
                           writes=["ps%d" % bi, "os%d_%d" % (ob, half)])
            pr.add("sp", lambda e, ob=ob, tt=tt: e.dma_start(out=C.out[tt * 128:(tt + 1) * 128, :], in_=OS[ob]),
                   reads=["os%d_0" % ob, "os%d_1" % ob], dma="os%d" % ob)


def phase_ffn(C, layer, which):
    pr = C.pr
    pre = "ffn%d_" % which
    wg = C.w[pre + "w_gate"][layer]
    wu = C.w[pre + "w_up"][layer]
    wd = C.w[pre + "w_down"][layer]
    gcol = (V_FFN1 if which == 1 else V_FFN2) + layer * 8
    X = C.Xr[:, :].rearrange("p (a b) -> p a b", a=NDC)
    SLOT = 36864
    WG = [C.aw(s * SLOT, (NDC, 768), BF16) for s in range(2)]
    WU = [C.aw(s * SLOT + 12288, (NDC, 768), BF16) for s in range(2)]
    WD = [C.aw(s * SLOT + 24576, (6, 1024), BF16) for s in range(2)]
    HID = [C.aw(73728, (6, TG), BF16), C.aw(79872, (6, TG), BF16)]
    SG = [C.aw(86016, (TG,), BF16), C.aw(87040, (TG,), BF16)]
    tmp = {"sq0": (C.aw(88064, (TG,), BF16), ["sq0"]), "sq1": (C.aw(89088, (TG,), BF16), ["sq1"]),
           "lnt": (C.aw(90112, (TG,), F32), ["lnt"]), "rs": (C.aw(92160, (TG,), F32), ["rs"])}
    wgv = wg.rearrange("(dc p) f -> p dc f", p=128)
    wuv = wu.rearrange("(dc p) f -> p dc f", p=128)

    def load_slice(s):
        f0, n = SLICES[s]
        slot = s % 2
        k = "slot%d" % slot
        pr.add("pool", lambda e: e.dma_start(out=WG[slot][:, :, 0:n * 128], in_=wgv[:, :, f0 * 128:(f0 + n) * 128]),
               writes=[k], dma="w%d" % slot)
        pr.add("pool", lambda e: e.dma_start(out=WU[slot][:, :, 0:n * 128], in_=wuv[:, :, f0 * 128:(f0 + n) * 128]),
               writes=[k], dma="w%d" % slot)
        pr.add("pool", lambda e: e.dma_start(
            out=WD[slot][:, 0:n, :], in_=wd[f0 * 128:(f0 + n) * 128, :].rearrange("(fc p) d -> p fc d", p=128)),
            writes=[k], dma="w%d" % slot)

    load_slice(0)
    load_slice(1)
    for tg in range(NTG):
        cols = slice(tg * TG, (tg + 1) * TG)
        norm_tg(C, tg, gcol, lambda dc: X[:, dc, cols], lambda dc: ["x%d_%d" % (dc, tg)], tmp)

    units = [(s, tg) for s in range(len(SLICES)) for tg in range(NTG)]
    cnt = [0, 0]

    def gu(ui):
        s, tg = units[ui]
        f0, n = SLICES[s]
        slot = s % 2
        hid = HID[ui % 2]
        cols = slice(tg * TG, (tg + 1) * TG)
        for j in range(n):
            b = cnt[0] % 2
            cnt[0] += 1
            pg = C.PS[b]
            pu = C.PS[2 + b]
            for dc in range(NDC):
                pr.add("pe", lambda e, pg=pg, j=j, dc=dc: e.matmul(
                    pg[:], lhsT=WG[slot][:, dc, j * 128:(j + 1) * 128], rhs=X[:, dc, cols],
                    start=(dc == 0), stop=(dc == NDC - 1)),
                    reads=["slot%d" % slot, "x%d_%d" % (dc, tg)], writes=["ps%d" % b])
            for dc in range(NDC):
                pr.add("pe", lambda e, pu=pu, j=j, dc=dc: e.matmul(
                    pu[:], lhsT=WU[slot][:, dc, j * 128:(j + 1) * 128], rhs=X[:, dc, cols],
                    start=(dc == 0), stop=(dc == NDC - 1)),
                    reads=["slot%d" % slot, "x%d_%d" % (dc, tg)], writes=["ps%d" % (2 + b)])
            sg = SG[b]
            pr.add("act", lambda e, sg=sg, pg=pg: e.activation(out=sg, in_=pg[:], func=AF.Silu),
                   writes=["ps%d" % b, "sg%d" % b])
            pr.add("dve", lambda e, hid=hid, j=j, pu=pu, sg=sg: e.tensor_tensor(
                out=hid[:, j, :], in0=pu[:], in1=sg, op=ALU.mult),
                reads=["sg%d" % b], writes=["ps%d" % (2 + b), "hid%d_%d" % (ui % 2, j)])

    def down(ui):
        s, tg = units[ui]
        f0, n = SLICES[s]
        slot = s % 2
        hid = HID[ui % 2]
        cols = slice(tg * TG, (tg + 1) * TG)
        for dc in range(NDC):
            b = 4 + cnt[1] % 2
            cnt[1] += 1
            pd = C.PS[b]
            for j in range(n):
                pr.add("pe", lambda e, pd=pd, j=j, dc=dc: e.matmul(
                    pd[:], lhsT=WD[slot][:, j, dc * 128:(dc + 1) * 128], rhs=hid[:, j, :],
                    start=(j == 0), stop=(j == n - 1)),
                    reads=["slot%d" % slot, "hid%d_%d" % (ui % 2, j)], writes=["ps%d" % b])
            hh = C.H[:, dc, cols]
            pr.add("dve", lambda e, pd=pd, hh=hh: e.scalar_tensor_tensor(
                out=hh, in0=pd[:], scalar=0.5, in1=hh, op0=ALU.mult, op1=ALU.add),
                reads=[hk(dc, tg)], writes=["ps%d" % b, hk(dc, tg)])

    gu(0)
    for ui in range(len(units)):
        if ui + 1 < len(units):
            gu(ui + 1)
        down(ui)
        s, tg = units[ui]
        if tg == NTG - 1 and s + 2 < len(SLICES):
            load_slice(s + 2)


def phase_conv(C, i):
    pr = C.pr
    w1 = C.w["conv_w_pw1"][i].rearrange("(dc p) f -> p dc f", p=128)
    w2 = C.w["conv_w_pw2"][i].rearrange("(dc p) f -> p dc f", p=128)
    vb = V_CONV + i * 296
    c_b1a, c_b1g, c_bdw, c_lng, c_lnb, c_b2, c_wdw = vb, vb + 8, vb + 16, vb + 24, vb + 32, vb + 40, vb + 48
    gcol = V_MIX + i * 8
    W1 = C.aw(0, (NDC, 2048), BF16)
    W2 = C.aw(32768, (NDC, 1024), BF16)
    G = C.aw(49152, (NDC, 542), BF16)
    Y = C.aw(57856, (NDC, TG), F32)
    YSQ = C.aw(74240, (NDC, TG), BF16)
    SIG = [C.aw(82432, (TG,), F32), C.aw(84480, (TG,), F32)]
    LNT = C.aw(86528, (TG,), F32)
    RS = C.aw(88576, (TG,), F32)
    MEAN = C.aw(90624, (TG,), F32)
    MSQ = C.aw(92672, (TG,), F32)
    NMR = C.aw(94720, (TG,), F32)
    DG = [C.aw(96768, (16, 128), BF16), C.aw(100864, (16, 128), BF16)]
    UT = C.xr(0, (NDC, TG), BF16)
    SS = C.xr(8192, (NDC, TG), BF16)
    YB = C.xr(16384, (NDC, TG), BF16)
    tmp = {"sq0": (C.xr(24576, (TG,), BF16), ["sq0"]), "sq1": (C.xr(25600, (TG,), BF16), ["sq1"]),
           "lnt": (LNT, ["lnt"]), "rs": (RS, ["rs"])}
    for hf in range(4):
        pr.add("pool", lambda e, hf=hf: e.dma_start(out=W1[:, 2 * hf:2 * hf + 2, :], in_=w1[:, 2 * hf:2 * hf + 2, :]),
               writes=["w1"], dma="cw1")
    for hf in range(2):
        pr.add("pool", lambda e, hf=hf: e.dma_start(out=W2[:, 4 * hf:4 * hf + 4, :], in_=w2[:, 4 * hf:4 * hf + 4, :]),
               writes=["w2"], dma="cw2")
    pr.add("dve", lambda e: e.memset(G[:, :, 0:30], 0.0), writes=["g%d" % c for c in range(NDC)])
    identb3 = C.identb.unsqueeze(1)
    wdw3 = C.vecs[:, c_wdw:c_wdw + 8 * CW].rearrange("p (c k) -> p c k", c=NDC)
    dgcnt = 0
    ccnt = 0
    for tg in range(NTG):
        cols = slice(tg * TG, (tg + 1) * TG)
        norm_tg(C, tg, gcol, lambda dc: UT[:, dc, :], lambda dc: ["ut%d" % dc], tmp, st_bank=6)
        for c in range(NDC):
            b = c % 2
            pa = C.PS[b]
            pg = C.PS[2 + b]
            for dc in range(NDC):
                pr.add("pe", lambda e, pa=pa, c=c, dc=dc: e.matmul(
                    pa[:], lhsT=W1[:, dc, c * 128:(c + 1) * 128], rhs=UT[:, dc, :],
                    start=(dc == 0), stop=(dc == NDC - 1)),
                    reads=["w1", "ut%d" % dc], writes=["ps%d" % b])
            for dc in range(NDC):
                pr.add("pe", lambda e, pg=pg, c=c, dc=dc: e.matmul(
                    pg[:], lhsT=W1[:, dc, 1024 + c * 128:1024 + (c + 1) * 128], rhs=UT[:, dc, :],
                    start=(dc == 0), stop=(dc == NDC - 1)),
                    reads=["w1", "ut%d" % dc], writes=["ps%d" % (2 + b)])
            sig = SIG[b]
            pr.add("act", lambda e, sig=sig, pg=pg, c=c: e.activation(
                out=sig, in_=pg[:], func=AF.Sigmoid, bias=C.vcol(c_b1g + c), scale=1.0),
                reads=["vecs"], writes=["ps%d" % (2 + b), "sig%d" % b])
            pr.add("dve", lambda e, pa=pa, sig=sig, c=c: e.scalar_tensor_tensor(
                out=G[:, c, 30:542], in0=pa[:], scalar=C.vcol(c_b1a + c), in1=sig, op0=ALU.add, op1=ALU.mult),
                reads=["sig%d" % b, "vecs"], writes=["ps%d" % b, "g%d" % c])
            pc = C.PS[4 + ccnt % 2]
            pck = "ps%d" % (4 + ccnt % 2)
            ccnt += 1
            for half in range(2):
                k0 = 16 * half
                nk = 16 if half == 0 else CW - 16
                dg = DG[dgcnt % 2]
                dgk = "dg%d" % (dgcnt % 2)
                dgcnt += 1
                in0 = identb3.broadcast_to([128, nk, 128])
                in1 = wdw3[:, c, k0:k0 + nk].unsqueeze(2).broadcast_to([128, nk, 128])
                pr.add("dve", lambda e, dg=dg, nk=nk, in0=in0, in1=in1: e.tensor_tensor(
                    out=dg[:, 0:nk, :], in0=in0, in1=in1, op=ALU.mult),
                    reads=["cb", "vecs"], writes=[dgk])
                for j in range(nk):
                    k = k0 + j
                    pr.add("pe", lambda e, pc=pc, dg=dg, j=j, k=k, c=c: e.matmul(
                        pc[:], lhsT=dg[:, j, :], rhs=G[:, c, k:k + TG], start=(k == 0), stop=(k == CW - 1)),
                        reads=[dgk, "g%d" % c], writes=[pck])
            pr.add("act", lambda e, pc=pc, c=c: e.activation(
                out=Y[:, c, :], in_=pc[:], func=AF.Identity, bias=C.vcol(c_bdw + c), scale=1.0),
                reads=["vecs"], writes=[pck, "y%d" % c])
            pr.add("act", lambda e, pc=pc, c=c: e.activation(
                out=YB[:, c, :], in_=pc[:], func=AF.Identity, bias=C.vcol(c_bdw + c), scale=1.0),
                reads=["vecs"], writes=[pck, "yb%d" % c])
            pr.add("act", lambda e, pc=pc, c=c: e.activation(
                out=YSQ[:, c, :], in_=pc[:], func=AF.Square, bias=C.vcol(c_bdw + c), scale=1.0),
                reads=["vecs"], writes=[pck, "ysq%d" % c])
            pr.add("dve", lambda e, c=c: e.tensor_copy(out=G[:, c, 0:30], in_=G[:, c, 512:542]),
                   reads=["g%d" % c], writes=["g%d" % c])
        for c in range(NDC):
            pr.add("pe", lambda e, c=c: e.matmul(C.PS[6][:], lhsT=C.onesM, rhs=YB[:, c, :],
                                                 start=(c == 0), stop=(c == NDC - 1)),
                   reads=["yb%d" % c, "cb"], writes=["ps6"])
        for c in range(NDC):
            pr.add("pe", lambda e, c=c: e.matmul(C.PS[7][:], lhsT=C.onesM, rhs=YSQ[:, c, :],
                                                 start=(c == 0), stop=(c == NDC - 1)),
                   reads=["ysq%d" % c, "cb"], writes=["ps7"])
        pr.add("dve", lambda e: e.tensor_copy(out=MEAN, in_=C.PS[6][:]), writes=["ps6", "mean"])
        pr.add("dve", lambda e: e.tensor_tensor(out=MSQ, in0=MEAN, in1=MEAN, op=ALU.mult),
               reads=["mean"], writes=["msq"])
        pr.add("dve", lambda e: e.tensor_tensor(out=MSQ, in0=C.PS[7][:], in1=MSQ, op=ALU.subtract),
               reads=["msq"], writes=["ps7", "msq"])
        pr.add("act", lambda e: e.activation(out=LNT, in_=MSQ, func=AF.Ln, bias=1e-5, scale=1.0),
               reads=["msq"], writes=["lnt"])
        pr.add("act", lambda e: e.activation(out=RS, in_=LNT, func=AF.Exp, scale=-0.5),
               reads=["lnt"], writes=["rs"])
        pr.add("dve", lambda e: e.scalar_tensor_tensor(out=NMR, in0=MEAN, scalar=-1.0, in1=RS, op0=ALU.mult, op1=ALU.mult),
               reads=["mean", "rs"], writes=["nmr"])
        for c in range(NDC):
            pr.add("dve", lambda e, c=c: e.tensor_tensor(out=Y[:, c, :], in0=Y[:, c, :], in1=RS, op=ALU.mult),
                   reads=["y%d" % c, "rs"], writes=["y%d" % c])
        for c in range(NDC):
            pr.add("dve", lambda e, c=c: e.tensor_tensor(out=Y[:, c, :], in0=Y[:, c, :], in1=NMR, op=ALU.add),
                   reads=["y%d" % c, "nmr"], writes=["y%d" % c])
        for c in range(NDC):
            pr.add("act", lambda e, c=c: e.activation(
                out=SS[:, c, :], in_=Y[:, c, :], func=AF.Silu, bias=C.vcol(c_lnb + c), scale=C.vcol(c_lng + c)),
                reads=["y%d" % c, "vecs"], writes=["s%d" % c])
        for dc in range(NDC):
            b = dc % 2
            pd = C.PS[b]
            for c in range(NDC):
                pr.add("pe", lambda e, pd=pd, c=c, dc=dc: e.matmul(
                    pd[:], lhsT=W2[:, c, dc * 128:(dc + 1) * 128], rhs=SS[:, c, :],
                    start=(c == 0), stop=(c == NDC - 1)),
                    reads=["w2", "s%d" % c], writes=["ps%d" % b])
            hh = C.H[:, dc, cols]
            pr.add("dve", lambda e, pd=pd, hh=hh, dc=dc: e.scalar_tensor_tensor(
                out=hh, in0=pd[:], scalar=C.vcol(c_b2 + dc), in1=hh, op0=ALU.add, op1=ALU.add),
                reads=[hk(dc, tg), "vecs"], writes=["ps%d" % b, hk(dc, tg)])


def phase_kv(C):
    pr = C.pr
    wkv = C.w["w_kv"].rearrange("(dc p) f -> p dc f", p=128)
    KT = C.aw(0, (NDC, S), BF16)
    V = C.aw(32768, (NTT, D), BF16)
    WKV = C.aw(65536, (NDC, 2048), BF16)
    tmp = {"lnt": (C.aw(98304, (TG,), F32), ["lnt"]), "rs": (C.aw(100352, (TG,), F32), ["rs"]),
           "sq0": (C.aw(102400, (TG,), BF16), ["sq0"]), "sq1": (C.aw(103424, (TG,), BF16), ["sq1"])}
    UT = C.xr(0, (NDC, TG), BF16)
    for hf in range(4):
        pr.add("pool", lambda e, hf=hf: e.dma_start(out=WKV[:, 2 * hf:2 * hf + 2, :], in_=wkv[:, 2 * hf:2 * hf + 2, :]),
               writes=["wkv"], dma="wkv")
    cnt = 0
    for tg in range(NTG):
        cols = slice(tg * TG, (tg + 1) * TG)
        norm_tg(C, tg, V_KV, lambda dc: UT[:, dc, :], lambda dc: ["ut%d" % dc], tmp)
        for c in range(NDC):
            b = cnt % 4
            cnt += 1
            ps = C.PS[b]
            for dc in range(NDC):
                pr.add("pe", lambda e, ps=ps, c=c, dc=dc: e.matmul(
                    ps[:], lhsT=WKV[:, dc, c * 128:(c + 1) * 128], rhs=UT[:, dc, :],
                    start=(dc == 0), stop=(dc == NDC - 1)),
                    reads=["wkv", "ut%d" % dc], writes=["ps%d" % b])
            kdst = KT[:, c, cols]
            pr.add("act", lambda e, ps=ps, kdst=kdst: e.activation(out=kdst, in_=ps[:], func=AF.Copy),
                   writes=["ps%d" % b, "K"])
        for tl in range(4):
            tt = tg * 4 + tl
            for half in range(2):
                b = cnt % 4
                cnt += 1
                ps = C.PS[b]
                for dc in range(NDC):
                    pr.add("pe", lambda e, ps=ps, dc=dc, tl=tl, half=half: e.matmul(
                        ps[:], lhsT=UT[:, dc, tl * 128:(tl + 1) * 128],
                        rhs=WKV[:, dc, 1024 + half * 512:1024 + (half + 1) * 512],
                        start=(dc == 0), stop=(dc == NDC - 1)),
                        reads=["wkv", "ut%d" % dc], writes=["ps%d" % b])
                pr.add("dve", lambda e, ps=ps, tt=tt, half=half: e.tensor_copy(
                    out=V[:, tt, half * 512:(half + 1) * 512], in_=ps[:]),
                    writes=["ps%d" % b, "V"])
    KTf = C.AW[:, 0:16384]
    Vf = C.AW[:, 16384:32768]
    for q in range(4):
        pr.add("sp", lambda e, q=q: e.dma_start(out=C.kscr[:, q * 4096:(q + 1) * 4096], in_=KTf[:, q * 4096:(q + 1) * 4096]),
               reads=["K"], dma="kvs")
    for q in range(4):
        pr.add("sp", lambda e, q=q: e.dma_start(out=C.vscr[:, q * 4096:(q + 1) * 4096], in_=Vf[:, q * 4096:(q + 1) * 4096]),
               reads=["V"], dma="kvs")


def phase_attn(C, i):
    pr = C.pr
    wq = C.w["attn_w_q"][i].rearrange("(dc p) f -> p dc f", p=128)
    wo = C.w["attn_w_o"][i].rearrange("(dc p) f -> p dc f", p=128)
    gcol = V_MIX + (2 + i) * 8
    KT = C.aw(0, (NDC, S), BF16)
    V = C.aw(32768, (NTT, D), BF16)
    WQ = C.aw(65536, (NDC, 1024), BF16)
    WO = C.aw(81920, (NDC, 1024), BF16)
    Eb = [C.aw(98304, (TG,), F32), C.aw(100352, (TG,), F32)]
    Lb = [C.xr(4096, (TG,), BF16), C.xr(5120, (TG,), BF16)]
    lk = [["ut4"], ["ut5"]]
    Wb = [C.aw(102400, (TG,), BF16), C.aw(103424, (TG,), BF16), C.aw(104448, (TG,), BF16)]
    SSt = C.aw(105472, (TG,), BF16)
    UT = C.xr(0, (NDC, TG), BF16)
    QE = C.xr(8192, (NDC, TG), BF16)
    QO = C.xr(16384, (NDC, TG), BF16)
    OT = C.xr(24576, (NDC, TG), BF16)
    XRb = [C.xr(0, (TG,), F32), C.xr(2048, (TG,), F32)]
    xrk = [["ut0", "ut1"], ["ut2", "ut3"]]
    tmp = {"sq0": (C.xr(16384, (TG,), BF16), ["qo0"]), "sq1": (C.xr(17408, (TG,), BF16), ["qo1"]),
           "lnt": (C.xr(18432, (TG,), F32), ["qo2", "qo3"]), "rs": (C.xr(20480, (TG,), F32), ["qo4", "qo5"])}
    KTf = C.AW[:, 0:16384]
    Vf = C.AW[:, 16384:32768]
    for q in range(4):
        pr.add("sp", lambda e, q=q: e.dma_start(out=KTf[:, q * 4096:(q + 1) * 4096], in_=C.kscr[:, q * 4096:(q + 1) * 4096]),
               writes=["K"], dma="kld")
    for q in range(4):
        pr.add("sp", lambda e, q=q: e.dma_start(out=Vf[:, q * 4096:(q + 1) * 4096], in_=C.vscr[:, q * 4096:(q + 1) * 4096]),
               writes=["V"], dma="vld")
    for hf in range(2):
        pr.add("pool", lambda e, hf=hf: e.dma_start(out=WQ[:, 4 * hf:4 * hf + 4, :], in_=wq[:, 4 * hf:4 * hf + 4, :]),
               writes=["wq"], dma="wq")
    for hf in range(2):
        pr.add("pool", lambda e, hf=hf: e.dma_start(out=WO[:, 4 * hf:4 * hf + 4, :], in_=wo[:, 4 * hf:4 * hf + 4, :]),
               writes=["wo"], dma="wo")

    qcnt = 0
    for tg in range(NTG):
        cols = slice(tg * TG, (tg + 1) * TG)
        norm_tg(C, tg, gcol, lambda dc: UT[:, dc, :], lambda dc: ["ut%d" % dc], tmp, sq_eng="dve", st_bank=3)
        pr.add(AENG, lambda e: e.memset(QE[64:128, :, :], 0.0), writes=["qe%d" % c for c in range(NDC)])
        pr.add(AENG, lambda e: e.memset(QO[0:64, :, :], 0.0), writes=["qo%d" % c for c in range(NDC)])
        for c in range(NDC):
            b = qcnt % 3
            qcnt += 1
            ps = C.PS[b]
            for dc in range(NDC):
                pr.add("pe", lambda e, ps=ps, c=c, dc=dc: e.matmul(
                    ps[:], lhsT=WQ[:, dc, c * 128:(c + 1) * 128], rhs=UT[:, dc, :],
                    start=(dc == 0), stop=(dc == NDC - 1)),
                    reads=["wq", "ut%d" % dc], writes=["ps%d" % b])
            pr.add("dve", lambda e, ps=ps, c=c: e.tensor_scalar(
                out=QE[0:64, c, :], in0=ps[0:64, :], scalar1=0.125, scalar2=None, op0=ALU.mult),
                writes=["ps%d" % b, "qe%d" % c])
            pr.add("dve", lambda e, ps=ps, c=c: e.tensor_scalar(
                out=QO[64:128, c, :], in0=ps[64:128, :], scalar1=0.125, scalar2=None, op0=ALU.mult),
                writes=["ps%d" % b, "qo%d" % c])
        kmax = 4 * tg + 3
        items = []
        for c in range(NDC):
            for j2 in range(2):
                for kb in range(kmax, -1, -1):
                    items.append((c, j2, kb))
        n = len(items)

        def info(it):
            c, j2, kb = items[it]
            c0 = (kb - 4 * tg) * 128 if kb >= 4 * tg else 0
            return c, j2, kb, c0

        def z_pe(it):
            c, j2, kb, c0 = info(it)
            b = it % 2
            pz = C.PS[b]
            Q = QE if j2 == 0 else QO
            qk = ("qe%d" if j2 == 0 else "qo%d") % c
            pr.add("pe", lambda e: e.matmul(pz[:, c0:TG], lhsT=KT[:, c, kb * 128:(kb + 1) * 128], rhs=Q[:, c, c0:TG],
                                            start=True, stop=True),
                   reads=["K", qk], writes=["ps%d" % b])

        def exp_act(it):
            c, j2, kb, c0 = info(it)
            b = it % 2
            pz = C.PS[b]
            E = Eb[it % 2]
            pr.add("act", lambda e: e.activation(out=E[:, c0:TG], in_=pz[:, c0:TG], func=AF.Exp),
                   writes=["ps%d" % b, "E%d" % (it % 2)])

        def ln_act(it):
            c, j2, kb, c0 = info(it)
            L = Lb[it % 2]
            E = Eb[it % 2]
            pr.add("act", lambda e: e.activation(out=L[:, c0:TG], in_=E[:, c0:TG], func=AF.Ln, bias=1.0, scale=1.0),
                   reads=["E%d" % (it % 2)], writes=lk[it % 2])

        def maskl_dve(it):
            c, j2, kb, c0 = info(it)
            L = Lb[it % 2]
            if kb >= 4 * tg:
                pr.add("dve", lambda e: e.tensor_tensor(out=L[:, c0:c0 + 128], in0=L[:, c0:c0 + 128], in1=C.triu, op=ALU.mult),
                       reads=lk[it % 2] + ["cb"], writes=lk[it % 2])

        def r_pe(it):
            c, j2, kb, c0 = info(it)
            b = 2 + it % 2
            pb = C.PS[b]
            L = Lb[it % 2]
            first = (kb == 4 * tg + 3)
            pr.add("pe", lambda e: e.matmul(pb[:, c0:TG], lhsT=C.tril, rhs=L[:, c0:TG], start=True, stop=first),
                   reads=lk[it % 2] + ["cb"], writes=["ps%d" % b])
            if not first:
                pr.add("pe", lambda e: e.matmul(pb[:, c0:TG], lhsT=C.ones1, rhs=SSt[:, c0:TG], start=False, stop=True),
                       reads=["ss", "cb"], writes=["ps%d" % b])

        def ssum_dve(it):
            c, j2, kb, c0 = info(it)
            L = Lb[it % 2]
            first = (kb == 4 * tg + 3)
            if first:
                pr.add("dve", lambda e: e.tensor_copy(out=SSt[:, c0:TG], in_=L[:, c0:TG]), reads=lk[it % 2], writes=["ss"])
                if c0 > 0:
                    pr.add("dve", lambda e: e.memset(SSt[:, 0:c0], 0.0), writes=["ss"])
            elif kb > 0:
                pr.add("dve", lambda e: e.tensor_tensor(out=SSt[:, c0:TG], in0=SSt[:, c0:TG], in1=L[:, c0:TG], op=ALU.add),
                       reads=["ss"] + lk[it % 2], writes=["ss"])

        def xr_act(it):
            c, j2, kb, c0 = info(it)
            b = 2 + it % 2
            pb = C.PS[b]
            XR = XRb[it % 2]
            pr.add("act", lambda e: e.activation(out=XR[:, c0:TG], in_=pb[:, c0:TG], func=AF.Exp, scale=-1.0),
                   writes=["ps%d" % b] + xrk[it % 2])

        def w_dve(it):
            c, j2, kb, c0 = info(it)
            E = Eb[it % 2]
            XR = XRb[it % 2]
            Wt = Wb[it % 3]
            wk = "wt%d" % (it % 3)
            pr.add("dve", lambda e: e.tensor_tensor(out=Wt[:, c0:TG], in0=E[:, c0:TG], in1=XR[:, c0:TG], op=ALU.mult),
                   reads=["E%d" % (it % 2)] + xrk[it % 2], writes=[wk])
            if kb >= 4 * tg:
                pr.add("dve", lambda e: e.tensor_tensor(out=Wt[:, c0:c0 + 128], in0=Wt[:, c0:c0 + 128], in1=C.triu, op=ALU.mult),
                       reads=[wk, "cb"], writes=[wk])
                if c0 > 0:
                    pr.add("dve", lambda e: e.memset(Wt[:, 0:c0], 0.0), writes=[wk])

        def pv_pe(it, tg_=tg):
            c, j2, kb, c0 = info(it)
            hp = slice(j2 * 64, (j2 + 1) * 64)
            b = 4 + 2 * (c % 2) + j2
            po = C.PS[b]
            Wt = Wb[it % 3]
            pv_first = (kb == 4 * tg_ + 3)
            pr.add("pe", lambda e: e.matmul(po[:], lhsT=V[:, kb, c * 128:(c + 1) * 128], rhs=Wt[:, :],
                                            start=pv_first, stop=(kb == 0)),
                   reads=["V", "wt%d" % (it % 3)], writes=["ps%d" % b])
            if kb == 0:
                pr.add("dve", lambda e: e.tensor_copy(out=OT[hp, c, :], in_=po[hp, :]),
                       writes=["ps%d" % b, "ot%d" % c])

        for t in range(n + 3):
            if t < n:
                for _j in range(NJUNK + 1):
                    z_pe(t)
                exp_act(t)
                ln_act(t)
                maskl_dve(t)
            if 0 <= t - 1 < n:
                r_pe(t - 1)
                ssum_dve(t - 1)
                xr_act(t - 1)
                w_dve(t - 1)
            if 0 <= t - 3 < n:
                pv_pe(t - 3)
        if C.dbg is not None and tg == 0:
            pr.barrier()
            pr.add("sp", lambda e: e.dma_start(out=C.dbg[0][:, 0:16384], in_=C.Xr[:, 0:16384]), dma="dbg")
            pr.add("sp", lambda e: e.dma_start(out=C.dbg[0][:, 16384:16384 + 3072], in_=C.AW[:, 100352 // 2:106496 // 2]),
                   dma="dbg")
            pr.add("sp", lambda e: e.dma_start(out=C.dbg[1][:, :], in_=Eb[0]), dma="dbg")
            pr.add("sp", lambda e: e.dma_start(out=C.dbg[2][:, :], in_=C.AW[:, 0:32768]), dma="dbg")
            pr.barrier()
        for dc in range(NDC):
            b = qcnt % 3
            qcnt += 1
            ps = C.PS[b]
            for c in range(NDC):
                pr.add("pe", lambda e, ps=ps, c=c, dc=dc: e.matmul(
                    ps[:], lhsT=WO[:, c, dc * 128:(dc + 1) * 128], rhs=OT[:, c, :],
                    start=(c == 0), stop=(c == NDC - 1)),
                    reads=["wo", "ot%d" % c], writes=["ps%d" % b])
            hh = C.H[:, dc, cols]
            pr.add("dve", lambda e, ps=ps, hh=hh: e.tensor_tensor(out=hh, in0=ps[:], in1=hh, op=ALU.add),
                   reads=[hk(dc, tg)], writes=["ps%d" % b, hk(dc, tg)])


def _fm(v):
    return np.ascontiguousarray(np.asarray(v, dtype=np.float32).reshape(8, 128).T)


def _host_tables(inp):
    vecs = np.zeros((128, NV), dtype=np.float32)
    for l in range(4):
        vecs[:, V_FFN1 + l * 8:V_FFN1 + l * 8 + 8] = _fm(inp["ffn1_norm"][l])
        vecs[:, V_MIX + l * 8:V_MIX + l * 8 + 8] = _fm(inp["mix_norm"][l])
        vecs[:, V_FFN2 + l * 8:V_FFN2 + l * 8 + 8] = _fm(inp["ffn2_norm"][l])
    vecs[:, V_KV:V_KV + 8] = _fm(inp["kv_norm"])
    vecs[:, V_FIN:V_FIN + 8] = _fm(inp["final_norm"])
    for i in range(2):
        vb = V_CONV + i * 296
        b1 = np.asarray(inp["conv_b_pw1"][i], dtype=np.float32)
        vecs[:, vb:vb + 8] = _fm(b1[:D])
        vecs[:, vb + 8:vb + 16] = _fm(b1[D:])
        vecs[:, vb + 16:vb + 24] = _fm(inp["conv_b_dw"][i])
        vecs[:, vb + 24:vb + 32] = _fm(inp["conv_ln_g"][i])
        vecs[:, vb + 32:vb + 40] = _fm(inp["conv_ln_b"][i])
        vecs[:, vb + 40:vb + 48] = _fm(inp["conv_b_pw2"][i])
        wdw = np.asarray(inp["conv_w_dw"][i], dtype=np.float32)
        vecs[:, vb + 48:vb + 296] = wdw.T.reshape(8, 128, CW).transpose(1, 0, 2).reshape(128, 8 * CW)
    ident = np.eye(128, dtype=np.float32)
    cb = np.zeros((128, 640), dtype=np.float32)
    cb[:, 0:128] = 1.0 / 1024.0
    cb[:, 128:256] = 1.0
    r = np.arange(128)
    cb[:, 256:384] = (r[:, None] >= r[None, :]).astype(np.float32)
    cb[:, 384:512] = (r[None, :] > r[:, None]).astype(np.float32)
    cb[:, 512:640] = np.eye(128, dtype=np.float32)
    return vecs, ident, cb


_NC_CACHE = {}
NPH = 12


def kernel(**inputs):
    inp = {k: np.asarray(v) for k, v in inputs.items()}
    vecs, ident, cb = _host_tables(inp)
    if NPH not in _NC_CACHE:
        _NC_CACHE[NPH] = _build(NPH)
    nc = _NC_CACHE[NPH]
    shared = {"vecs": vecs, "ident": ident, "cbf": cb}
    for nm in nc._used_w:
        shared[nm] = np.ascontiguousarray(inp[nm], dtype=np.float32)
    x = np.ascontiguousarray(inp["x"], dtype=np.float32)
    in_maps = []
    for b in range(NCORES):
        m = dict(shared)
        m["x"] = x[b]
        in_maps.append(m)
    res = run_bass_kernel_spmd(nc, in_maps, core_ids=list(range(NCORES)), **({"trace": True} if TRACE else {}))
    if TRACE:
        print("EXEC_TIME_NS", res.exec_time_ns)
    out = np.stack([np.asarray(res.results[b]["out"], dtype=np.float32) for b in range(NCORES)], axis=0)
    if NPH < 0:
        kernel.dbg = res.results
    return out
```

```python
import numpy as np
from contextlib import ExitStack

import concourse.bass as bass
import concourse.mybir as mybir
from concourse.bass_utils import run_bass_kernel_spmd

F32 = mybir.dt.float32
BF16 = mybir.dt.bfloat16
AF = mybir.ActivationFunctionType
ALU = mybir.AluOpType

D = 1024
S = 2048
F = 2816
NDC = 8
NFC = 22
TG = 512
NTG = 4
NTT = 16
CW = 31
NCORES = 8
SLICES = [(0, 2), (2, 5), (7, 5), (12, 5), (17, 5)]
AW_BYTES = 108032
POOL_CONV_CHUNKS = ()
SEQ_ATTN = False
TRACE = False
NJUNK = 0
AENG = "dve"

V_FFN1 = 0
V_MIX = 32
V_FFN2 = 64
V_KV = 96
V_FIN = 104
V_CONV = 112
NV = 112 + 2 * 296


class _Op:
    __slots__ = ("eng", "fn", "idx", "waits", "signal", "dma", "sig", "isdma")


class _Stream:
    def __init__(self, name):
        self.name = name
        self.count = 0
        self.sem = None


class Prog:
    ENGS = ("pe", "act", "dve", "pool", "sp")

    def __init__(self):
        self.ops = {e: [] for e in self.ENGS}
        self.lastw = {}
        self.readers = {}
        self.seen = {e: {} for e in self.ENGS}
        self.streams = {}
        self.nwaits = 0

    def stream(self, name):
        st = self.streams.get(name)
        if st is None:
            st = _Stream(name)
            self.streams[name] = st
        return st

    def add(self, eng, fn, reads=(), writes=(), dma=None):
        op = _Op()
        op.eng = eng
        op.fn = fn
        op.idx = len(self.ops[eng])
        op.waits = []
        op.signal = False
        op.dma = None
        op.sig = None
        op.isdma = dma is not None
        for k in reads:
            self._need(op, self.lastw.get(k), True)
        for k in writes:
            self._need(op, self.lastw.get(k), False)
            rd = self.readers.get(k)
            if rd:
                for r in rd.values():
                    self._need(op, r, False)
        if dma is not None:
            st = self.stream(dma)
            st.count += 1
            op.dma = st
        agent = eng if dma is None else "dma:" + dma
        for k in reads:
            self.readers.setdefault(k, {})[agent] = op
        for k in writes:
            self.lastw[k] = op
            self.readers[k] = {}
        self.ops[eng].append(op)
        return op

    def _need(self, X, P_, raw):
        if P_ is None or P_ is X:
            return
        e = X.eng
        if P_.dma is not None:
            st = P_.dma
            val = 16 * st.count
            key = ("s", st.name)
            if self.seen[e].get(key, 0) >= val:
                return
            X.waits.append((st, val))
            self.seen[e][key] = val
            self.nwaits += 1
            return
        if P_.eng == e and not X.isdma:
            if e == "pe" or e == "sp":
                return
            if not raw:
                return
            if e != "pool" and X.idx - P_.idx >= 4:
                return
        if self.seen[e].get(P_.eng, -1) >= P_.idx:
            return
        P_.signal = True
        X.waits.append(P_)
        self.seen[e][P_.eng] = P_.idx
        self.nwaits += 1

    def barrier(self):
        lasts = [self.ops[e][-1] for e in self.ENGS if e != "sp" and self.ops[e]]
        hub = _Op()
        hub.eng = "sp"
        hub.fn = lambda eng: eng.nop()
        hub.idx = len(self.ops["sp"])
        hub.waits = []
        hub.signal = False
        hub.dma = None
        hub.sig = None
        hub.isdma = False
        for P_ in lasts:
            self._need(hub, P_, True)
        for st in self.streams.values():
            if st.count > 0:
                key = ("s", st.name)
                val = 16 * st.count
                if self.seen["sp"].get(key, 0) < val:
                    hub.waits.append((st, val))
                    self.seen["sp"][key] = val
        self.ops["sp"].append(hub)
        for e in self.ENGS:
            if e == "sp":
                continue
            op = _Op()
            op.eng = e
            op.fn = lambda eng: eng.nop()
            op.idx = len(self.ops[e])
            op.waits = []
            op.signal = False
            op.dma = None
            op.sig = None
            op.isdma = False
            self._need(op, hub, True)
            self.ops[e].append(op)

    def finalize(self, nc, stack, nsig=8):
        for st in self.streams.values():
            st.sem = stack.enter_context(nc.semaphore("d_" + st.name))
        self.sig_sems = {}
        for e in self.ENGS:
            sems = [stack.enter_context(nc.semaphore("g_%s_%d" % (e, i))) for i in range(nsig)]
            self.sig_sems[e] = sems
            c = 0
            for op in self.ops[e]:
                if op.signal and op.dma is None:
                    op.sig = (sems[c % nsig], c // nsig + 1)
                    c += 1

    def emit(self, e, eng, final_streams=()):
        for op in self.ops[e]:
            for w in op.waits:
                if isinstance(w, tuple):
                    eng.wait_ge(w[0].sem, w[1])
                else:
                    eng.wait_ge(w.sig[0], w.sig[1])
            ins = op.fn(eng)
            if op.dma is not None:
                ins.then_inc(op.dma.sem, 16)
            elif op.signal:
                ins.then_inc(op.sig[0], 1)
        for name in final_streams:
            st = self.streams[name]
            eng.wait_ge(st.sem, 16 * st.count)


class Ctx:
    pass


def _build(nph=12):
    nc = bass.Bass("TRN2", target_bir_lowering=False)
    stack = ExitStack()
    C = Ctx()
    C.nc = nc
    pr = Prog()
    C.pr = pr

    def din(name, shape):
        return nc.dram_tensor(name, list(shape), F32, kind="ExternalInput").ap()

    C.x = din("x", (S, D))
    shapes = {"ffn1_w_gate": (4, D, F), "ffn1_w_up": (4, D, F), "ffn1_w_down": (4, F, D),
              "ffn2_w_gate": (4, D, F), "ffn2_w_up": (4, D, F), "ffn2_w_down": (4, F, D),
              "conv_w_pw1": (2, D, 2 * D), "conv_w_pw2": (2, D, D), "w_kv": (D, 2 * D),
              "attn_w_q": (2, D, D), "attn_w_o": (2, D, D)}

    class LazyW(dict):
        def __missing__(self, nm):
            self[nm] = din(nm, shapes[nm])
            return self[nm]
    C.w = LazyW()
    C.vecs_d = din("vecs", (128, NV))
    C.ident_d = din("ident", (128, 128))
    C.cb_d = din("cbf", (128, 640))
    C.out = nc.dram_tensor("out", [S, D], F32, kind="ExternalOutput").ap()
    skind = "ExternalOutput" if nph < 0 else "Internal"
    C.kscr = nc.dram_tensor("kscr", [128, 16384], BF16, kind=skind).ap()
    C.vscr = nc.dram_tensor("vscr", [128, 16384], BF16, kind=skind).ap()

    C.dbg = None
    if nph == -3:
        C.dbg = (nc.dram_tensor("dbg_bf", [128, 16384 + 3072], BF16, kind="ExternalOutput").ap(),
                 nc.dram_tensor("dbg_f32", [128, 512], F32, kind="ExternalOutput").ap(),
                 nc.dram_tensor("dbg_kv", [128, 32768], BF16, kind="ExternalOutput").ap())
    C.H = stack.enter_context(nc.sbuf_tensor("H", [128, NDC, S], F32))
    C.Xr = stack.enter_context(nc.sbuf_tensor("Xr", [128, 16384], BF16))
    C.AW = stack.enter_context(nc.sbuf_tensor("AW", [128, AW_BYTES // 2], BF16))
    C.vecs = stack.enter_context(nc.sbuf_tensor("vecs_sb", [128, NV], F32))
    C.ident = stack.enter_context(nc.sbuf_tensor("ident_sb", [128, 128], F32))
    C.cb = stack.enter_context(nc.sbuf_tensor("cb_sb", [128, 640], BF16))
    C.PS = [stack.enter_context(nc.psum_tensor("ps%d" % i, [128, 512], F32)) for i in range(8)]

    def carve(base, off, shape, dtype):
        n = 1
        for s_ in shape:
            n *= s_
        nb = n * (4 if dtype == F32 else 2)
        assert off % 4 == 0
        v = base[:, off // 2:(off + nb) // 2]
        if dtype == F32:
            v = v.bitcast(F32)
        if len(shape) == 2:
            v = v.rearrange("p (a b) -> p a b", a=shape[0])
        return v

    C.aw = lambda off, shape, dtype: carve(C.AW, off, shape, dtype)
    C.xr = lambda off, shape, dtype: carve(C.Xr, off, shape, dtype)
    C.onesM = C.cb[:, 0:128]
    C.ones1 = C.cb[:, 128:256]
    C.tril = C.cb[:, 256:384]
    C.triu = C.cb[:, 384:512]
    C.identb = C.cb[:, 512:640]

    def vcol(c):
        return C.vecs[:, c:c + 1]
    C.vcol = vcol

    pr.add("sp", lambda e: e.dma_start(out=C.vecs[:], in_=C.vecs_d), writes=["vecs"], dma="c_vecs")
    pr.add("sp", lambda e: e.dma_start(out=C.ident[:], in_=C.ident_d), writes=["ident"], dma="c_ident")
    pr.add("pool", lambda e: e.dma_start(out=C.cb[:], in_=C.cb_d), writes=["cb"], dma="c_cb")

    phase_load(C)
    pr.barrier()
    n = 0
    if nph < 0:
        phase_kv(C)
        pr.barrier()
        if nph < -1:
            phase_attn(C, 0)
            pr.barrier()
    for layer in range(4):
        if n >= nph:
            break
        phase_ffn(C, layer, 1)
        pr.barrier()
        n += 1
        if n >= nph:
            break
        if layer < 2:
            phase_conv(C, layer)
        else:
            phase_attn(C, layer - 2)
        pr.barrier()
        n += 1
        if n >= nph:
            break
        phase_ffn(C, layer, 2)
        pr.barrier()
        n += 1
        if layer == 1 and n < nph:
            phase_kv(C)
            pr.barrier()
    phase_final(C)

    pr.finalize(nc, stack)
    with nc.Block() as block:
        @block.sync
        def _(e):
            pr.emit("sp", e, final_streams=("os0", "os1"))

        @block.tensor
        def _(e):
            pr.emit("pe", e)

        @block.scalar
        def _(e):
            pr.emit("act", e)

        @block.vector
        def _(e):
            pr.emit("dve", e)

        @block.gpsimd
        def _(e):
            pr.emit("pool", e)
    stack.close()
    nc._used_w = sorted(C.w.keys())
    return nc


def hk(dc, tg):
    return "h%d_%d" % (dc, tg)


def norm_tg(C, tg, gcol, out_ap, out_key, tmp, sq_eng="act", st_bank=6, out_engs=("dve",)):
    pr = C.pr
    cols = slice(tg * TG, (tg + 1) * TG)
    ps = C.PS[st_bank]
    psk = "ps%d" % st_bank
    for dc in range(NDC):
        sq, sqk = tmp["sq%d" % (dc % 2)]
        hin = C.H[:, dc, cols]
        if sq_eng == "act":
            pr.add("act", lambda e, sq=sq, hin=hin: e.activation(out=sq, in_=hin, func=AF.Square),
                   reads=[hk(dc, tg)], writes=sqk)
        else:
            pr.add(sq_eng, lambda e, sq=sq, hin=hin: e.tensor_tensor(out=sq, in0=hin, in1=hin, op=ALU.mult),
                   reads=[hk(dc, tg)], writes=sqk)
        pr.add("pe", lambda e, sq=sq, dc=dc: e.matmul(ps[:], lhsT=C.onesM, rhs=sq, start=(dc == 0), stop=(dc == NDC - 1)),
               reads=sqk + ["cb"], writes=[psk])
    lnt, lntk = tmp["lnt"]
    rs, rsk = tmp["rs"]
    pr.add("act", lambda e: e.activation(out=lnt, in_=ps[:], func=AF.Ln, bias=1e-6, scale=1.0),
           writes=[psk] + lntk)
    pr.add("act", lambda e: e.activation(out=rs, in_=lnt, func=AF.Exp, scale=-0.5),
           reads=lntk, writes=rsk)
    for dc in range(NDC):
        eng = out_engs[dc % len(out_engs)]
        o = out_ap(dc)
        hin = C.H[:, dc, cols]
        pr.add(eng, lambda e, o=o, hin=hin, dc=dc: e.scalar_tensor_tensor(
            out=o, in0=hin, scalar=C.vcol(gcol + dc), in1=rs, op0=ALU.mult, op1=ALU.mult),
            reads=[hk(dc, tg), "vecs"] + rsk, writes=out_key(dc))


def phase_load(C):
    pr = C.pr
    XS = [C.aw(0, (1024,), F32), C.aw(4096, (1024,), F32)]
    for tt in range(NTT):
        b = tt % 2
        xs = XS[b]
        pr.add("sp", lambda e, xs=xs, tt=tt: e.dma_start(out=xs, in_=C.x[tt * 128:(tt + 1) * 128, :]),
               writes=["xs%d" % b], dma="xs%d" % b)
        tg = tt // 4
        for half in range(2):
            bi = (2 * tt + half) % 4
            ps = C.PS[bi]
            for q in range(4):
                dc = half * 4 + q
                pr.add("pe", lambda e, ps=ps, q=q, xs=xs, dc=dc: e.transpose(
                    ps[:, q * 128:(q + 1) * 128], xs[:, dc * 128:(dc + 1) * 128], C.ident[:]),
                    reads=["xs%d" % b, "ident"], writes=["ps%d" % bi])
            dst = C.H[:, half * 4:(half + 1) * 4, tt * 128:(tt + 1) * 128]
            src = ps[:].rearrange("p (q t) -> p q t", q=4)
            pr.add("dve", lambda e, dst=dst, src=src: e.tensor_copy(out=dst, in_=src),
                   writes=["ps%d" % bi] + [hk(half * 4 + q, tg) for q in range(4)])


def phase_final(C):
    pr = C.pr
    YF = C.aw(0, (NDC, TG), F32)
    OS = [C.aw(32768, (1024,), F32), C.aw(36864, (1024,), F32)]
    tmp = {"sq0": (C.aw(88064, (TG,), BF16), ["sq0"]), "sq1": (C.aw(89088, (TG,), BF16), ["sq1"]),
           "lnt": (C.aw(90112, (TG,), F32), ["lnt"]), "rs": (C.aw(92160, (TG,), F32), ["rs"])}
    for tg in range(NTG):
        norm_tg(C, tg, V_FIN, lambda dc: YF[:, dc, :], lambda dc: ["yf%d" % dc], tmp)
        for tl in range(4):
            tt = tg * 4 + tl
            ob = tt % 2
            for half in range(2):
                bi = (2 * tt + half) % 4
                ps = C.PS[bi]
                for q in range(4):
                    dc = half * 4 + q
                    pr.add("pe", lambda e, ps=ps, q=q, dc=dc, tl=tl: e.transpose(
                        ps[:, q * 128:(q + 1) * 128], YF[:, dc, tl * 128:(tl + 1) * 128], C.ident[:]),
                        reads=["yf%d" % dc, "ident"], writes=["ps%d" % bi])
                dst = OS[ob][:, half * 512:(half + 1) * 512]
                eng = "dve" if half == 0 else "act"
                if eng == "dve":
                    pr.add("dve", lambda e, dst=dst, ps=ps: e.tensor_copy(out=dst, in_=ps[:]),
                           writes=["ps%d" % bi, "os%d_%d" % (ob, half)])
                else:
                    pr.add("act", lambda e, dst=dst, ps=ps: e.activation(out=dst, in_=ps[:], func=AF.Copy),
                           writes=["ps%d" % bi, "os%d_%d" % (ob, half)])
            pr.add("sp", lambda e, ob=ob, tt=tt: e.dma_start(out=C.out[tt * 128:(tt + 1) * 128, :], in_=OS[ob]),
                   reads=["os%d_0" % ob, "os%d_1" % ob], dma="os%d" % ob)


def phase_ffn(C, layer, which):
    pr = C.pr
    pre = "ffn%d_" % which
    wg = C.w[pre + "w_gate"][layer]
    wu = C.w[pre + "w_up"][layer]
    wd = C.w[pre + "w_down"][layer]
    gcol = (V_FFN1 if which == 1 else V_FFN2) + layer * 8
    X = C.Xr[:, :].rearrange("p (a b) -> p a b", a=NDC)
    SLOT = 36864
    WG = [C.aw(s * SLOT, (NDC, 768), BF16) for s in range(2)]
    WU = [C.aw(s * SLOT + 12288, (NDC, 768), BF16) for s in range(2)]
    WD = [C.aw(s * SLOT + 24576, (6, 1024), BF16) for s in range(2)]
    HID = [C.aw(73728, (6, TG), BF16), C.aw(79872, (6, TG), BF16)]
    SG = [C.aw(86016, (TG,), BF16), C.aw(87040, (TG,), BF16)]
    tmp = {"sq0": (C.aw(88064, (TG,), BF16), ["sq0"]), "sq1": (C.aw(89088, (TG,), BF16), ["sq1"]),
           "lnt": (C.aw(90112, (TG,), F32), ["lnt"]), "rs": (C.aw(92160, (TG,), F32), ["rs"])}
    wgv = wg.rearrange("(dc p) f -> p dc f", p=128)
    wuv = wu.rearrange("(dc p) f -> p dc f", p=128)

    def load_slice(s):
        f0, n = SLICES[s]
        slot = s % 2
        k = "slot%d" % slot
        pr.add("pool", lambda e: e.dma_start(out=WG[slot][:, :, 0:n * 128], in_=wgv[:, :, f0 * 128:(f0 + n) * 128]),
               writes=[k], dma="w%d" % slot)
        pr.add("pool", lambda e: e.dma_start(out=WU[slot][:, :, 0:n * 128], in_=wuv[:, :, f0 * 128:(f0 + n) * 128]),
               writes=[k], dma="w%d" % slot)
        pr.add("pool", lambda e: e.dma_start(
            out=WD[slot][:, 0:n, :], in_=wd[f0 * 128:(f0 + n) * 128, :].rearrange("(fc p) d -> p fc d", p=128)),
            writes=[k], dma="w%d" % slot)

    load_slice(0)
    load_slice(1)
    for tg in range(NTG):
        cols = slice(tg * TG, (tg + 1) * TG)
        norm_tg(C, tg, gcol, lambda dc: X[:, dc, cols], lambda dc: ["x%d_%d" % (dc, tg)], tmp)

    units = [(s, tg) for s in range(len(SLICES)) for tg in range(NTG)]
    cnt = [0, 0]

    def gu(ui):
        s, tg = units[ui]
        f0, n = SLICES[s]
        slot = s % 2
        hid = HID[ui % 2]
        cols = slice(tg * TG, (tg + 1) * TG)
        for j in range(n):
            b = cnt[0] % 2
            cnt[0] += 1
            pg = C.PS[b]
            pu = C.PS[2 + b]
            for dc in range(NDC):
                pr.add("pe", lambda e, pg=pg, j=j, dc=dc: e.matmul(
                    pg[:], lhsT=WG[slot][:, dc, j * 128:(j + 1) * 128], rhs=X[:, dc, cols],
                    start=(dc == 0), stop=(dc == NDC - 1)),
                    reads=["slot%d" % slot, "x%d_%d" % (dc, tg)], writes=["ps%d" % b])
            for dc in range(NDC):
                pr.add("pe", lambda e, pu=pu, j=j, dc=dc: e.matmul(
                    pu[:], lhsT=WU[slot][:, dc, j * 128:(j + 1) * 128], rhs=X[:, dc, cols],
                    start=(dc == 0), stop=(dc == NDC - 1)),
                    reads=["slot%d" % slot, "x%d_%d" % (dc, tg)], writes=["ps%d" % (2 + b)])
            sg = SG[b]
            pr.add("act", lambda e, sg=sg, pg=pg: e.activation(out=sg, in_=pg[:], func=AF.Silu),
                   writes=["ps%d" % b, "sg%d" % b])
            pr.add("dve", lambda e, hid=hid, j=j, pu=pu, sg=sg: e.tensor_tensor(
                out=hid[:, j, :], in0=pu[:], in1=sg, op=ALU.mult),
                reads=["sg%d" % b], writes=["ps%d" % (2 + b), "hid%d_%d" % (ui % 2, j)])

    def down(ui):
        s, tg = units[ui]
        f0, n = SLICES[s]
        slot = s % 2
        hid = HID[ui % 2]
        cols = slice(tg * TG, (tg + 1) * TG)
        for dc in range(NDC):
            b = 4 + cnt[1] % 2
            cnt[1] += 1
            pd = C.PS[b]
            for j in range(n):
                pr.add("pe", lambda e, pd=pd, j=j, dc=dc: e.matmul(
                    pd[:], lhsT=WD[slot][:, j, dc * 128:(dc + 1) * 128], rhs=hid[:, j, :],
                    start=(j == 0), stop=(j == n - 1)),
                    reads=["slot%d" % slot, "hid%d_%d" % (ui % 2, j)], writes=["ps%d" % b])
            hh = C.H[:, dc, cols]
            pr.add("dve", lambda e, pd=pd, hh=hh: e.scalar_tensor_tensor(
                out=hh, in0=pd[:], scalar=0.5, in1=hh, op0=ALU.mult, op1=ALU.add),
                reads=[hk(dc, tg)], writes=["ps%d" % b, hk(dc, tg)])

    gu(0)
    for ui in range(len(units)):
        if ui + 1 < len(units):
            gu(ui + 1)
        down(ui)
        s, tg = units[ui]
        if tg == NTG - 1 and s + 2 < len(SLICES):
            load_slice(s + 2)


def phase_conv(C, i):
    pr = C.pr
    w1 = C.w["conv_w_pw1"][i].rearrange("(dc p) f -> p dc f", p=128)
    w2 = C.w["conv_w_pw2"][i].rearrange("(dc p) f -> p dc f", p=128)
    vb = V_CONV + i * 296
    c_b1a, c_b1g, c_bdw, c_lng, c_lnb, c_b2, c_wdw = vb, vb + 8, vb + 16, vb + 24, vb + 32, vb + 40, vb + 48
    gcol = V_MIX + i * 8
    W1 = C.aw(0, (NDC, 2048), BF16)
    W2 = C.aw(32768, (NDC, 1024), BF16)
    G = C.aw(49152, (NDC, 542), BF16)
    Y = C.aw(57856, (NDC, TG), F32)
    YSQ = C.aw(74240, (NDC, TG), BF16)
    SIG = [C.aw(82432, (TG,), F32), C.aw(84480, (TG,), F32)]
    LNT = C.aw(86528, (TG,), F32)
    RS = C.aw(88576, (TG,), F32)
    MEAN = C.aw(90624, (TG,), F32)
    MSQ = C.aw(92672, (TG,), F32)
    NMR = C.aw(94720, (TG,), F32)
    DG = [C.aw(96768, (16, 128), BF16), C.aw(100864, (16, 128), BF16)]
    UT = C.xr(0, (NDC, TG), BF16)
    SS = C.xr(8192, (NDC, TG), BF16)
    YB = C.xr(16384, (NDC, TG), BF16)
    tmp = {"sq0": (C.xr(24576, (TG,), BF16), ["sq0"]), "sq1": (C.xr(25600, (TG,), BF16), ["sq1"]),
           "lnt": (LNT, ["lnt"]), "rs": (RS, ["rs"])}
    for hf in range(4):
        pr.add("pool", lambda e, hf=hf: e.dma_start(out=W1[:, 2 * hf:2 * hf + 2, :], in_=w1[:, 2 * hf:2 * hf + 2, :]),
               writes=["w1"], dma="cw1")
    for hf in range(2):
        pr.add("pool", lambda e, hf=hf: e.dma_start(out=W2[:, 4 * hf:4 * hf + 4, :], in_=w2[:, 4 * hf:4 * hf + 4, :]),
               writes=["w2"], dma="cw2")
    pr.add("dve", lambda e: e.memset(G[:, :, 0:30], 0.0), writes=["g%d" % c for c in range(NDC)])
    identb3 = C.identb.unsqueeze(1)
    wdw3 = C.vecs[:, c_wdw:c_wdw + 8 * CW].rearrange("p (c k) -> p c k", c=NDC)
    dgcnt = 0
    ccnt = 0
    for tg in range(NTG):
        cols = slice(tg * TG, (tg + 1) * TG)
        norm_tg(C, tg, gcol, lambda dc: UT[:, dc, :], lambda dc: ["ut%d" % dc], tmp, st_bank=6)
        def pw1_glu(c):
            b = c % 2
            pa = C.PS[b]
            pg = C.PS[2 + b]
            for dc in range(NDC):
                pr.add("pe", lambda e, pa=pa, c=c, dc=dc: e.matmul(
                    pa[:], lhsT=W1[:, dc, c * 128:(c + 1) * 128], rhs=UT[:, dc, :],
                    start=(dc == 0), stop=(dc == NDC - 1)),
                    reads=["w1", "ut%d" % dc], writes=["ps%d" % b])
            for dc in range(NDC):
                pr.add("pe", lambda e, pg=pg, c=c, dc=dc: e.matmul(
                    pg[:], lhsT=W1[:, dc, 1024 + c * 128:1024 + (c + 1) * 128], rhs=UT[:, dc, :],
                    start=(dc == 0), stop=(dc == NDC - 1)),
                    reads=["w1", "ut%d" % dc], writes=["ps%d" % (2 + b)])
            sig = SIG[b]
            pr.add("act", lambda e, sig=sig, pg=pg, c=c: e.activation(
                out=sig, in_=pg[:], func=AF.Sigmoid, bias=C.vcol(c_b1g + c), scale=1.0),
                reads=["vecs"], writes=["ps%d" % (2 + b), "sig%d" % b])
            pr.add("dve", lambda e, pa=pa, sig=sig, c=c: e.scalar_tensor_tensor(
                out=G[:, c, 30:542], in0=pa[:], scalar=C.vcol(c_b1a + c), in1=sig, op0=ALU.add, op1=ALU.mult),
                reads=["sig%d" % b, "vecs"], writes=["ps%d" % b, "g%d" % c])

        def taps(c, ccnt, dgcnt):
            pc = C.PS[4 + ccnt % 2]
            pck = "ps%d" % (4 + ccnt % 2)
            for half in range(2):
                k0 = 16 * half
                nk = 16 if half == 0 else CW - 16
                dg = DG[(dgcnt + half) % 2]
                dgk = "dg%d" % ((dgcnt + half) % 2)
                in0 = identb3.broadcast_to([128, nk, 128])
                in1 = wdw3[:, c, k0:k0 + nk].unsqueeze(2).broadcast_to([128, nk, 128])
                pr.add("dve", lambda e, dg=dg, nk=nk, in0=in0, in1=in1: e.tensor_tensor(
                    out=dg[:, 0:nk, :], in0=in0, in1=in1, op=ALU.mult),
                    reads=["cb", "vecs"], writes=[dgk])
                for j in range(nk):
                    k = k0 + j
                    pr.add("pe", lambda e, pc=pc, dg=dg, j=j, k=k, c=c: e.matmul(
                        pc[:], lhsT=dg[:, j, :], rhs=G[:, c, k:k + TG], start=(k == 0), stop=(k == CW - 1)),
                        reads=[dgk, "g%d" % c], writes=[pck])
            pr.add("act", lambda e, pc=pc, c=c: e.activation(
                out=Y[:, c, :], in_=pc[:], func=AF.Identity, bias=C.vcol(c_bdw + c), scale=1.0),
                reads=["vecs"], writes=[pck, "y%d" % c])
            pr.add("act", lambda e, pc=pc, c=c: e.activation(
                out=YB[:, c, :], in_=pc[:], func=AF.Identity, bias=C.vcol(c_bdw + c), scale=1.0),
                reads=["vecs"], writes=[pck, "yb%d" % c])
            pr.add("act", lambda e, pc=pc, c=c: e.activation(
                out=YSQ[:, c, :], in_=pc[:], func=AF.Square, bias=C.vcol(c_bdw + c), scale=1.0),
                reads=["vecs"], writes=[pck, "ysq%d" % c])
            pr.add("dve", lambda e, c=c: e.tensor_copy(out=G[:, c, 0:30], in_=G[:, c, 512:542]),
                   reads=["g%d" % c], writes=["g%d" % c])

        for c in range(NDC + 1):
            if c < NDC:
                pw1_glu(c)
            if c >= 1:
                taps(c - 1, ccnt, dgcnt)
                ccnt += 1
                dgcnt += 2
        for c in range(NDC):
            pr.add("pe", lambda e, c=c: e.matmul(C.PS[6][:], lhsT=C.onesM, rhs=YB[:, c, :],
                                                 start=(c == 0), stop=(c == NDC - 1)),
                   reads=["yb%d" % c, "cb"], writes=["ps6"])
        for c in range(NDC):
            pr.add("pe", lambda e, c=c: e.matmul(C.PS[7][:], lhsT=C.onesM, rhs=YSQ[:, c, :],
                                                 start=(c == 0), stop=(c == NDC - 1)),
                   reads=["ysq%d" % c, "cb"], writes=["ps7"])
        pr.add("dve", lambda e: e.tensor_copy(out=MEAN, in_=C.PS[6][:]), writes=["ps6", "mean"])
        pr.add("dve", lambda e: e.tensor_tensor(out=MSQ, in0=MEAN, in1=MEAN, op=ALU.mult),
               reads=["mean"], writes=["msq"])
        pr.add("dve", lambda e: e.tensor_tensor(out=MSQ, in0=C.PS[7][:], in1=MSQ, op=ALU.subtract),
               reads=["msq"], writes=["ps7", "msq"])
        pr.add("act", lambda e: e.activation(out=LNT, in_=MSQ, func=AF.Ln, bias=1e-5, scale=1.0),
               reads=["msq"], writes=["lnt"])
        pr.add("act", lambda e: e.activation(out=RS, in_=LNT, func=AF.Exp, scale=-0.5),
               reads=["lnt"], writes=["rs"])
        pr.add("dve", lambda e: e.scalar_tensor_tensor(out=NMR, in0=MEAN, scalar=-1.0, in1=RS, op0=ALU.mult, op1=ALU.mult),
               reads=["mean", "rs"], writes=["nmr"])
        for c in range(NDC):
            pr.add("dve", lambda e, c=c: e.tensor_tensor(out=Y[:, c, :], in0=Y[:, c, :], in1=RS, op=ALU.mult),
                   reads=["y%d" % c, "rs"], writes=["y%d" % c])
        for c in range(NDC):
            pr.add("dve", lambda e, c=c: e.tensor_tensor(out=Y[:, c, :], in0=Y[:, c, :], in1=NMR, op=ALU.add),
                   reads=["y%d" % c, "nmr"], writes=["y%d" % c])
        for c in range(NDC):
            pr.add("act", lambda e, c=c: e.activation(
                out=SS[:, c, :], in_=Y[:, c, :], func=AF.Silu, bias=C.vcol(c_lnb + c), scale=C.vcol(c_lng + c)),
                reads=["y%d" % c, "vecs"], writes=["s%d" % c])
        for dc in range(NDC):
            b = dc % 2
            pd = C.PS[b]
            for c in range(NDC):
                pr.add("pe", lambda e, pd=pd, c=c, dc=dc: e.matmul(
                    pd[:], lhsT=W2[:, c, dc * 128:(dc + 1) * 128], rhs=SS[:, c, :],
                    start=(c == 0), stop=(c == NDC - 1)),
                    reads=["w2", "s%d" % c], writes=["ps%d" % b])
            hh = C.H[:, dc, cols]
            pr.add("dve", lambda e, pd=pd, hh=hh, dc=dc: e.scalar_tensor_tensor(
                out=hh, in0=pd[:], scalar=C.vcol(c_b2 + dc), in1=hh, op0=ALU.add, op1=ALU.add),
                reads=[hk(dc, tg), "vecs"], writes=["ps%d" % b, hk(dc, tg)])


def phase_kv(C):
    pr = C.pr
    wkv = C.w["w_kv"].rearrange("(dc p) f -> p dc f", p=128)
    KT = C.aw(0, (NDC, S), BF16)
    V = C.aw(32768, (NTT, D), BF16)
    WKV = C.aw(65536, (NDC, 2048), BF16)
    tmp = {"lnt": (C.aw(98304, (TG,), F32), ["lnt"]), "rs": (C.aw(100352, (TG,), F32), ["rs"]),
           "sq0": (C.aw(102400, (TG,), BF16), ["sq0"]), "sq1": (C.aw(103424, (TG,), BF16), ["sq1"])}
    UT = C.xr(0, (NDC, TG), BF16)
    for hf in range(4):
        pr.add("pool", lambda e, hf=hf: e.dma_start(out=WKV[:, 2 * hf:2 * hf + 2, :], in_=wkv[:, 2 * hf:2 * hf + 2, :]),
               writes=["wkv"], dma="wkv")
    cnt = 0
    for tg in range(NTG):
        cols = slice(tg * TG, (tg + 1) * TG)
        norm_tg(C, tg, V_KV, lambda dc: UT[:, dc, :], lambda dc: ["ut%d" % dc], tmp)
        for c in range(NDC):
            b = cnt % 4
            cnt += 1
            ps = C.PS[b]
            for dc in range(NDC):
                pr.add("pe", lambda e, ps=ps, c=c, dc=dc: e.matmul(
                    ps[:], lhsT=WKV[:, dc, c * 128:(c + 1) * 128], rhs=UT[:, dc, :],
                    start=(dc == 0), stop=(dc == NDC - 1)),
                    reads=["wkv", "ut%d" % dc], writes=["ps%d" % b])
            kdst = KT[:, c, cols]
            pr.add("act", lambda e, ps=ps, kdst=kdst: e.activation(out=kdst, in_=ps[:], func=AF.Copy),
                   writes=["ps%d" % b, "K"])
        for tl in range(4):
            tt = tg * 4 + tl
            for half in range(2):
                b = cnt % 4
                cnt += 1
                ps = C.PS[b]
                for dc in range(NDC):
                    pr.add("pe", lambda e, ps=ps, dc=dc, tl=tl, half=half: e.matmul(
                        ps[:], lhsT=UT[:, dc, tl * 128:(tl + 1) * 128],
                        rhs=WKV[:, dc, 1024 + half * 512:1024 + (half + 1) * 512],
                        start=(dc == 0), stop=(dc == NDC - 1)),
                        reads=["wkv", "ut%d" % dc], writes=["ps%d" % b])
                pr.add("dve", lambda e, ps=ps, tt=tt, half=half: e.tensor_copy(
                    out=V[:, tt, half * 512:(half + 1) * 512], in_=ps[:]),
                    writes=["ps%d" % b, "V"])
    KTf = C.AW[:, 0:16384]
    Vf = C.AW[:, 16384:32768]
    for q in range(4):
        pr.add("sp", lambda e, q=q: e.dma_start(out=C.kscr[:, q * 4096:(q + 1) * 4096], in_=KTf[:, q * 4096:(q + 1) * 4096]),
               reads=["K"], dma="kvs")
    for q in range(4):
        pr.add("sp", lambda e, q=q: e.dma_start(out=C.vscr[:, q * 4096:(q + 1) * 4096], in_=Vf[:, q * 4096:(q + 1) * 4096]),
               reads=["V"], dma="kvs")


def phase_attn(C, i):
    pr = C.pr
    wq = C.w["attn_w_q"][i].rearrange("(dc p) f -> p dc f", p=128)
    wo = C.w["attn_w_o"][i].rearrange("(dc p) f -> p dc f", p=128)
    gcol = V_MIX + (2 + i) * 8
    KT = C.aw(0, (NDC, S), BF16)
    V = C.aw(32768, (NTT, D), BF16)
    WQ = C.aw(65536, (NDC, 1024), BF16)
    WO = C.aw(81920, (NDC, 1024), BF16)
    Eb = [C.aw(98304, (TG,), F32), C.aw(100352, (TG,), F32)]
    Lb = [C.xr(4096, (TG,), BF16), C.xr(5120, (TG,), BF16)]
    lk = [["ut4"], ["ut5"]]
    Wb = [C.aw(102400, (TG,), BF16), C.aw(103424, (TG,), BF16), C.aw(104448, (TG,), BF16)]
    SSt = C.aw(105472, (TG,), BF16)
    UT = C.xr(0, (NDC, TG), BF16)
    QE = C.xr(8192, (NDC, TG), BF16)
    QO = C.xr(16384, (NDC, TG), BF16)
    OT = C.xr(24576, (NDC, TG), BF16)
    XRb = [C.xr(0, (TG,), F32), C.xr(2048, (TG,), F32)]
    xrk = [["ut0", "ut1"], ["ut2", "ut3"]]
    tmp = {"sq0": (C.xr(16384, (TG,), BF16), ["qo0"]), "sq1": (C.xr(17408, (TG,), BF16), ["qo1"]),
           "lnt": (C.xr(18432, (TG,), F32), ["qo2", "qo3"]), "rs": (C.xr(20480, (TG,), F32), ["qo4", "qo5"])}
    KTf = C.AW[:, 0:16384]
    Vf = C.AW[:, 16384:32768]
    for q in range(4):
        pr.add("sp", lambda e, q=q: e.dma_start(out=KTf[:, q * 4096:(q + 1) * 4096], in_=C.kscr[:, q * 4096:(q + 1) * 4096]),
               writes=["K"], dma="kld")
    for q in range(4):
        pr.add("sp", lambda e, q=q: e.dma_start(out=Vf[:, q * 4096:(q + 1) * 4096], in_=C.vscr[:, q * 4096:(q + 1) * 4096]),
               writes=["V"], dma="vld")
    for hf in range(2):
        pr.add("pool", lambda e, hf=hf: e.dma_start(out=WQ[:, 4 * hf:4 * hf + 4, :], in_=wq[:, 4 * hf:4 * hf + 4, :]),
               writes=["wq"], dma="wq")
    for hf in range(2):
        pr.add("pool", lambda e, hf=hf: e.dma_start(out=WO[:, 4 * hf:4 * hf + 4, :], in_=wo[:, 4 * hf:4 * hf + 4, :]),
               writes=["wo"], dma="wo")

    qcnt = 0
    for tg in range(NTG):
        cols = slice(tg * TG, (tg + 1) * TG)
        norm_tg(C, tg, gcol, lambda dc: UT[:, dc, :], lambda dc: ["ut%d" % dc], tmp, sq_eng="dve", st_bank=3)
        pr.add(AENG, lambda e: e.memset(QE[64:128, :, :], 0.0), writes=["qe%d" % c for c in range(NDC)])
        pr.add(AENG, lambda e: e.memset(QO[0:64, :, :], 0.0), writes=["qo%d" % c for c in range(NDC)])
        for c in range(NDC):
            b = qcnt % 3
            qcnt += 1
            ps = C.PS[b]
            for dc in range(NDC):
                pr.add("pe", lambda e, ps=ps, c=c, dc=dc: e.matmul(
                    ps[:], lhsT=WQ[:, dc, c * 128:(c + 1) * 128], rhs=UT[:, dc, :],
                    start=(dc == 0), stop=(dc == NDC - 1)),
                    reads=["wq", "ut%d" % dc], writes=["ps%d" % b])
            pr.add("dve", lambda e, ps=ps, c=c: e.tensor_scalar(
                out=QE[0:64, c, :], in0=ps[0:64, :], scalar1=0.125, scalar2=None, op0=ALU.mult),
                writes=["ps%d" % b, "qe%d" % c])
            pr.add("dve", lambda e, ps=ps, c=c: e.tensor_scalar(
                out=QO[64:128, c, :], in0=ps[64:128, :], scalar1=0.125, scalar2=None, op0=ALU.mult),
                writes=["ps%d" % b, "qo%d" % c])
        kmax = 4 * tg + 3
        items = []
        for c in range(NDC):
            for j2 in range(2):
                for kb in range(kmax, -1, -1):
                    items.append((c, j2, kb))
        n = len(items)

        def info(it):
            c, j2, kb = items[it]
            c0 = (kb - 4 * tg) * 128 if kb >= 4 * tg else 0
            return c, j2, kb, c0

        def z_pe(it):
            c, j2, kb, c0 = info(it)
            b = it % 2
            pz = C.PS[b]
            Q = QE if j2 == 0 else QO
            qk = ("qe%d" if j2 == 0 else "qo%d") % c
            pr.add("pe", lambda e: e.matmul(pz[:, c0:TG], lhsT=KT[:, c, kb * 128:(kb + 1) * 128], rhs=Q[:, c, c0:TG],
                                            start=True, stop=True),
                   reads=["K", qk], writes=["ps%d" % b])

        def exp_act(it):
            c, j2, kb, c0 = info(it)
            b = it % 2
            pz = C.PS[b]
            E = Eb[it % 2]
            pr.add("act", lambda e: e.activation(out=E[:, c0:TG], in_=pz[:, c0:TG], func=AF.Exp),
                   writes=["ps%d" % b, "E%d" % (it % 2)])

        def ln_act(it):
            c, j2, kb, c0 = info(it)
            L = Lb[it % 2]
            E = Eb[it % 2]
            pr.add("act", lambda e: e.activation(out=L[:, c0:TG], in_=E[:, c0:TG], func=AF.Ln, bias=1.0, scale=1.0),
                   reads=["E%d" % (it % 2)], writes=lk[it % 2])

        def maskl_dve(it):
            c, j2, kb, c0 = info(it)
            L = Lb[it % 2]
            if kb >= 4 * tg:
                pr.add("dve", lambda e: e.tensor_tensor(out=L[:, c0:c0 + 128], in0=L[:, c0:c0 + 128], in1=C.triu, op=ALU.mult),
                       reads=lk[it % 2] + ["cb"], writes=lk[it % 2])

        def r_pe(it):
            c, j2, kb, c0 = info(it)
            b = 2 + it % 2
            pb = C.PS[b]
            L = Lb[it % 2]
            first = (kb == 4 * tg + 3)
            pr.add("pe", lambda e: e.matmul(pb[:, c0:TG], lhsT=C.tril, rhs=L[:, c0:TG], start=True, stop=first),
                   reads=lk[it % 2] + ["cb"], writes=["ps%d" % b])
            if not first:
                pr.add("pe", lambda e: e.matmul(pb[:, c0:TG], lhsT=C.ones1, rhs=SSt[:, c0:TG], start=False, stop=True),
                       reads=["ss", "cb"], writes=["ps%d" % b])

        def ssum_dve(it):
            c, j2, kb, c0 = info(it)
            L = Lb[it % 2]
            first = (kb == 4 * tg + 3)
            if first:
                pr.add("dve", lambda e: e.tensor_copy(out=SSt[:, c0:TG], in_=L[:, c0:TG]), reads=lk[it % 2], writes=["ss"])
                if c0 > 0:
                    pr.add("dve", lambda e: e.memset(SSt[:, 0:c0], 0.0), writes=["ss"])
            elif kb > 0:
                pr.add("dve", lambda e: e.tensor_tensor(out=SSt[:, c0:TG], in0=SSt[:, c0:TG], in1=L[:, c0:TG], op=ALU.add),
                       reads=["ss"] + lk[it % 2], writes=["ss"])

        def xr_act(it):
            c, j2, kb, c0 = info(it)
            b = 2 + it % 2
            pb = C.PS[b]
            XR = XRb[it % 2]
            pr.add("act", lambda e: e.activation(out=XR[:, c0:TG], in_=pb[:, c0:TG], func=AF.Exp, scale=-1.0),
                   writes=["ps%d" % b] + xrk[it % 2])

        def w_dve(it):
            c, j2, kb, c0 = info(it)
            E = Eb[it % 2]
            XR = XRb[it % 2]
            Wt = Wb[it % 3]
            wk = "wt%d" % (it % 3)
            pr.add("dve", lambda e: e.tensor_tensor(out=Wt[:, c0:TG], in0=E[:, c0:TG], in1=XR[:, c0:TG], op=ALU.mult),
                   reads=["E%d" % (it % 2)] + xrk[it % 2], writes=[wk])
            if kb >= 4 * tg:
                pr.add("dve", lambda e: e.tensor_tensor(out=Wt[:, c0:c0 + 128], in0=Wt[:, c0:c0 + 128], in1=C.triu, op=ALU.mult),
                       reads=[wk, "cb"], writes=[wk])
                if c0 > 0:
                    pr.add("dve", lambda e: e.memset(Wt[:, 0:c0], 0.0), writes=[wk])

        def pv_pe(it, tg_=tg):
            c, j2, kb, c0 = info(it)
            hp = slice(j2 * 64, (j2 + 1) * 64)
            b = 4 + 2 * (c % 2) + j2
            po = C.PS[b]
            Wt = Wb[it % 3]
            pv_first = (kb == 4 * tg_ + 3)
            pr.add("pe", lambda e: e.matmul(po[:], lhsT=V[:, kb, c * 128:(c + 1) * 128], rhs=Wt[:, :],
                                            start=pv_first, stop=(kb == 0)),
                   reads=["V", "wt%d" % (it % 3)], writes=["ps%d" % b])
            if kb == 0:
                pr.add("dve", lambda e: e.tensor_copy(out=OT[hp, c, :], in_=po[hp, :]),
                       writes=["ps%d" % b, "ot%d" % c])

        for t in range(n + 3):
            if t < n:
                for _j in range(NJUNK + 1):
                    z_pe(t)
                exp_act(t)
                ln_act(t)
                maskl_dve(t)
            if 0 <= t - 1 < n:
                r_pe(t - 1)
                ssum_dve(t - 1)
                xr_act(t - 1)
                w_dve(t - 1)
            if 0 <= t - 3 < n:
                pv_pe(t - 3)
        if C.dbg is not None and tg == 0:
            pr.barrier()
            pr.add("sp", lambda e: e.dma_start(out=C.dbg[0][:, 0:16384], in_=C.Xr[:, 0:16384]), dma="dbg")
            pr.add("sp", lambda e: e.dma_start(out=C.dbg[0][:, 16384:16384 + 3072], in_=C.AW[:, 100352 // 2:106496 // 2]),
                   dma="dbg")
            pr.add("sp", lambda e: e.dma_start(out=C.dbg[1][:, :], in_=Eb[0]), dma="dbg")
            pr.add("sp", lambda e: e.dma_start(out=C.dbg[2][:, :], in_=C.AW[:, 0:32768]), dma="dbg")
            pr.barrier()
        for dc in range(NDC):
            b = qcnt % 3
            qcnt += 1
            ps = C.PS[b]
            for c in range(NDC):
                pr.add("pe", lambda e, ps=ps, c=c, dc=dc: e.matmul(
                    ps[:], lhsT=WO[:, c, dc * 128:(dc + 1) * 128], rhs=OT[:, c, :],
                    start=(c == 0), stop=(c == NDC - 1)),
                    reads=["wo", "ot%d" % c], writes=["ps%d" % b])
            hh = C.H[:, dc, cols]
            pr.add("dve", lambda e, ps=ps, hh=hh: e.tensor_tensor(out=hh, in0=ps[:], in1=hh, op=ALU.add),
                   reads=[hk(dc, tg)], writes=["ps%d" % b, hk(dc, tg)])


def _fm(v):
    return np.ascontiguousarray(np.asarray(v, dtype=np.float32).reshape(8, 128).T)


def _host_tables(inp):
    vecs = np.zeros((128, NV), dtype=np.float32)
    for l in range(4):
        vecs[:, V_FFN1 + l * 8:V_FFN1 + l * 8 + 8] = _fm(inp["ffn1_norm"][l])
        vecs[:, V_MIX + l * 8:V_MIX + l * 8 + 8] = _fm(inp["mix_norm"][l])
        vecs[:, V_FFN2 + l * 8:V_FFN2 + l * 8 + 8] = _fm(inp["ffn2_norm"][l])
    vecs[:, V_KV:V_KV + 8] = _fm(inp["kv_norm"])
    vecs[:, V_FIN:V_FIN + 8] = _fm(inp["final_norm"])
    for i in range(2):
        vb = V_CONV + i * 296
        b1 = np.asarray(inp["conv_b_pw1"][i], dtype=np.float32)
        vecs[:, vb:vb + 8] = _fm(b1[:D])
        vecs[:, vb + 8:vb + 16] = _fm(b1[D:])
        vecs[:, vb + 16:vb + 24] = _fm(inp["conv_b_dw"][i])
        vecs[:, vb + 24:vb + 32] = _fm(inp["conv_ln_g"][i])
        vecs[:, vb + 32:vb + 40] = _fm(inp["conv_ln_b"][i])
        vecs[:, vb + 40:vb + 48] = _fm(inp["conv_b_pw2"][i])
        wdw = np.asarray(inp["conv_w_dw"][i], dtype=np.float32)
        vecs[:, vb + 48:vb + 296] = wdw.T.reshape(8, 128, CW).transpose(1, 0, 2).reshape(128, 8 * CW)
    ident = np.eye(128, dtype=np.float32)
    cb = np.zeros((128, 640), dtype=np.float32)
    cb[:, 0:128] = 1.0 / 1024.0
    cb[:, 128:256] = 1.0
    r = np.arange(128)
    cb[:, 256:384] = (r[:, None] >= r[None, :]).astype(np.float32)
    cb[:, 384:512] = (r[None, :] > r[:, None]).astype(np.float32)
    cb[:, 512:640] = np.eye(128, dtype=np.float32)
    return vecs, ident, cb


_NC_CACHE = {}
NPH = 12


def kernel(**inputs):
    inp = {k: np.asarray(v) for k, v in inputs.items()}
    vecs, ident, cb = _host_tables(inp)
    if NPH not in _NC_CACHE:
        _NC_CACHE[NPH] = _build(NPH)
    nc = _NC_CACHE[NPH]
    shared = {"vecs": vecs, "ident": ident, "cbf": cb}
    for nm in nc._used_w:
        shared[nm] = np.ascontiguousarray(inp[nm], dtype=np.float32)
    x = np.ascontiguousarray(inp["x"], dtype=np.float32)
    in_maps = []
    for b in range(NCORES):
        m = dict(shared)
        m["x"] = x[b]
        in_maps.append(m)
    res = run_bass_kernel_spmd(nc, in_maps, core_ids=list(range(NCORES)), **({"trace": True} if TRACE else {}))
    if TRACE:
        print("EXEC_TIME_NS", res.exec_time_ns)
    out = np.stack([np.asarray(res.results[b]["out"], dtype=np.float32) for b in range(NCORES)], axis=0)
    if NPH < 0:
        kernel.dbg = res.results
    return out
```

```python
import numpy as np
from contextlib import ExitStack

import concourse.bass as bass
import concourse.mybir as mybir
from concourse.bass_utils import run_bass_kernel_spmd

F32 = mybir.dt.float32
BF16 = mybir.dt.bfloat16
AF = mybir.ActivationFunctionType
ALU = mybir.AluOpType

D = 1024
S = 2048
F = 2816
NDC = 8
NFC = 22
TG = 512
NTG = 4
NTT = 16
CW = 31
NCORES = 8
SLICES = [(0, 2), (2, 5), (7, 5), (12, 5), (17, 5)]
AW_BYTES = 108032
POOL_CONV_CHUNKS = ()
SEQ_ATTN = False
TRACE = False
NJUNK = 0
AENG = "dve"

V_FFN1 = 0
V_MIX = 32
V_FFN2 = 64
V_KV = 96
V_FIN = 104
V_CONV = 112
NV = 112 + 2 * 296


class _Op:
    __slots__ = ("eng", "fn", "idx", "waits", "signal", "dma", "sig", "isdma")


class _Stream:
    def __init__(self, name):
        self.name = name
        self.count = 0
        self.sem = None


class Prog:
    ENGS = ("pe", "act", "dve", "pool", "sp")

    def __init__(self):
        self.ops = {e: [] for e in self.ENGS}
        self.lastw = {}
        self.readers = {}
        self.seen = {e: {} for e in self.ENGS}
        self.streams = {}
        self.nwaits = 0

    def stream(self, name):
        st = self.streams.get(name)
        if st is None:
            st = _Stream(name)
            self.streams[name] = st
        return st

    def add(self, eng, fn, reads=(), writes=(), dma=None):
        op = _Op()
        op.eng = eng
        op.fn = fn
        op.idx = len(self.ops[eng])
        op.waits = []
        op.signal = False
        op.dma = None
        op.sig = None
        op.isdma = dma is not None
        for k in reads:
            self._need(op, self.lastw.get(k), True)
        for k in writes:
            self._need(op, self.lastw.get(k), False)
            rd = self.readers.get(k)
            if rd:
                for r in rd.values():
                    self._need(op, r, False)
        if dma is not None:
            st = self.stream(dma)
            st.count += 1
            op.dma = st
        agent = eng if dma is None else "dma:" + dma
        for k in reads:
            self.readers.setdefault(k, {})[agent] = op
        for k in writes:
            self.lastw[k] = op
            self.readers[k] = {}
        self.ops[eng].append(op)
        return op

    def _need(self, X, P_, raw):
        if P_ is None or P_ is X:
            return
        e = X.eng
        if P_.dma is not None:
            st = P_.dma
            val = 16 * st.count
            key = ("s", st.name)
            if self.seen[e].get(key, 0) >= val:
                return
            X.waits.append((st, val))
            self.seen[e][key] = val
            self.nwaits += 1
            return
        if P_.eng == e and not X.isdma:
            if e == "pe" or e == "sp":
                return
            if not raw:
                return
            if e != "pool" and X.idx - P_.idx >= 4:
                return
        if self.seen[e].get(P_.eng, -1) >= P_.idx:
            return
        P_.signal = True
        X.waits.append(P_)
        self.seen[e][P_.eng] = P_.idx
        self.nwaits += 1

    def barrier(self):
        lasts = [self.ops[e][-1] for e in self.ENGS if e != "sp" and self.ops[e]]
        hub = _Op()
        hub.eng = "sp"
        hub.fn = lambda eng: eng.nop()
        hub.idx = len(self.ops["sp"])
        hub.waits = []
        hub.signal = False
        hub.dma = None
        hub.sig = None
        hub.isdma = False
        for P_ in lasts:
            self._need(hub, P_, True)
        for st in self.streams.values():
            if st.count > 0:
                key = ("s", st.name)
                val = 16 * st.count
                if self.seen["sp"].get(key, 0) < val:
                    hub.waits.append((st, val))
                    self.seen["sp"][key] = val
        self.ops["sp"].append(hub)
        for e in self.ENGS:
            if e == "sp":
                continue
            op = _Op()
            op.eng = e
            op.fn = lambda eng: eng.nop()
            op.idx = len(self.ops[e])
            op.waits = []
            op.signal = False
            op.dma = None
            op.sig = None
            op.isdma = False
            self._need(op, hub, True)
            self.ops[e].append(op)

    def finalize(self, nc, stack, nsig=8):
        for st in self.streams.values():
            st.sem = stack.enter_context(nc.semaphore("d_" + st.name))
        self.sig_sems = {}
        for e in self.ENGS:
            sems = [stack.enter_context(nc.semaphore("g_%s_%d" % (e, i))) for i in range(nsig)]
            self.sig_sems[e] = sems
            c = 0
            for op in self.ops[e]:
                if op.signal and op.dma is None:
                    op.sig = (sems[c % nsig], c // nsig + 1)
                    c += 1

    def emit(self, e, eng, final_streams=()):
        for op in self.ops[e]:
            for w in op.waits:
                if isinstance(w, tuple):
                    eng.wait_ge(w[0].sem, w[1])
                else:
                    eng.wait_ge(w.sig[0], w.sig[1])
            ins = op.fn(eng)
            if op.dma is not None:
                ins.then_inc(op.dma.sem, 16)
            elif op.signal:
                ins.then_inc(op.sig[0], 1)
        for name in final_streams:
            st = self.streams[name]
            eng.wait_ge(st.sem, 16 * st.count)


class Ctx:
    pass


def _build(nph=12):
    nc = bass.Bass("TRN2", target_bir_lowering=False)
    stack = ExitStack()
    C = Ctx()
    C.nc = nc
    pr = Prog()
    C.pr = pr

    def din(name, shape):
        return nc.dram_tensor(name, list(shape), F32, kind="ExternalInput").ap()

    C.x = din("x", (S, D))
    shapes = {"ffn1_w_gate": (4, D, F), "ffn1_w_up": (4, D, F), "ffn1_w_down": (4, F, D),
              "ffn2_w_gate": (4, D, F), "ffn2_w_up": (4, D, F), "ffn2_w_down": (4, F, D),
              "conv_w_pw1": (2, D, 2 * D), "conv_w_pw2": (2, D, D), "w_kv": (D, 2 * D),
              "attn_w_q": (2, D, D), "attn_w_o": (2, D, D)}

    class LazyW(dict):
        def __missing__(self, nm):
            self[nm] = din(nm, shapes[nm])
            return self[nm]
    C.w = LazyW()
    C.vecs_d = din("vecs", (128, NV))
    C.ident_d = din("ident", (128, 128))
    C.cb_d = din("cbf", (128, 640))
    C.out = nc.dram_tensor("out", [S, D], F32, kind="ExternalOutput").ap()
    skind = "ExternalOutput" if nph < 0 else "Internal"
    C.kscr = nc.dram_tensor("kscr", [128, 16384], BF16, kind=skind).ap()
    C.vscr = nc.dram_tensor("vscr", [128, 16384], BF16, kind=skind).ap()

    C.dbg = None
    if nph == -3:
        C.dbg = (nc.dram_tensor("dbg_bf", [128, 16384 + 3072], BF16, kind="ExternalOutput").ap(),
                 nc.dram_tensor("dbg_f32", [128, 512], F32, kind="ExternalOutput").ap(),
                 nc.dram_tensor("dbg_kv", [128, 32768], BF16, kind="ExternalOutput").ap())
    C.H = stack.enter_context(nc.sbuf_tensor("H", [128, NDC, S], F32))
    C.Xr = stack.enter_context(nc.sbuf_tensor("Xr", [128, 16384], BF16))
    C.AW = stack.enter_context(nc.sbuf_tensor("AW", [128, AW_BYTES // 2], BF16))
    C.vecs = stack.enter_context(nc.sbuf_tensor("vecs_sb", [128, NV], F32))
    C.ident = stack.enter_context(nc.sbuf_tensor("ident_sb", [128, 128], F32))
    C.cb = stack.enter_context(nc.sbuf_tensor("cb_sb", [128, 640], BF16))
    C.PS = [stack.enter_context(nc.psum_tensor("ps%d" % i, [128, 512], F32)) for i in range(8)]

    def carve(base, off, shape, dtype):
        n = 1
        for s_ in shape:
            n *= s_
        nb = n * (4 if dtype == F32 else 2)
        assert off % 4 == 0
        v = base[:, off // 2:(off + nb) // 2]
        if dtype == F32:
            v = v.bitcast(F32)
        if len(shape) == 2:
            v = v.rearrange("p (a b) -> p a b", a=shape[0])
        return v

    C.aw = lambda off, shape, dtype: carve(C.AW, off, shape, dtype)
    C.xr = lambda off, shape, dtype: carve(C.Xr, off, shape, dtype)
    C.onesM = C.cb[:, 0:128]
    C.ones1 = C.cb[:, 128:256]
    C.tril = C.cb[:, 256:384]
    C.triu = C.cb[:, 384:512]
    C.identb = C.cb[:, 512:640]

    def vcol(c):
        return C.vecs[:, c:c + 1]
    C.vcol = vcol

    pr.add("sp", lambda e: e.dma_start(out=C.vecs[:], in_=C.vecs_d), writes=["vecs"], dma="c_vecs")
    pr.add("sp", lambda e: e.dma_start(out=C.ident[:], in_=C.ident_d), writes=["ident"], dma="c_ident")
    pr.add("pool", lambda e: e.dma_start(out=C.cb[:], in_=C.cb_d), writes=["cb"], dma="c_cb")

    phase_load(C)
    pr.barrier()
    n = 0
    if nph < 0:
        phase_kv(C)
        pr.barrier()
        if nph < -1:
            phase_attn(C, 0)
            pr.barrier()
    for layer in range(4):
        if n >= nph:
            break
        phase_ffn(C, layer, 1)
        pr.barrier()
        n += 1
        if n >= nph:
            break
        if layer < 2:
            phase_conv(C, layer)
        else:
            phase_attn(C, layer - 2)
        pr.barrier()
        n += 1
        if n >= nph:
            break
        phase_ffn(C, layer, 2)
        pr.barrier()
        n += 1
        if layer == 1 and n < nph:
            phase_kv(C)
            pr.barrier()
    phase_final(C)

    pr.finalize(nc, stack)
    with nc.Block() as block:
        @block.sync
        def _(e):
            pr.emit("sp", e, final_streams=("os0", "os1"))

        @block.tensor
        def _(e):
            pr.emit("pe", e)

        @block.scalar
        def _(e):
            pr.emit("act", e)

        @block.vector
        def _(e):
            pr.emit("dve", e)

        @block.gpsimd
        def _(e):
            pr.emit("pool", e)
    stack.close()
    nc._used_w = sorted(C.w.keys())
    return nc


def hk(dc, tg):
    return "h%d_%d" % (dc, tg)


def norm_tg(C, tg, gcol, out_ap, out_key, tmp, sq_eng="act", st_bank=6, out_engs=("dve",)):
    pr = C.pr
    cols = slice(tg * TG, (tg + 1) * TG)
    ps = C.PS[st_bank]
    psk = "ps%d" % st_bank
    for dc in range(NDC):
        sq, sqk = tmp["sq%d" % (dc % 2)]
        hin = C.H[:, dc, cols]
        if sq_eng == "act":
            pr.add("act", lambda e, sq=sq, hin=hin: e.activation(out=sq, in_=hin, func=AF.Square),
                   reads=[hk(dc, tg)], writes=sqk)
        else:
            pr.add(sq_eng, lambda e, sq=sq, hin=hin: e.tensor_tensor(out=sq, in0=hin, in1=hin, op=ALU.mult),
                   reads=[hk(dc, tg)], writes=sqk)
        pr.add("pe", lambda e, sq=sq, dc=dc: e.matmul(ps[:], lhsT=C.onesM, rhs=sq, start=(dc == 0), stop=(dc == NDC - 1)),
               reads=sqk + ["cb"], writes=[psk])
    lnt, lntk = tmp["lnt"]
    rs, rsk = tmp["rs"]
    pr.add("act", lambda e: e.activation(out=lnt, in_=ps[:], func=AF.Ln, bias=1e-6, scale=1.0),
           writes=[psk] + lntk)
    pr.add("act", lambda e: e.activation(out=rs, in_=lnt, func=AF.Exp, scale=-0.5),
           reads=lntk, writes=rsk)
    for dc in range(NDC):
        eng = out_engs[dc % len(out_engs)]
        o = out_ap(dc)
        hin = C.H[:, dc, cols]
        pr.add(eng, lambda e, o=o, hin=hin, dc=dc: e.scalar_tensor_tensor(
            out=o, in0=hin, scalar=C.vcol(gcol + dc), in1=rs, op0=ALU.mult, op1=ALU.mult),
            reads=[hk(dc, tg), "vecs"] + rsk, writes=out_key(dc))


def phase_load(C):
    pr = C.pr
    XS = [C.aw(0, (1024,), F32), C.aw(4096, (1024,), F32)]
    for tt in range(NTT):
        b = tt % 2
        xs = XS[b]
        pr.add("sp", lambda e, xs=xs, tt=tt: e.dma_start(out=xs, in_=C.x[tt * 128:(tt + 1) * 128, :]),
               writes=["xs%d" % b], dma="xs%d" % b)
        tg = tt // 4
        for half in range(2):
            bi = (2 * tt + half) % 4
            ps = C.PS[bi]
            for q in range(4):
                dc = half * 4 + q
                pr.add("pe", lambda e, ps=ps, q=q, xs=xs, dc=dc: e.transpose(
                    ps[:, q * 128:(q + 1) * 128], xs[:, dc * 128:(dc + 1) * 128], C.ident[:]),
                    reads=["xs%d" % b, "ident"], writes=["ps%d" % bi])
            dst = C.H[:, half * 4:(half + 1) * 4, tt * 128:(tt + 1) * 128]
            src = ps[:].rearrange("p (q t) -> p q t", q=4)
            pr.add("dve", lambda e, dst=dst, src=src: e.tensor_copy(out=dst, in_=src),
                   writes=["ps%d" % bi] + [hk(half * 4 + q, tg) for q in range(4)])


def phase_final(C):
    pr = C.pr
    YF = C.aw(0, (NDC, TG), F32)
    OS = [C.aw(32768, (1024,), F32), C.aw(36864, (1024,), F32)]
    tmp = {"sq0": (C.aw(88064, (TG,), BF16), ["sq0"]), "sq1": (C.aw(89088, (TG,), BF16), ["sq1"]),
           "lnt": (C.aw(90112, (TG,), F32), ["lnt"]), "rs": (C.aw(92160, (TG,), F32), ["rs"])}
    for tg in range(NTG):
        norm_tg(C, tg, V_FIN, lambda dc: YF[:, dc, :], lambda dc: ["yf%d" % dc], tmp)
        for tl in range(4):
            tt = tg * 4 + tl
            ob = tt % 2
            for half in range(2):
                bi = (2 * tt + half) % 4
                ps = C.PS[bi]
                for q in range(4):
                    dc = half * 4 + q
                    pr.add("pe", lambda e, ps=ps, q=q, dc=dc, tl=tl: e.transpose(
                        ps[:, q * 128:(q + 1) * 128], YF[:, dc, tl * 128:(tl + 1) * 128], C.ident[:]),
                        reads=["yf%d" % dc, "ident"], writes=["ps%d" % bi])
                dst = OS[ob][:, half * 512:(half + 1) * 512]
                eng = "dve" if half == 0 else "act"
                if eng == "dve":
                    pr.add("dve", lambda e, dst=dst, ps=ps: e.tensor_copy(out=dst, in_=ps[:]),
                           writes=["ps%d" % bi, "os%d_%d" % (ob, half)])
                else:
                    pr.add("act", lambda e, dst=dst, ps=ps: e.activation(out=dst, in_=ps[:], func=AF.Copy),
                           writes=["ps%d" % bi, "os%d_%d" % (ob, half)])
            pr.add("sp", lambda e, ob=ob, tt=tt: e.dma_start(out=C.out[tt * 128:(tt + 1) * 128, :], in_=OS[ob]),
                   reads=["os%d_0" % ob, "os%d_1" % ob], dma="os%d" % ob)


def phase_ffn(C, layer, which):
    pr = C.pr
    pre = "ffn%d_" % which
    wg = C.w[pre + "w_gate"][layer]
    wu = C.w[pre + "w_up"][layer]
    wd = C.w[pre + "w_down"][layer]
    gcol = (V_FFN1 if which == 1 else V_FFN2) + layer * 8
    X = C.Xr[:, :].rearrange("p (a b) -> p a b", a=NDC)
    SLOT = 36864
    WG = [C.aw(s * SLOT, (NDC, 768), BF16) for s in range(2)]
    WU = [C.aw(s * SLOT + 12288, (NDC, 768), BF16) for s in range(2)]
    WD = [C.aw(s * SLOT + 24576, (6, 1024), BF16) for s in range(2)]
    HID = [C.aw(73728, (6, TG), BF16), C.aw(79872, (6, TG), BF16)]
    SG = [C.aw(86016, (TG,), BF16), C.aw(87040, (TG,), BF16)]
    tmp = {"sq0": (C.aw(88064, (TG,), BF16), ["sq0"]), "sq1": (C.aw(89088, (TG,), BF16), ["sq1"]),
           "lnt": (C.aw(90112, (TG,), F32), ["lnt"]), "rs": (C.aw(92160, (TG,), F32), ["rs"])}
    wgv = wg.rearrange("(dc p) f -> p dc f", p=128)
    wuv = wu.rearrange("(dc p) f -> p dc f", p=128)

    def load_slice(s):
        f0, n = SLICES[s]
        slot = s % 2
        k = "slot%d" % slot
        pr.add("pool", lambda e: e.dma_start(out=WG[slot][:, :, 0:n * 128], in_=wgv[:, :, f0 * 128:(f0 + n) * 128]),
               writes=[k], dma="w%d" % slot)
        pr.add("pool", lambda e: e.dma_start(out=WU[slot][:, :, 0:n * 128], in_=wuv[:, :, f0 * 128:(f0 + n) * 128]),
               writes=[k], dma="w%d" % slot)
        pr.add("pool", lambda e: e.dma_start(
            out=WD[slot][:, 0:n, :], in_=wd[f0 * 128:(f0 + n) * 128, :].rearrange("(fc p) d -> p fc d", p=128)),
            writes=[k], dma="w%d" % slot)

    load_slice(0)
    load_slice(1)
    for tg in range(NTG):
        cols = slice(tg * TG, (tg + 1) * TG)
        norm_tg(C, tg, gcol, lambda dc: X[:, dc, cols], lambda dc: ["x%d_%d" % (dc, tg)], tmp)

    units = [(s, tg) for s in range(len(SLICES)) for tg in range(NTG)]
    cnt = [0, 0]

    def gu(ui):
        s, tg = units[ui]
        f0, n = SLICES[s]
        slot = s % 2
        hid = HID[ui % 2]
        cols = slice(tg * TG, (tg + 1) * TG)
        for j in range(n):
            b = cnt[0] % 2
            cnt[0] += 1
            pg = C.PS[b]
            pu = C.PS[2 + b]
            for dc in range(NDC):
                pr.add("pe", lambda e, pg=pg, j=j, dc=dc: e.matmul(
                    pg[:], lhsT=WG[slot][:, dc, j * 128:(j + 1) * 128], rhs=X[:, dc, cols],
                    start=(dc == 0), stop=(dc == NDC - 1)),
                    reads=["slot%d" % slot, "x%d_%d" % (dc, tg)], writes=["ps%d" % b])
            for dc in range(NDC):
                pr.add("pe", lambda e, pu=pu, j=j, dc=dc: e.matmul(
                    pu[:], lhsT=WU[slot][:, dc, j * 128:(j + 1) * 128], rhs=X[:, dc, cols],
                    start=(dc == 0), stop=(dc == NDC - 1)),
                    reads=["slot%d" % slot, "x%d_%d" % (dc, tg)], writes=["ps%d" % (2 + b)])
            sg = SG[b]
            pr.add("act", lambda e, sg=sg, pg=pg: e.activation(out=sg, in_=pg[:], func=AF.Silu),
                   writes=["ps%d" % b, "sg%d" % b])
            pr.add("dve", lambda e, hid=hid, j=j, pu=pu, sg=sg: e.tensor_tensor(
                out=hid[:, j, :], in0=pu[:], in1=sg, op=ALU.mult),
                reads=["sg%d" % b], writes=["ps%d" % (2 + b), "hid%d_%d" % (ui % 2, j)])

    def down(ui):
        s, tg = units[ui]
        f0, n = SLICES[s]
        slot = s % 2
        hid = HID[ui % 2]
        cols = slice(tg * TG, (tg + 1) * TG)
        for dc in range(NDC):
            b = 4 + cnt[1] % 2
            cnt[1] += 1
            pd = C.PS[b]
            for j in range(n):
                pr.add("pe", lambda e, pd=pd, j=j, dc=dc: e.matmul(
                    pd[:], lhsT=WD[slot][:, j, dc * 128:(dc + 1) * 128], rhs=hid[:, j, :],
                    start=(j == 0), stop=(j == n - 1)),
                    reads=["slot%d" % slot, "hid%d_%d" % (ui % 2, j)], writes=["ps%d" % b])
            hh = C.H[:, dc, cols]
            pr.add("dve", lambda e, pd=pd, hh=hh: e.scalar_tensor_tensor(
                out=hh, in0=pd[:], scalar=0.5, in1=hh, op0=ALU.mult, op1=ALU.add),
                reads=[hk(dc, tg)], writes=["ps%d" % b, hk(dc, tg)])

    gu(0)
    for ui in range(len(units)):
        if ui + 1 < len(units):
            gu(ui + 1)
        down(ui)
        s, tg = units[ui]
        if tg == NTG - 1 and s + 2 < len(SLICES):
            load_slice(s + 2)


def phase_conv(C, i):
    pr = C.pr
    w1 = C.w["conv_w_pw1"][i].rearrange("(dc p) f -> p dc f", p=128)
    w2 = C.w["conv_w_pw2"][i].rearrange("(dc p) f -> p dc f", p=128)
    vb = V_CONV + i * 296
    c_b1a, c_b1g, c_bdw, c_lng, c_lnb, c_b2, c_wdw = vb, vb + 8, vb + 16, vb + 24, vb + 32, vb + 40, vb + 48
    gcol = V_MIX + i * 8
    W1 = C.aw(0, (NDC, 2048), BF16)
    W2 = C.aw(32768, (NDC, 1024), BF16)
    G = C.aw(49152, (NDC, 542), BF16)
    Y = C.aw(57856, (NDC, TG), F32)
    YSQ = C.aw(74240, (NDC, TG), BF16)
    SIG = [C.aw(82432, (TG,), F32), C.aw(82432, (TG,), F32)]
    LNT = C.aw(84480, (TG,), F32)
    RS = C.aw(86528, (TG,), F32)
    MEAN = C.aw(88576, (TG,), F32)
    MSQ = C.aw(90624, (TG,), F32)
    NMR = C.aw(92672, (TG,), F32)
    DG = [C.aw(94720, (16, 128), BF16), C.aw(98816, (16, 128), BF16), C.aw(102912, (16, 128), BF16),
          C.xr(26624, (16, 128), BF16)]
    UT = C.xr(0, (NDC, TG), BF16)
    SS = C.xr(8192, (NDC, TG), BF16)
    YB = C.xr(16384, (NDC, TG), BF16)
    tmp = {"sq0": (C.xr(24576, (TG,), BF16), ["sq0"]), "sq1": (C.xr(25600, (TG,), BF16), ["sq1"]),
           "lnt": (LNT, ["lnt"]), "rs": (RS, ["rs"])}
    for hf in range(4):
        pr.add("pool", lambda e, hf=hf: e.dma_start(out=W1[:, 2 * hf:2 * hf + 2, :], in_=w1[:, 2 * hf:2 * hf + 2, :]),
               writes=["w1"], dma="cw1")
    for hf in range(2):
        pr.add("pool", lambda e, hf=hf: e.dma_start(out=W2[:, 4 * hf:4 * hf + 4, :], in_=w2[:, 4 * hf:4 * hf + 4, :]),
               writes=["w2"], dma="cw2")
    pr.add("dve", lambda e: e.memset(G[:, :, 0:30], 0.0), writes=["g%d" % c for c in range(NDC)])
    identb3 = C.identb.unsqueeze(1)
    wdw3 = C.vecs[:, c_wdw:c_wdw + 8 * CW].rearrange("p (c k) -> p c k", c=NDC)
    dgcnt = 0
    ccnt = 0
    for tg in range(NTG):
        cols = slice(tg * TG, (tg + 1) * TG)
        norm_tg(C, tg, gcol, lambda dc: UT[:, dc, :], lambda dc: ["ut%d" % dc], tmp, st_bank=6)
        def pw1_glu(c):
            b = c % 2
            pa = C.PS[b]
            pg = C.PS[2 + b]
            for dc in range(NDC):
                pr.add("pe", lambda e, pa=pa, c=c, dc=dc: e.matmul(
                    pa[:], lhsT=W1[:, dc, c * 128:(c + 1) * 128], rhs=UT[:, dc, :],
                    start=(dc == 0), stop=(dc == NDC - 1)),
                    reads=["w1", "ut%d" % dc], writes=["ps%d" % b])
            for dc in range(NDC):
                pr.add("pe", lambda e, pg=pg, c=c, dc=dc: e.matmul(
                    pg[:], lhsT=W1[:, dc, 1024 + c * 128:1024 + (c + 1) * 128], rhs=UT[:, dc, :],
                    start=(dc == 0), stop=(dc == NDC - 1)),
                    reads=["w1", "ut%d" % dc], writes=["ps%d" % (2 + b)])
            sig = SIG[b]
            pr.add("act", lambda e, sig=sig, pg=pg, c=c: e.activation(
                out=sig, in_=pg[:], func=AF.Sigmoid, bias=C.vcol(c_b1g + c), scale=1.0),
                reads=["vecs"], writes=["ps%d" % (2 + b), "sig"])
            pr.add("dve", lambda e, pa=pa, sig=sig, c=c: e.scalar_tensor_tensor(
                out=G[:, c, 30:542], in0=pa[:], scalar=C.vcol(c_b1a + c), in1=sig, op0=ALU.add, op1=ALU.mult),
                reads=["sig", "vecs"], writes=["ps%d" % b, "g%d" % c])

        def diag_build(c):
            for half in range(2):
                k0 = 16 * half
                nk = 16 if half == 0 else CW - 16
                dg = DG[2 * (c % 2) + half]
                dgk = "dg%d" % (2 * (c % 2) + half)
                in0 = identb3.broadcast_to([128, nk, 128])
                in1 = wdw3[:, c, k0:k0 + nk].unsqueeze(2).broadcast_to([128, nk, 128])
                pr.add("dve", lambda e, dg=dg, nk=nk, in0=in0, in1=in1: e.tensor_tensor(
                    out=dg[:, 0:nk, :], in0=in0, in1=in1, op=ALU.mult),
                    reads=["cb", "vecs"], writes=[dgk])

        def taps(c, ccnt, dgcnt):
            pc = C.PS[4 + ccnt % 2]
            pck = "ps%d" % (4 + ccnt % 2)
            for half in range(2):
                k0 = 16 * half
                nk = 16 if half == 0 else CW - 16
                dg = DG[2 * (c % 2) + half]
                dgk = "dg%d" % (2 * (c % 2) + half)
                for j in range(nk):
                    k = k0 + j
                    pr.add("pe", lambda e, pc=pc, dg=dg, j=j, k=k, c=c: e.matmul(
                        pc[:], lhsT=dg[:, j, :], rhs=G[:, c, k:k + TG], start=(k == 0), stop=(k == CW - 1)),
                        reads=[dgk, "g%d" % c], writes=[pck])
            pr.add("act", lambda e, pc=pc, c=c: e.activation(
                out=Y[:, c, :], in_=pc[:], func=AF.Identity, bias=C.vcol(c_bdw + c), scale=1.0),
                reads=["vecs"], writes=[pck, "y%d" % c])
            pr.add("act", lambda e, pc=pc, c=c: e.activation(
                out=YB[:, c, :], in_=pc[:], func=AF.Identity, bias=C.vcol(c_bdw + c), scale=1.0),
                reads=["vecs"], writes=[pck, "yb%d" % c])
            pr.add("act", lambda e, pc=pc, c=c: e.activation(
                out=YSQ[:, c, :], in_=pc[:], func=AF.Square, bias=C.vcol(c_bdw + c), scale=1.0),
                reads=["vecs"], writes=[pck, "ysq%d" % c])
            pr.add("dve", lambda e, c=c: e.tensor_copy(out=G[:, c, 0:30], in_=G[:, c, 512:542]),
                   reads=["g%d" % c], writes=["g%d" % c])

        for c in range(NDC + 1):
            if c < NDC:
                diag_build(c)
                pw1_glu(c)
            if c >= 1:
                taps(c - 1, ccnt, dgcnt)
                ccnt += 1
                dgcnt += 2
        for c in range(NDC):
            pr.add("pe", lambda e, c=c: e.matmul(C.PS[6][:], lhsT=C.onesM, rhs=YB[:, c, :],
                                                 start=(c == 0), stop=(c == NDC - 1)),
                   reads=["yb%d" % c, "cb"], writes=["ps6"])
        for c in range(NDC):
            pr.add("pe", lambda e, c=c: e.matmul(C.PS[7][:], lhsT=C.onesM, rhs=YSQ[:, c, :],
                                                 start=(c == 0), stop=(c == NDC - 1)),
                   reads=["ysq%d" % c, "cb"], writes=["ps7"])
        pr.add("dve", lambda e: e.tensor_copy(out=MEAN, in_=C.PS[6][:]), writes=["ps6", "mean"])
        pr.add("dve", lambda e: e.tensor_tensor(out=MSQ, in0=MEAN, in1=MEAN, op=ALU.mult),
               reads=["mean"], writes=["msq"])
        pr.add("dve", lambda e: e.tensor_tensor(out=MSQ, in0=C.PS[7][:], in1=MSQ, op=ALU.subtract),
               reads=["msq"], writes=["ps7", "msq"])
        pr.add("act", lambda e: e.activation(out=LNT, in_=MSQ, func=AF.Ln, bias=1e-5, scale=1.0),
               reads=["msq"], writes=["lnt"])
        pr.add("act", lambda e: e.activation(out=RS, in_=LNT, func=AF.Exp, scale=-0.5),
               reads=["lnt"], writes=["rs"])
        pr.add("dve", lambda e: e.scalar_tensor_tensor(out=NMR, in0=MEAN, scalar=-1.0, in1=RS, op0=ALU.mult, op1=ALU.mult),
               reads=["mean", "rs"], writes=["nmr"])
        for c in range(NDC):
            pr.add("dve", lambda e, c=c: e.tensor_tensor(out=Y[:, c, :], in0=Y[:, c, :], in1=RS, op=ALU.mult),
                   reads=["y%d" % c, "rs"], writes=["y%d" % c])
        for c in range(NDC):
            pr.add("dve", lambda e, c=c: e.tensor_tensor(out=Y[:, c, :], in0=Y[:, c, :], in1=NMR, op=ALU.add),
                   reads=["y%d" % c, "nmr"], writes=["y%d" % c])
        for c in range(NDC):
            pr.add("act", lambda e, c=c: e.activation(
                out=SS[:, c, :], in_=Y[:, c, :], func=AF.Silu, bias=C.vcol(c_lnb + c), scale=C.vcol(c_lng + c)),
                reads=["y%d" % c, "vecs"], writes=["s%d" % c])
        for dc in range(NDC):
            b = dc % 2
            pd = C.PS[b]
            for c in range(NDC):
                pr.add("pe", lambda e, pd=pd, c=c, dc=dc: e.matmul(
                    pd[:], lhsT=W2[:, c, dc * 128:(dc + 1) * 128], rhs=SS[:, c, :],
                    start=(c == 0), stop=(c == NDC - 1)),
                    reads=["w2", "s%d" % c], writes=["ps%d" % b])
            hh = C.H[:, dc, cols]
            pr.add("dve", lambda e, pd=pd, hh=hh, dc=dc: e.scalar_tensor_tensor(
                out=hh, in0=pd[:], scalar=C.vcol(c_b2 + dc), in1=hh, op0=ALU.add, op1=ALU.add),
                reads=[hk(dc, tg), "vecs"], writes=["ps%d" % b, hk(dc, tg)])


def phase_kv(C):
    pr = C.pr
    wkv = C.w["w_kv"].rearrange("(dc p) f -> p dc f", p=128)
    KT = C.aw(0, (NDC, S), BF16)
    V = C.aw(32768, (NTT, D), BF16)
    WKV = C.aw(65536, (NDC, 2048), BF16)
    tmp = {"lnt": (C.aw(98304, (TG,), F32), ["lnt"]), "rs": (C.aw(100352, (TG,), F32), ["rs"]),
           "sq0": (C.aw(102400, (TG,), BF16), ["sq0"]), "sq1": (C.aw(103424, (TG,), BF16), ["sq1"])}
    UT = C.xr(0, (NDC, TG), BF16)
    for hf in range(4):
        pr.add("pool", lambda e, hf=hf: e.dma_start(out=WKV[:, 2 * hf:2 * hf + 2, :], in_=wkv[:, 2 * hf:2 * hf + 2, :]),
               writes=["wkv"], dma="wkv")
    cnt = 0
    for tg in range(NTG):
        cols = slice(tg * TG, (tg + 1) * TG)
        norm_tg(C, tg, V_KV, lambda dc: UT[:, dc, :], lambda dc: ["ut%d" % dc], tmp)
        for c in range(NDC):
            b = cnt % 4
            cnt += 1
            ps = C.PS[b]
            for dc in range(NDC):
                pr.add("pe", lambda e, ps=ps, c=c, dc=dc: e.matmul(
                    ps[:], lhsT=WKV[:, dc, c * 128:(c + 1) * 128], rhs=UT[:, dc, :],
                    start=(dc == 0), stop=(dc == NDC - 1)),
                    reads=["wkv", "ut%d" % dc], writes=["ps%d" % b])
            kdst = KT[:, c, cols]
            pr.add("act", lambda e, ps=ps, kdst=kdst: e.activation(out=kdst, in_=ps[:], func=AF.Copy),
                   writes=["ps%d" % b, "K"])
        for tl in range(4):
            tt = tg * 4 + tl
            for half in range(2):
                b = cnt % 4
                cnt += 1
                ps = C.PS[b]
                for dc in range(NDC):
                    pr.add("pe", lambda e, ps=ps, dc=dc, tl=tl, half=half: e.matmul(
                        ps[:], lhsT=UT[:, dc, tl * 128:(tl + 1) * 128],
                        rhs=WKV[:, dc, 1024 + half * 512:1024 + (half + 1) * 512],
                        start=(dc == 0), stop=(dc == NDC - 1)),
                        reads=["wkv", "ut%d" % dc], writes=["ps%d" % b])
                pr.add("dve", lambda e, ps=ps, tt=tt, half=half: e.tensor_copy(
                    out=V[:, tt, half * 512:(half + 1) * 512], in_=ps[:]),
                    writes=["ps%d" % b, "V"])
    KTf = C.AW[:, 0:16384]
    Vf = C.AW[:, 16384:32768]
    for q in range(4):
        pr.add("sp", lambda e, q=q: e.dma_start(out=C.kscr[:, q * 4096:(q + 1) * 4096], in_=KTf[:, q * 4096:(q + 1) * 4096]),
               reads=["K"], dma="kvs")
    for q in range(4):
        pr.add("sp", lambda e, q=q: e.dma_start(out=C.vscr[:, q * 4096:(q + 1) * 4096], in_=Vf[:, q * 4096:(q + 1) * 4096]),
               reads=["V"], dma="kvs")


def phase_attn(C, i):
    pr = C.pr
    wq = C.w["attn_w_q"][i].rearrange("(dc p) f -> p dc f", p=128)
    wo = C.w["attn_w_o"][i].rearrange("(dc p) f -> p dc f", p=128)
    gcol = V_MIX + (2 + i) * 8
    KT = C.aw(0, (NDC, S), BF16)
    V = C.aw(32768, (NTT, D), BF16)
    WQ = C.aw(65536, (NDC, 1024), BF16)
    WO = C.aw(81920, (NDC, 1024), BF16)
    Eb = [C.aw(98304, (TG,), F32), C.aw(100352, (TG,), F32), C.aw(102400, (TG,), F32)]
    Lb = [C.xr(4096, (TG,), BF16), C.xr(5120, (TG,), BF16)]
    lk = [["ut4"], ["ut5"]]
    Wb = [C.aw(104448, (TG,), BF16), C.aw(105472, (TG,), BF16), C.aw(106496, (TG,), BF16)]
    SSt = C.xr(6144, (TG,), BF16)
    ssk = "ut6"
    UT = C.xr(0, (NDC, TG), BF16)
    QE = C.xr(8192, (NDC, TG), BF16)
    QO = C.xr(16384, (NDC, TG), BF16)
    OT = C.xr(24576, (NDC, TG), BF16)
    XRb = [C.xr(0, (TG,), F32), C.xr(2048, (TG,), F32)]
    xrk = [["ut0", "ut1"], ["ut2", "ut3"]]
    tmp = {"sq0": (C.xr(16384, (TG,), BF16), ["qo0"]), "sq1": (C.xr(17408, (TG,), BF16), ["qo1"]),
           "lnt": (C.xr(18432, (TG,), F32), ["qo2", "qo3"]), "rs": (C.xr(20480, (TG,), F32), ["qo4", "qo5"])}
    KTf = C.AW[:, 0:16384]
    Vf = C.AW[:, 16384:32768]
    for q in range(4):
        pr.add("sp", lambda e, q=q: e.dma_start(out=KTf[:, q * 4096:(q + 1) * 4096], in_=C.kscr[:, q * 4096:(q + 1) * 4096]),
               writes=["K"], dma="kld")
    for q in range(4):
        pr.add("sp", lambda e, q=q: e.dma_start(out=Vf[:, q * 4096:(q + 1) * 4096], in_=C.vscr[:, q * 4096:(q + 1) * 4096]),
               writes=["V"], dma="vld")
    for hf in range(2):
        pr.add("pool", lambda e, hf=hf: e.dma_start(out=WQ[:, 4 * hf:4 * hf + 4, :], in_=wq[:, 4 * hf:4 * hf + 4, :]),
               writes=["wq"], dma="wq")
    for hf in range(2):
        pr.add("pool", lambda e, hf=hf: e.dma_start(out=WO[:, 4 * hf:4 * hf + 4, :], in_=wo[:, 4 * hf:4 * hf + 4, :]),
               writes=["wo"], dma="wo")

    qcnt = 0
    for tg in range(NTG):
        cols = slice(tg * TG, (tg + 1) * TG)
        norm_tg(C, tg, gcol, lambda dc: UT[:, dc, :], lambda dc: ["ut%d" % dc], tmp, sq_eng="dve", st_bank=3)
        pr.add(AENG, lambda e: e.memset(QE[64:128, :, :], 0.0), writes=["qe%d" % c for c in range(NDC)])
        pr.add(AENG, lambda e: e.memset(QO[0:64, :, :], 0.0), writes=["qo%d" % c for c in range(NDC)])
        for c in range(NDC):
            b = qcnt % 3
            qcnt += 1
            ps = C.PS[b]
            for dc in range(NDC):
                pr.add("pe", lambda e, ps=ps, c=c, dc=dc: e.matmul(
                    ps[:], lhsT=WQ[:, dc, c * 128:(c + 1) * 128], rhs=UT[:, dc, :],
                    start=(dc == 0), stop=(dc == NDC - 1)),
                    reads=["wq", "ut%d" % dc], writes=["ps%d" % b])
            pr.add("dve", lambda e, ps=ps, c=c: e.tensor_scalar(
                out=QE[0:64, c, :], in0=ps[0:64, :], scalar1=0.125, scalar2=None, op0=ALU.mult),
                writes=["ps%d" % b, "qe%d" % c])
            pr.add("dve", lambda e, ps=ps, c=c: e.tensor_scalar(
                out=QO[64:128, c, :], in0=ps[64:128, :], scalar1=0.125, scalar2=None, op0=ALU.mult),
                writes=["ps%d" % b, "qo%d" % c])
        kmax = 4 * tg + 3
        items = []
        for c in range(NDC):
            for j2 in range(2):
                for kb in range(kmax, -1, -1):
                    items.append((c, j2, kb))
        n = len(items)

        def info(it):
            c, j2, kb = items[it]
            c0 = (kb - 4 * tg) * 128 if kb >= 4 * tg else 0
            return c, j2, kb, c0

        def z_pe(it):
            c, j2, kb, c0 = info(it)
            b = it % 2
            pz = C.PS[b]
            Q = QE if j2 == 0 else QO
            qk = ("qe%d" if j2 == 0 else "qo%d") % c
            pr.add("pe", lambda e: e.matmul(pz[:, c0:TG], lhsT=KT[:, c, kb * 128:(kb + 1) * 128], rhs=Q[:, c, c0:TG],
                                            start=True, stop=True),
                   reads=["K", qk], writes=["ps%d" % b])

        def exp_act(it):
            c, j2, kb, c0 = info(it)
            b = it % 2
            pz = C.PS[b]
            E = Eb[it % 3]
            pr.add("act", lambda e: e.activation(out=E[:, c0:TG], in_=pz[:, c0:TG], func=AF.Exp),
                   writes=["ps%d" % b, "E%d" % (it % 3)])

        def ln_act(it):
            c, j2, kb, c0 = info(it)
            L = Lb[it % 2]
            E = Eb[it % 3]
            pr.add("act", lambda e: e.activation(out=L[:, c0:TG], in_=E[:, c0:TG], func=AF.Ln, bias=1.0, scale=1.0),
                   reads=["E%d" % (it % 3)], writes=lk[it % 2])

        def maskl_dve(it):
            c, j2, kb, c0 = info(it)
            L = Lb[it % 2]
            if kb >= 4 * tg:
                pr.add("dve", lambda e: e.tensor_tensor(out=L[:, c0:c0 + 128], in0=L[:, c0:c0 + 128], in1=C.triu, op=ALU.mult),
                       reads=lk[it % 2] + ["cb"], writes=lk[it % 2])

        def r_pe(it):
            c, j2, kb, c0 = info(it)
            b = 2 + it % 2
            pb = C.PS[b]
            L = Lb[it % 2]
            first = (kb == 4 * tg + 3)
            pr.add("pe", lambda e: e.matmul(pb[:, c0:TG], lhsT=C.tril, rhs=L[:, c0:TG], start=True, stop=first),
                   reads=lk[it % 2] + ["cb"], writes=["ps%d" % b])
            if not first:
                pr.add("pe", lambda e: e.matmul(pb[:, c0:TG], lhsT=C.ones1, rhs=SSt[:, c0:TG], start=False, stop=True),
                       reads=[ssk, "cb"], writes=["ps%d" % b])

        def ssum_dve(it):
            c, j2, kb, c0 = info(it)
            L = Lb[it % 2]
            first = (kb == 4 * tg + 3)
            if first:
                pr.add("dve", lambda e: e.tensor_copy(out=SSt[:, c0:TG], in_=L[:, c0:TG]), reads=lk[it % 2], writes=[ssk])
                if c0 > 0:
                    pr.add("dve", lambda e: e.memset(SSt[:, 0:c0], 0.0), writes=[ssk])
            elif kb > 0:
                pr.add("dve", lambda e: e.tensor_tensor(out=SSt[:, c0:TG], in0=SSt[:, c0:TG], in1=L[:, c0:TG], op=ALU.add),
                       reads=[ssk] + lk[it % 2], writes=[ssk])

        def xr_act(it):
            c, j2, kb, c0 = info(it)
            b = 2 + it % 2
            pb = C.PS[b]
            XR = XRb[it % 2]
            pr.add("act", lambda e: e.activation(out=XR[:, c0:TG], in_=pb[:, c0:TG], func=AF.Exp, scale=-1.0),
                   writes=["ps%d" % b] + xrk[it % 2])

        def w_dve(it):
            c, j2, kb, c0 = info(it)
            E = Eb[it % 3]
            XR = XRb[it % 2]
            Wt = Wb[it % 3]
            wk = "wt%d" % (it % 3)
            pr.add("dve", lambda e: e.tensor_tensor(out=Wt[:, c0:TG], in0=E[:, c0:TG], in1=XR[:, c0:TG], op=ALU.mult),
                   reads=["E%d" % (it % 3)] + xrk[it % 2], writes=[wk])
            if kb >= 4 * tg:
                pr.add("dve", lambda e: e.tensor_tensor(out=Wt[:, c0:c0 + 128], in0=Wt[:, c0:c0 + 128], in1=C.triu, op=ALU.mult),
                       reads=[wk, "cb"], writes=[wk])
                if c0 > 0:
                    pr.add("dve", lambda e: e.memset(Wt[:, 0:c0], 0.0), writes=[wk])

        def pv_pe(it, tg_=tg):
            c, j2, kb, c0 = info(it)
            hp = slice(j2 * 64, (j2 + 1) * 64)
            b = 4 + 2 * (c % 2) + j2
            po = C.PS[b]
            Wt = Wb[it % 3]
            pv_first = (kb == 4 * tg_ + 3)
            pr.add("pe", lambda e: e.matmul(po[:], lhsT=V[:, kb, c * 128:(c + 1) * 128], rhs=Wt[:, :],
                                            start=pv_first, stop=(kb == 0)),
                   reads=["V", "wt%d" % (it % 3)], writes=["ps%d" % b])
            if kb == 0:
                pr.add("dve", lambda e: e.tensor_copy(out=OT[hp, c, :], in_=po[hp, :]),
                       writes=["ps%d" % b, "ot%d" % c])

        for t in range(n + 3):
            if t < n:
                for _j in range(NJUNK + 1):
                    z_pe(t)
                exp_act(t)
                ln_act(t)
                maskl_dve(t)
            if 0 <= t - 1 < n:
                r_pe(t - 1)
                ssum_dve(t - 1)
                xr_act(t - 1)
                w_dve(t - 1)
            if 0 <= t - 3 < n:
                pv_pe(t - 3)
        if C.dbg is not None and tg == 0:
            pr.barrier()
            pr.add("sp", lambda e: e.dma_start(out=C.dbg[0][:, 0:16384], in_=C.Xr[:, 0:16384]), dma="dbg")
            pr.add("sp", lambda e: e.dma_start(out=C.dbg[0][:, 16384:16384 + 3072], in_=C.AW[:, 100352 // 2:106496 // 2]),
                   dma="dbg")
            pr.add("sp", lambda e: e.dma_start(out=C.dbg[1][:, :], in_=Eb[0]), dma="dbg")
            pr.add("sp", lambda e: e.dma_start(out=C.dbg[2][:, :], in_=C.AW[:, 0:32768]), dma="dbg")
            pr.barrier()
        for dc in range(NDC):
            b = qcnt % 3
            qcnt += 1
            ps = C.PS[b]
            for c in range(NDC):
                pr.add("pe", lambda e, ps=ps, c=c, dc=dc: e.matmul(
                    ps[:], lhsT=WO[:, c, dc * 128:(dc + 1) * 128], rhs=OT[:, c, :],
                    start=(c == 0), stop=(c == NDC - 1)),
                    reads=["wo", "ot%d" % c], writes=["ps%d" % b])
            hh = C.H[:, dc, cols]
            pr.add("dve", lambda e, ps=ps, hh=hh: e.tensor_tensor(out=hh, in0=ps[:], in1=hh, op=ALU.add),
                   reads=[hk(dc, tg)], writes=["ps%d" % b, hk(dc, tg)])


def _fm(v):
    return np.ascontiguousarray(np.asarray(v, dtype=np.float32).reshape(8, 128).T)


def _host_tables(inp):
    vecs = np.zeros((128, NV), dtype=np.float32)
    for l in range(4):
        vecs[:, V_FFN1 + l * 8:V_FFN1 + l * 8 + 8] = _fm(inp["ffn1_norm"][l])
        vecs[:, V_MIX + l * 8:V_MIX + l * 8 + 8] = _fm(inp["mix_norm"][l])
        vecs[:, V_FFN2 + l * 8:V_FFN2 + l * 8 + 8] = _fm(inp["ffn2_norm"][l])
    vecs[:, V_KV:V_KV + 8] = _fm(inp["kv_norm"])
    vecs[:, V_FIN:V_FIN + 8] = _fm(inp["final_norm"])
    for i in range(2):
        vb = V_CONV + i * 296
        b1 = np.asarray(inp["conv_b_pw1"][i], dtype=np.float32)
        vecs[:, vb:vb + 8] = _fm(b1[:D])
        vecs[:, vb + 8:vb + 16] = _fm(b1[D:])
        vecs[:, vb + 16:vb + 24] = _fm(inp["conv_b_dw"][i])
        vecs[:, vb + 24:vb + 32] = _fm(inp["conv_ln_g"][i])
        vecs[:, vb + 32:vb + 40] = _fm(inp["conv_ln_b"][i])
        vecs[:, vb + 40:vb + 48] = _fm(inp["conv_b_pw2"][i])
        wdw = np.asarray(inp["conv_w_dw"][i], dtype=np.float32)
        vecs[:, vb + 48:vb + 296] = wdw.T.reshape(8, 128, CW).transpose(1, 0, 2).reshape(128, 8 * CW)
    ident = np.eye(128, dtype=np.float32)
    cb = np.zeros((128, 640), dtype=np.float32)
    cb[:, 0:128] = 1.0 / 1024.0
    cb[:, 128:256] = 1.0
    r = np.arange(128)
    cb[:, 256:384] = (r[:, None] >= r[None, :]).astype(np.float32)
    cb[:, 384:512] = (r[None, :] > r[:, None]).astype(np.float32)
    cb[:, 512:640] = np.eye(128, dtype=np.float32)
    return vecs, ident, cb


_NC_CACHE = {}
NPH = 12


def kernel(**inputs):
    inp = {k: np.asarray(v) for k, v in inputs.items()}
    vecs, ident, cb = _host_tables(inp)
    if NPH not in _NC_CACHE:
        _NC_CACHE[NPH] = _build(NPH)
    nc = _NC_CACHE[NPH]
    shared = {"vecs": vecs, "ident": ident, "cbf": cb}
    for nm in nc._used_w:
        shared[nm] = np.ascontiguousarray(inp[nm], dtype=np.float32)
    x = np.ascontiguousarray(inp["x"], dtype=np.float32)
    in_maps = []
    for b in range(NCORES):
        m = dict(shared)
        m["x"] = x[b]
        in_maps.append(m)
    res = run_bass_kernel_spmd(nc, in_maps, core_ids=list(range(NCORES)), **({"trace": True} if TRACE else {}))
    if TRACE:
        print("EXEC_TIME_NS", res.exec_time_ns)
    out = np.stack([np.asarray(res.results[b]["out"], dtype=np.float32) for b in range(NCORES)], axis=0)
    if NPH < 0:
        kernel.dbg = res.results
    return out
```

```python
import numpy as np
from contextlib import ExitStack

import concourse.bass as bass
import concourse.mybir as mybir
from concourse.bass_utils import run_bass_kernel_spmd

F32 = mybir.dt.float32
BF16 = mybir.dt.bfloat16
AF = mybir.ActivationFunctionType
ALU = mybir.AluOpType

D = 1024
S = 2048
F = 2816
NDC = 8
NFC = 22
TG = 512
NTG = 4
NTT = 16
CW = 31
NCORES = 8
SLICES = [(0, 2), (2, 5), (7, 5), (12, 5), (17, 5)]
AW_BYTES = 108032
POOL_CONV_CHUNKS = ()
SEQ_ATTN = False
TRACE = False
NJUNK = 0
AENG = "dve"

V_FFN1 = 0
V_MIX = 32
V_FFN2 = 64
V_KV = 96
V_FIN = 104
V_CONV = 112
NV = 112 + 2 * 296


class _Op:
    __slots__ = ("eng", "fn", "idx", "waits", "signal", "dma", "sig", "isdma")


class _Stream:
    def __init__(self, name):
        self.name = name
        self.count = 0
        self.sem = None


class Prog:
    ENGS = ("pe", "act", "dve", "pool", "sp")

    def __init__(self):
        self.ops = {e: [] for e in self.ENGS}
        self.lastw = {}
        self.readers = {}
        self.seen = {e: {} for e in self.ENGS}
        self.streams = {}
        self.nwaits = 0

    def stream(self, name):
        st = self.streams.get(name)
        if st is None:
            st = _Stream(name)
            self.streams[name] = st
        return st

    def add(self, eng, fn, reads=(), writes=(), dma=None):
        op = _Op()
        op.eng = eng
        op.fn = fn
        op.idx = len(self.ops[eng])
        op.waits = []
        op.signal = False
        op.dma = None
        op.sig = None
        op.isdma = dma is not None
        for k in reads:
            self._need(op, self.lastw.get(k), True)
        for k in writes:
            self._need(op, self.lastw.get(k), False)
            rd = self.readers.get(k)
            if rd:
                for r in rd.values():
                    self._need(op, r, False)
        if dma is not None:
            st = self.stream(dma)
            st.count += 1
            op.dma = st
        agent = eng if dma is None else "dma:" + dma
        for k in reads:
            self.readers.setdefault(k, {})[agent] = op
        for k in writes:
            self.lastw[k] = op
            self.readers[k] = {}
        self.ops[eng].append(op)
        return op

    def _need(self, X, P_, raw):
        if P_ is None or P_ is X:
            return
        e = X.eng
        if P_.dma is not None:
            st = P_.dma
            val = 16 * st.count
            key = ("s", st.name)
            if self.seen[e].get(key, 0) >= val:
                return
            X.waits.append((st, val))
            self.seen[e][key] = val
            self.nwaits += 1
            return
        if P_.eng == e and not X.isdma:
            if e == "pe" or e == "sp":
                return
            if not raw:
                return
            if e != "pool" and X.idx - P_.idx >= 4:
                return
        if self.seen[e].get(P_.eng, -1) >= P_.idx:
            return
        P_.signal = True
        X.waits.append(P_)
        self.seen[e][P_.eng] = P_.idx
        self.nwaits += 1

    def barrier(self):
        lasts = [self.ops[e][-1] for e in self.ENGS if e != "sp" and self.ops[e]]
        hub = _Op()
        hub.eng = "sp"
        hub.fn = lambda eng: eng.nop()
        hub.idx = len(self.ops["sp"])
        hub.waits = []
        hub.signal = False
        hub.dma = None
        hub.sig = None
        hub.isdma = False
        for P_ in lasts:
            self._need(hub, P_, True)
        for st in self.streams.values():
            if st.count > 0:
                key = ("s", st.name)
                val = 16 * st.count
                if self.seen["sp"].get(key, 0) < val:
                    hub.waits.append((st, val))
                    self.seen["sp"][key] = val
        self.ops["sp"].append(hub)
        for e in self.ENGS:
            if e == "sp":
                continue
            op = _Op()
            op.eng = e
            op.fn = lambda eng: eng.nop()
            op.idx = len(self.ops[e])
            op.waits = []
            op.signal = False
            op.dma = None
            op.sig = None
            op.isdma = False
            self._need(op, hub, True)
            self.ops[e].append(op)

    def finalize(self, nc, stack, nsig=8):
        for st in self.streams.values():
            st.sem = stack.enter_context(nc.semaphore("d_" + st.name))
        self.sig_sems = {}
        for e in self.ENGS:
            sems = [stack.enter_context(nc.semaphore("g_%s_%d" % (e, i))) for i in range(nsig)]
            self.sig_sems[e] = sems
            c = 0
            for op in self.ops[e]:
                if op.signal and op.dma is None:
                    op.sig = (sems[c % nsig], c // nsig + 1)
                    c += 1

    def emit(self, e, eng, final_streams=()):
        for op in self.ops[e]:
            for w in op.waits:
                if isinstance(w, tuple):
                    eng.wait_ge(w[0].sem, w[1])
                else:
                    eng.wait_ge(w.sig[0], w.sig[1])
            ins = op.fn(eng)
            if op.dma is not None:
                ins.then_inc(op.dma.sem, 16)
            elif op.signal:
                ins.then_inc(op.sig[0], 1)
        for name in final_streams:
            st = self.streams[name]
            eng.wait_ge(st.sem, 16 * st.count)


class Ctx:
    pass


def _build(nph=12):
    nc = bass.Bass("TRN2", target_bir_lowering=False)
    stack = ExitStack()
    C = Ctx()
    C.nc = nc
    pr = Prog()
    C.pr = pr

    def din(name, shape):
        return nc.dram_tensor(name, list(shape), F32, kind="ExternalInput").ap()

    C.x = din("x", (S, D))
    shapes = {"ffn1_w_gate": (4, D, F), "ffn1_w_up": (4, D, F), "ffn1_w_down": (4, F, D),
              "ffn2_w_gate": (4, D, F), "ffn2_w_up": (4, D, F), "ffn2_w_down": (4, F, D),
              "conv_w_pw1": (2, D, 2 * D), "conv_w_pw2": (2, D, D), "w_kv": (D, 2 * D),
              "attn_w_q": (2, D, D), "attn_w_o": (2, D, D)}

    class LazyW(dict):
        def __missing__(self, nm):
            self[nm] = din(nm, shapes[nm])
            return self[nm]
    C.w = LazyW()
    C.vecs_d = din("vecs", (128, NV))
    C.ident_d = din("ident", (128, 128))
    C.cb_d = din("cbf", (128, 640))
    C.out = nc.dram_tensor("out", [S, D], F32, kind="ExternalOutput").ap()
    skind = "ExternalOutput" if nph < 0 else "Internal"
    C.kscr = nc.dram_tensor("kscr", [128, 16384], BF16, kind=skind).ap()
    C.vscr = nc.dram_tensor("vscr", [128, 16384], BF16, kind=skind).ap()

    C.dbg = None
    if nph == -3:
        C.dbg = (nc.dram_tensor("dbg_bf", [128, 16384 + 3072], BF16, kind="ExternalOutput").ap(),
                 nc.dram_tensor("dbg_f32", [128, 512], F32, kind="ExternalOutput").ap(),
                 nc.dram_tensor("dbg_kv", [128, 32768], BF16, kind="ExternalOutput").ap())
    C.H = stack.enter_context(nc.sbuf_tensor("H", [128, NDC, S], F32))
    C.Xr = stack.enter_context(nc.sbuf_tensor("Xr", [128, 16384], BF16))
    C.AW = stack.enter_context(nc.sbuf_tensor("AW", [128, AW_BYTES // 2], BF16))
    C.vecs = stack.enter_context(nc.sbuf_tensor("vecs_sb", [128, NV], F32))
    C.ident = stack.enter_context(nc.sbuf_tensor("ident_sb", [128, 128], F32))
    C.cb = stack.enter_context(nc.sbuf_tensor("cb_sb", [128, 640], BF16))
    C.PS = [stack.enter_context(nc.psum_tensor("ps%d" % i, [128, 512], F32)) for i in range(8)]

    def carve(base, off, shape, dtype):
        n = 1
        for s_ in shape:
            n *= s_
        nb = n * (4 if dtype == F32 else 2)
        assert off % 4 == 0
        v = base[:, off // 2:(off + nb) // 2]
        if dtype == F32:
            v = v.bitcast(F32)
        if len(shape) == 2:
            v = v.rearrange("p (a b) -> p a b", a=shape[0])
        return v

    C.aw = lambda off, shape, dtype: carve(C.AW, off, shape, dtype)
    C.xr = lambda off, shape, dtype: carve(C.Xr, off, shape, dtype)
    C.onesM = C.cb[:, 0:128]
    C.ones1 = C.cb[:, 128:256]
    C.tril = C.cb[:, 256:384]
    C.triu = C.cb[:, 384:512]
    C.identb = C.cb[:, 512:640]

    def vcol(c):
        return C.vecs[:, c:c + 1]
    C.vcol = vcol

    pr.add("sp", lambda e: e.dma_start(out=C.vecs[:], in_=C.vecs_d), writes=["vecs"], dma="c_vecs")
    pr.add("sp", lambda e: e.dma_start(out=C.ident[:], in_=C.ident_d), writes=["ident"], dma="c_ident")
    pr.add("pool", lambda e: e.dma_start(out=C.cb[:], in_=C.cb_d), writes=["cb"], dma="c_cb")

    phase_load(C)
    pr.barrier()
    n = 0
    if nph < 0:
        phase_kv(C)
        pr.barrier()
        if nph < -1:
            phase_attn(C, 0)
            pr.barrier()
    for layer in range(4):
        if n >= nph:
            break
        more = (n + 1 < nph)
        if layer < 2:
            pf = (lambda C_, layer=layer: conv_load_w1(C_, layer)) if more else None
        else:
            pf = (lambda C_, layer=layer: attn_load_w(C_, layer - 2)) if more else None
        phase_ffn(C, layer, 1, prefetch=pf)
        pr.barrier()
        n += 1
        if n >= nph:
            break
        if layer < 2:
            phase_conv(C, layer, prefetched=True)
        else:
            phase_attn(C, layer - 2, prefetched=True)
        pr.barrier()
        n += 1
        if n >= nph:
            break
        do_kv = (layer == 1 and n + 1 < nph)
        phase_ffn(C, layer, 2, prefetch=(kv_load_w if do_kv else None))
        pr.barrier()
        n += 1
        if do_kv:
            phase_kv(C, prefetched=True)
            pr.barrier()
    phase_final(C)

    pr.finalize(nc, stack)
    with nc.Block() as block:
        @block.sync
        def _(e):
            pr.emit("sp", e, final_streams=("os0", "os1"))

        @block.tensor
        def _(e):
            pr.emit("pe", e)

        @block.scalar
        def _(e):
            pr.emit("act", e)

        @block.vector
        def _(e):
            pr.emit("dve", e)

        @block.gpsimd
        def _(e):
            pr.emit("pool", e)
    stack.close()
    nc._used_w = sorted(C.w.keys())
    return nc


def hk(dc, tg):
    return "h%d_%d" % (dc, tg)


def norm_tg(C, tg, gcol, out_ap, out_key, tmp, sq_eng="act", st_bank=6, out_engs=("dve",)):
    pr = C.pr
    cols = slice(tg * TG, (tg + 1) * TG)
    ps = C.PS[st_bank]
    psk = "ps%d" % st_bank
    for dc in range(NDC):
        sq, sqk = tmp["sq%d" % (dc % 2)]
        hin = C.H[:, dc, cols]
        if sq_eng == "act":
            pr.add("act", lambda e, sq=sq, hin=hin: e.activation(out=sq, in_=hin, func=AF.Square),
                   reads=[hk(dc, tg)], writes=sqk)
        else:
            pr.add(sq_eng, lambda e, sq=sq, hin=hin: e.tensor_tensor(out=sq, in0=hin, in1=hin, op=ALU.mult),
                   reads=[hk(dc, tg)], writes=sqk)
        pr.add("pe", lambda e, sq=sq, dc=dc: e.matmul(ps[:], lhsT=C.onesM, rhs=sq, start=(dc == 0), stop=(dc == NDC - 1)),
               reads=sqk + ["cb"], writes=[psk])
    lnt, lntk = tmp["lnt"]
    rs, rsk = tmp["rs"]
    pr.add("act", lambda e: e.activation(out=lnt, in_=ps[:], func=AF.Ln, bias=1e-6, scale=1.0),
           writes=[psk] + lntk)
    pr.add("act", lambda e: e.activation(out=rs, in_=lnt, func=AF.Exp, scale=-0.5),
           reads=lntk, writes=rsk)
    for dc in range(NDC):
        eng = out_engs[dc % len(out_engs)]
        o = out_ap(dc)
        hin = C.H[:, dc, cols]
        pr.add(eng, lambda e, o=o, hin=hin, dc=dc: e.scalar_tensor_tensor(
            out=o, in0=hin, scalar=C.vcol(gcol + dc), in1=rs, op0=ALU.mult, op1=ALU.mult),
            reads=[hk(dc, tg), "vecs"] + rsk, writes=out_key(dc))


def phase_load(C):
    pr = C.pr
    XS = [C.aw(0, (1024,), F32), C.aw(4096, (1024,), F32)]
    for tt in range(NTT):
        b = tt % 2
        xs = XS[b]
        pr.add("sp", lambda e, xs=xs, tt=tt: e.dma_start(out=xs, in_=C.x[tt * 128:(tt + 1) * 128, :]),
               writes=["xs%d" % b], dma="xs%d" % b)
        tg = tt // 4
        for half in range(2):
            bi = (2 * tt + half) % 4
            ps = C.PS[bi]
            for q in range(4):
                dc = half * 4 + q
                pr.add("pe", lambda e, ps=ps, q=q, xs=xs, dc=dc: e.transpose(
                    ps[:, q * 128:(q + 1) * 128], xs[:, dc * 128:(dc + 1) * 128], C.ident[:]),
                    reads=["xs%d" % b, "ident"], writes=["ps%d" % bi])
            dst = C.H[:, half * 4:(half + 1) * 4, tt * 128:(tt + 1) * 128]
            src = ps[:].rearrange("p (q t) -> p q t", q=4)
            pr.add("dve", lambda e, dst=dst, src=src: e.tensor_copy(out=dst, in_=src),
                   writes=["ps%d" % bi] + [hk(half * 4 + q, tg) for q in range(4)])


def phase_final(C):
    pr = C.pr
    YF = C.aw(0, (NDC, TG), F32)
    OS = [C.aw(32768, (1024,), F32), C.aw(36864, (1024,), F32)]
    tmp = {"sq0": (C.aw(88064, (TG,), BF16), ["sq0"]), "sq1": (C.aw(89088, (TG,), BF16), ["sq1"]),
           "lnt": (C.aw(90112, (TG,), F32), ["lnt"]), "rs": (C.aw(92160, (TG,), F32), ["rs"])}
    for tg in range(NTG):
        norm_tg(C, tg, V_FIN, lambda dc: YF[:, dc, :], lambda dc: ["yf%d" % dc], tmp)
        for tl in range(4):
            tt = tg * 4 + tl
            ob = tt % 2
            for half in range(2):
                bi = (2 * tt + half) % 4
                ps = C.PS[bi]
                for q in range(4):
                    dc = half * 4 + q
                    pr.add("pe", lambda e, ps=ps, q=q, dc=dc, tl=tl: e.transpose(
                        ps[:, q * 128:(q + 1) * 128], YF[:, dc, tl * 128:(tl + 1) * 128], C.ident[:]),
                        reads=["yf%d" % dc, "ident"], writes=["ps%d" % bi])
                dst = OS[ob][:, half * 512:(half + 1) * 512]
                eng = "dve" if half == 0 else "act"
                if eng == "dve":
                    pr.add("dve", lambda e, dst=dst, ps=ps: e.tensor_copy(out=dst, in_=ps[:]),
                           writes=["ps%d" % bi, "os%d_%d" % (ob, half)])
                else:
                    pr.add("act", lambda e, dst=dst, ps=ps: e.activation(out=dst, in_=ps[:], func=AF.Copy),
                           writes=["ps%d" % bi, "os%d_%d" % (ob, half)])
            pr.add("sp", lambda e, ob=ob, tt=tt: e.dma_start(out=C.out[tt * 128:(tt + 1) * 128, :], in_=OS[ob]),
                   reads=["os%d_0" % ob, "os%d_1" % ob], dma="os%d" % ob)


def phase_ffn(C, layer, which, prefetch=None):
    pr = C.pr
    pre = "ffn%d_" % which
    wg = C.w[pre + "w_gate"][layer]
    wu = C.w[pre + "w_up"][layer]
    wd = C.w[pre + "w_down"][layer]
    gcol = (V_FFN1 if which == 1 else V_FFN2) + layer * 8
    X = C.Xr[:, :].rearrange("p (a b) -> p a b", a=NDC)
    SLOT = 36864
    WG = [C.aw(s * SLOT, (NDC, 768), BF16) for s in range(2)]
    WU = [C.aw(s * SLOT + 12288, (NDC, 768), BF16) for s in range(2)]
    WD = [C.aw(s * SLOT + 24576, (6, 1024), BF16) for s in range(2)]
    HID = [C.aw(73728, (6, TG), BF16), C.aw(79872, (6, TG), BF16)]
    SG = [C.aw(86016, (TG,), BF16), C.aw(87040, (TG,), BF16)]
    tmp = {"sq0": (C.aw(88064, (TG,), BF16), ["sq0"]), "sq1": (C.aw(89088, (TG,), BF16), ["sq1"]),
           "lnt": (C.aw(90112, (TG,), F32), ["lnt"]), "rs": (C.aw(92160, (TG,), F32), ["rs"])}
    wgv = wg.rearrange("(dc p) f -> p dc f", p=128)
    wuv = wu.rearrange("(dc p) f -> p dc f", p=128)

    def load_slice(s):
        f0, n = SLICES[s]
        slot = s % 2
        k = "slot%d" % slot
        pr.add("pool", lambda e: e.dma_start(out=WG[slot][:, :, 0:n * 128], in_=wgv[:, :, f0 * 128:(f0 + n) * 128]),
               writes=[k], dma="w%d" % slot)
        pr.add("pool", lambda e: e.dma_start(out=WU[slot][:, :, 0:n * 128], in_=wuv[:, :, f0 * 128:(f0 + n) * 128]),
               writes=[k], dma="w%d" % slot)
        pr.add("pool", lambda e: e.dma_start(
            out=WD[slot][:, 0:n, :], in_=wd[f0 * 128:(f0 + n) * 128, :].rearrange("(fc p) d -> p fc d", p=128)),
            writes=[k], dma="w%d" % slot)

    load_slice(0)
    load_slice(1)
    for tg in range(NTG):
        cols = slice(tg * TG, (tg + 1) * TG)
        norm_tg(C, tg, gcol, lambda dc: X[:, dc, cols], lambda dc: ["x%d_%d" % (dc, tg)], tmp)

    units = [(s, tg) for s in range(len(SLICES)) for tg in range(NTG)]
    cnt = [0, 0]

    def gu(ui):
        s, tg = units[ui]
        f0, n = SLICES[s]
        slot = s % 2
        hid = HID[ui % 2]
        cols = slice(tg * TG, (tg + 1) * TG)
        for j in range(n):
            b = cnt[0] % 2
            cnt[0] += 1
            pg = C.PS[b]
            pu = C.PS[2 + b]
            for dc in range(NDC):
                pr.add("pe", lambda e, pg=pg, j=j, dc=dc: e.matmul(
                    pg[:], lhsT=WG[slot][:, dc, j * 128:(j + 1) * 128], rhs=X[:, dc, cols],
                    start=(dc == 0), stop=(dc == NDC - 1)),
                    reads=["slot%d" % slot, "x%d_%d" % (dc, tg)], writes=["ps%d" % b])
            for dc in range(NDC):
                pr.add("pe", lambda e, pu=pu, j=j, dc=dc: e.matmul(
                    pu[:], lhsT=WU[slot][:, dc, j * 128:(j + 1) * 128], rhs=X[:, dc, cols],
                    start=(dc == 0), stop=(dc == NDC - 1)),
                    reads=["slot%d" % slot, "x%d_%d" % (dc, tg)], writes=["ps%d" % (2 + b)])
            sg = SG[b]
            pr.add("act", lambda e, sg=sg, pg=pg: e.activation(out=sg, in_=pg[:], func=AF.Silu),
                   writes=["ps%d" % b, "sg%d" % b])
            pr.add("dve", lambda e, hid=hid, j=j, pu=pu, sg=sg: e.tensor_tensor(
                out=hid[:, j, :], in0=pu[:], in1=sg, op=ALU.mult),
                reads=["sg%d" % b], writes=["ps%d" % (2 + b), "hid%d_%d" % (ui % 2, j)])

    def down(ui):
        s, tg = units[ui]
        f0, n = SLICES[s]
        slot = s % 2
        hid = HID[ui % 2]
        cols = slice(tg * TG, (tg + 1) * TG)
        for dc in range(NDC):
            b = 4 + cnt[1] % 2
            cnt[1] += 1
            pd = C.PS[b]
            for j in range(n):
                pr.add("pe", lambda e, pd=pd, j=j, dc=dc: e.matmul(
                    pd[:], lhsT=WD[slot][:, j, dc * 128:(dc + 1) * 128], rhs=hid[:, j, :],
                    start=(j == 0), stop=(j == n - 1)),
                    reads=["slot%d" % slot, "hid%d_%d" % (ui % 2, j)], writes=["ps%d" % b])
            hh = C.H[:, dc, cols]
            pr.add("dve", lambda e, pd=pd, hh=hh: e.scalar_tensor_tensor(
                out=hh, in0=pd[:], scalar=0.5, in1=hh, op0=ALU.mult, op1=ALU.add),
                reads=[hk(dc, tg)], writes=["ps%d" % b, hk(dc, tg)])

    gu(0)
    for ui in range(len(units)):
        if ui + 1 < len(units):
            gu(ui + 1)
        down(ui)
        s, tg = units[ui]
        if tg == NTG - 1 and s + 2 < len(SLICES):
            load_slice(s + 2)
        if tg == NTG - 1 and s == len(SLICES) - 2 and prefetch is not None:
            assert s % 2 == 1
            prefetch(C)


def conv_load_w1(C, i):
    w1 = C.w["conv_w_pw1"][i].rearrange("(dc p) f -> p dc f", p=128)
    W1 = C.aw(36864, (NDC, 2048), BF16)
    for hf in range(4):
        C.pr.add("pool", lambda e, hf=hf: e.dma_start(out=W1[:, 2 * hf:2 * hf + 2, :], in_=w1[:, 2 * hf:2 * hf + 2, :]),
                 writes=["slot1", "w1"], dma="cw1")


def phase_conv(C, i, prefetched=False):
    pr = C.pr
    w1 = C.w["conv_w_pw1"][i].rearrange("(dc p) f -> p dc f", p=128)
    w2 = C.w["conv_w_pw2"][i].rearrange("(dc p) f -> p dc f", p=128)
    vb = V_CONV + i * 296
    c_b1a, c_b1g, c_bdw, c_lng, c_lnb, c_b2, c_wdw = vb, vb + 8, vb + 16, vb + 24, vb + 32, vb + 40, vb + 48
    gcol = V_MIX + i * 8
    W1 = C.aw(36864, (NDC, 2048), BF16)
    W2 = C.aw(69632, (NDC, 1024), BF16)
    G = C.aw(0, (NDC, 542), BF16)
    Y = C.aw(8704, (NDC, TG), F32)
    YSQ = C.aw(25088, (NDC, TG), BF16)
    SIG = [C.aw(33280, (TG,), F32), C.aw(33280, (TG,), F32)]
    LNT = C.aw(86016, (TG,), F32)
    RS = C.aw(88064, (TG,), F32)
    MEAN = C.aw(90112, (TG,), F32)
    MSQ = C.aw(92160, (TG,), F32)
    NMR = LNT
    DG = [C.aw(94208, (16, 128), BF16), C.aw(98304, (16, 128), BF16), C.aw(102400, (16, 128), BF16),
          C.xr(26624, (16, 128), BF16)]
    UT = C.xr(0, (NDC, TG), BF16)
    SS = C.xr(8192, (NDC, TG), BF16)
    YB = C.xr(16384, (NDC, TG), BF16)
    tmp = {"sq0": (C.xr(24576, (TG,), BF16), ["sq0"]), "sq1": (C.xr(25600, (TG,), BF16), ["sq1"]),
           "lnt": (LNT, ["lnt"]), "rs": (RS, ["rs"])}
    if not prefetched:
        conv_load_w1(C, i)
    for hf in range(2):
        pr.add("pool", lambda e, hf=hf: e.dma_start(out=W2[:, 4 * hf:4 * hf + 4, :], in_=w2[:, 4 * hf:4 * hf + 4, :]),
               writes=["w2"], dma="cw2")
    pr.add("dve", lambda e: e.memset(G[:, :, 0:30], 0.0), writes=["g%d" % c for c in range(NDC)])
    identb3 = C.identb.unsqueeze(1)
    wdw3 = C.vecs[:, c_wdw:c_wdw + 8 * CW].rearrange("p (c k) -> p c k", c=NDC)
    dgcnt = 0
    ccnt = 0
    for tg in range(NTG):
        cols = slice(tg * TG, (tg + 1) * TG)
        norm_tg(C, tg, gcol, lambda dc: UT[:, dc, :], lambda dc: ["ut%d" % dc], tmp, st_bank=6)
        def pw1_glu(c):
            b = c % 2
            pa = C.PS[b]
            pg = C.PS[2 + b]
            for dc in range(NDC):
                pr.add("pe", lambda e, pa=pa, c=c, dc=dc: e.matmul(
                    pa[:], lhsT=W1[:, dc, c * 128:(c + 1) * 128], rhs=UT[:, dc, :],
                    start=(dc == 0), stop=(dc == NDC - 1)),
                    reads=["w1", "ut%d" % dc], writes=["ps%d" % b])
            for dc in range(NDC):
                pr.add("pe", lambda e, pg=pg, c=c, dc=dc: e.matmul(
                    pg[:], lhsT=W1[:, dc, 1024 + c * 128:1024 + (c + 1) * 128], rhs=UT[:, dc, :],
                    start=(dc == 0), stop=(dc == NDC - 1)),
                    reads=["w1", "ut%d" % dc], writes=["ps%d" % (2 + b)])
            sig = SIG[b]
            pr.add("act", lambda e, sig=sig, pg=pg, c=c: e.activation(
                out=sig, in_=pg[:], func=AF.Sigmoid, bias=C.vcol(c_b1g + c), scale=1.0),
                reads=["vecs"], writes=["ps%d" % (2 + b), "sig"])
            pr.add("dve", lambda e, pa=pa, sig=sig, c=c: e.scalar_tensor_tensor(
                out=G[:, c, 30:542], in0=pa[:], scalar=C.vcol(c_b1a + c), in1=sig, op0=ALU.add, op1=ALU.mult),
                reads=["sig", "vecs"], writes=["ps%d" % b, "g%d" % c])

        def diag_build(c):
            for half in range(2):
                k0 = 16 * half
                nk = 16 if half == 0 else CW - 16
                dg = DG[2 * (c % 2) + half]
                dgk = "dg%d" % (2 * (c % 2) + half)
                in0 = identb3.broadcast_to([128, nk, 128])
                in1 = wdw3[:, c, k0:k0 + nk].unsqueeze(2).broadcast_to([128, nk, 128])
                pr.add("dve", lambda e, dg=dg, nk=nk, in0=in0, in1=in1: e.tensor_tensor(
                    out=dg[:, 0:nk, :], in0=in0, in1=in1, op=ALU.mult),
                    reads=["cb", "vecs"], writes=[dgk])

        def taps(c, ccnt, dgcnt):
            pc = C.PS[4 + ccnt % 2]
            pck = "ps%d" % (4 + ccnt % 2)
            for half in range(2):
                k0 = 16 * half
                nk = 16 if half == 0 else CW - 16
                dg = DG[2 * (c % 2) + half]
                dgk = "dg%d" % (2 * (c % 2) + half)
                for j in range(nk):
                    k = k0 + j
                    pr.add("pe", lambda e, pc=pc, dg=dg, j=j, k=k, c=c: e.matmul(
                        pc[:], lhsT=dg[:, j, :], rhs=G[:, c, k:k + TG], start=(k == 0), stop=(k == CW - 1)),
                        reads=[dgk, "g%d" % c], writes=[pck])
            pr.add("act", lambda e, pc=pc, c=c: e.activation(
                out=Y[:, c, :], in_=pc[:], func=AF.Identity, bias=C.vcol(c_bdw + c), scale=1.0),
                reads=["vecs"], writes=[pck, "y%d" % c])
            pr.add("act", lambda e, pc=pc, c=c: e.activation(
                out=YB[:, c, :], in_=pc[:], func=AF.Identity, bias=C.vcol(c_bdw + c), scale=1.0),
                reads=["vecs"], writes=[pck, "yb%d" % c])
            pr.add("act", lambda e, pc=pc, c=c: e.activation(
                out=YSQ[:, c, :], in_=pc[:], func=AF.Square, bias=C.vcol(c_bdw + c), scale=1.0),
                reads=["vecs"], writes=[pck, "ysq%d" % c])
            pr.add("dve", lambda e, c=c: e.tensor_copy(out=G[:, c, 0:30], in_=G[:, c, 512:542]),
                   reads=["g%d" % c], writes=["g%d" % c])

        for c in range(NDC + 1):
            if c < NDC:
                diag_build(c)
                pw1_glu(c)
            if c >= 1:
                taps(c - 1, ccnt, dgcnt)
                ccnt += 1
                dgcnt += 2
        for c in range(NDC):
            pr.add("pe", lambda e, c=c: e.matmul(C.PS[6][:], lhsT=C.onesM, rhs=YB[:, c, :],
                                                 start=(c == 0), stop=(c == NDC - 1)),
                   reads=["yb%d" % c, "cb"], writes=["ps6"])
        for c in range(NDC):
            pr.add("pe", lambda e, c=c: e.matmul(C.PS[7][:], lhsT=C.onesM, rhs=YSQ[:, c, :],
                                                 start=(c == 0), stop=(c == NDC - 1)),
                   reads=["ysq%d" % c, "cb"], writes=["ps7"])
        pr.add("dve", lambda e: e.tensor_copy(out=MEAN, in_=C.PS[6][:]), writes=["ps6", "mean"])
        pr.add("dve", lambda e: e.tensor_tensor(out=MSQ, in0=MEAN, in1=MEAN, op=ALU.mult),
               reads=["mean"], writes=["msq"])
        pr.add("dve", lambda e: e.tensor_tensor(out=MSQ, in0=C.PS[7][:], in1=MSQ, op=ALU.subtract),
               reads=["msq"], writes=["ps7", "msq"])
        pr.add("act", lambda e: e.activation(out=LNT, in_=MSQ, func=AF.Ln, bias=1e-5, scale=1.0),
               reads=["msq"], writes=["lnt"])
        pr.add("act", lambda e: e.activation(out=RS, in_=LNT, func=AF.Exp, scale=-0.5),
               reads=["lnt"], writes=["rs"])
        pr.add("dve", lambda e: e.scalar_tensor_tensor(out=NMR, in0=MEAN, scalar=-1.0, in1=RS, op0=ALU.mult, op1=ALU.mult),
               reads=["mean", "rs"], writes=["lnt"])
        for c in range(NDC):
            pr.add("dve", lambda e, c=c: e.tensor_tensor(out=Y[:, c, :], in0=Y[:, c, :], in1=RS, op=ALU.mult),
                   reads=["y%d" % c, "rs"], writes=["y%d" % c])
        for c in range(NDC):
            pr.add("dve", lambda e, c=c: e.tensor_tensor(out=Y[:, c, :], in0=Y[:, c, :], in1=NMR, op=ALU.add),
                   reads=["y%d" % c, "lnt"], writes=["y%d" % c])
        for c in range(NDC):
            pr.add("act", lambda e, c=c: e.activation(
                out=SS[:, c, :], in_=Y[:, c, :], func=AF.Silu, bias=C.vcol(c_lnb + c), scale=C.vcol(c_lng + c)),
                reads=["y%d" % c, "vecs"], writes=["s%d" % c])
        for dc in range(NDC):
            b = dc % 2
            pd = C.PS[b]
            for c in range(NDC):
                pr.add("pe", lambda e, pd=pd, c=c, dc=dc: e.matmul(
                    pd[:], lhsT=W2[:, c, dc * 128:(dc + 1) * 128], rhs=SS[:, c, :],
                    start=(c == 0), stop=(c == NDC - 1)),
                    reads=["w2", "s%d" % c], writes=["ps%d" % b])
            hh = C.H[:, dc, cols]
            pr.add("dve", lambda e, pd=pd, hh=hh, dc=dc: e.scalar_tensor_tensor(
                out=hh, in0=pd[:], scalar=C.vcol(c_b2 + dc), in1=hh, op0=ALU.add, op1=ALU.add),
                reads=[hk(dc, tg), "vecs"], writes=["ps%d" % b, hk(dc, tg)])


def kv_load_w(C):
    wkv = C.w["w_kv"].rearrange("(dc p) f -> p dc f", p=128)
    WKV = C.aw(36864, (NDC, 2048), BF16)
    for hf in range(4):
        C.pr.add("pool", lambda e, hf=hf: e.dma_start(out=WKV[:, 2 * hf:2 * hf + 2, :], in_=wkv[:, 2 * hf:2 * hf + 2, :]),
                 writes=["slot1", "wkv"], dma="wkv")


def phase_kv(C, prefetched=False):
    pr = C.pr
    wkv = C.w["w_kv"].rearrange("(dc p) f -> p dc f", p=128)
    KT = C.aw(69632, (NDC, S), BF16)
    V = C.aw(0, (NTT, D), BF16)
    tmp = {"lnt": (C.aw(32768, (TG,), F32), ["lnt"]), "rs": (C.aw(34816, (TG,), F32), ["rs"]),
           "sq0": (C.aw(102400, (TG,), BF16), ["sq0"]), "sq1": (C.aw(103424, (TG,), BF16), ["sq1"])}
    WKV = C.aw(36864, (NDC, 2048), BF16)
    UT = C.xr(0, (NDC, TG), BF16)
    if not prefetched:
        kv_load_w(C)
    cnt = 0
    for tg in range(NTG):
        cols = slice(tg * TG, (tg + 1) * TG)
        norm_tg(C, tg, V_KV, lambda dc: UT[:, dc, :], lambda dc: ["ut%d" % dc], tmp)
        for c in range(NDC):
            b = cnt % 4
            cnt += 1
            ps = C.PS[b]
            for dc in range(NDC):
                pr.add("pe", lambda e, ps=ps, c=c, dc=dc: e.matmul(
                    ps[:], lhsT=WKV[:, dc, c * 128:(c + 1) * 128], rhs=UT[:, dc, :],
                    start=(dc == 0), stop=(dc == NDC - 1)),
                    reads=["wkv", "ut%d" % dc], writes=["ps%d" % b])
            kdst = KT[:, c, cols]
            pr.add("act", lambda e, ps=ps, kdst=kdst: e.activation(out=kdst, in_=ps[:], func=AF.Copy),
                   writes=["ps%d" % b, "K"])
        for tl in range(4):
            tt = tg * 4 + tl
            for half in range(2):
                b = cnt % 4
                cnt += 1
                ps = C.PS[b]
                for dc in range(NDC):
                    pr.add("pe", lambda e, ps=ps, dc=dc, tl=tl, half=half: e.matmul(
                        ps[:], lhsT=UT[:, dc, tl * 128:(tl + 1) * 128],
                        rhs=WKV[:, dc, 1024 + half * 512:1024 + (half + 1) * 512],
                        start=(dc == 0), stop=(dc == NDC - 1)),
                        reads=["wkv", "ut%d" % dc], writes=["ps%d" % b])
                pr.add("dve", lambda e, ps=ps, tt=tt, half=half: e.tensor_copy(
                    out=V[:, tt, half * 512:(half + 1) * 512], in_=ps[:]),
                    writes=["ps%d" % b, "V"])
    KTf = C.AW[:, 34816:51200]
    Vf = C.AW[:, 0:16384]
    for q in range(4):
        pr.add("sp", lambda e, q=q: e.dma_start(out=C.kscr[:, q * 4096:(q + 1) * 4096], in_=KTf[:, q * 4096:(q + 1) * 4096]),
               reads=["K"], dma="kvs")
    for q in range(4):
        pr.add("sp", lambda e, q=q: e.dma_start(out=C.vscr[:, q * 4096:(q + 1) * 4096], in_=Vf[:, q * 4096:(q + 1) * 4096]),
               reads=["V"], dma="kvs")


def attn_load_w(C, i):
    wq = C.w["attn_w_q"][i].rearrange("(dc p) f -> p dc f", p=128)
    wo = C.w["attn_w_o"][i].rearrange("(dc p) f -> p dc f", p=128)
    WQ = C.aw(36864, (NDC, 1024), BF16)
    WO = C.aw(53248, (NDC, 1024), BF16)
    for hf in range(2):
        C.pr.add("pool", lambda e, hf=hf: e.dma_start(out=WQ[:, 4 * hf:4 * hf + 4, :], in_=wq[:, 4 * hf:4 * hf + 4, :]),
                 writes=["slot1", "wq"], dma="wq")
    for hf in range(2):
        C.pr.add("pool", lambda e, hf=hf: e.dma_start(out=WO[:, 4 * hf:4 * hf + 4, :], in_=wo[:, 4 * hf:4 * hf + 4, :]),
                 writes=["slot1", "wo"], dma="wo")


def phase_attn(C, i, prefetched=False):
    pr = C.pr
    wq = C.w["attn_w_q"][i].rearrange("(dc p) f -> p dc f", p=128)
    wo = C.w["attn_w_o"][i].rearrange("(dc p) f -> p dc f", p=128)
    gcol = V_MIX + (2 + i) * 8
    KT = C.aw(0, (NDC, S), BF16)
    V = C.aw(69632, (NTT, D), BF16)
    WQ = C.aw(36864, (NDC, 1024), BF16)
    WO = C.aw(53248, (NDC, 1024), BF16)
    Eb = [C.aw(32768, (TG,), F32), C.aw(34816, (TG,), F32), C.aw(102400, (TG,), F32)]
    Lb = [C.xr(4096, (TG,), BF16), C.xr(5120, (TG,), BF16)]
    lk = [["ut4"], ["ut5"]]
    Wb = [C.aw(104448, (TG,), BF16), C.aw(105472, (TG,), BF16), C.aw(106496, (TG,), BF16)]
    SS2 = [C.xr(6144, (TG,), BF16), C.xr(7168, (TG,), BF16)]
    ssk2 = ["ut6", "ut7"]
    UT = C.xr(0, (NDC, TG), BF16)
    QE = C.xr(8192, (NDC, TG), BF16)
    QO = C.xr(16384, (NDC, TG), BF16)
    OT = C.xr(24576, (NDC, TG), BF16)
    XRb = [C.xr(0, (TG,), F32), C.xr(2048, (TG,), F32)]
    xrk = [["ut0", "ut1"], ["ut2", "ut3"]]
    tmp = {"sq0": (C.xr(16384, (TG,), BF16), ["qo0"]), "sq1": (C.xr(17408, (TG,), BF16), ["qo1"]),
           "lnt": (C.xr(18432, (TG,), F32), ["qo2", "qo3"]), "rs": (C.xr(20480, (TG,), F32), ["qo4", "qo5"])}
    KTf = C.AW[:, 0:16384]
    Vf = C.AW[:, 34816:51200]
    for q in range(4):
        pr.add("sp", lambda e, q=q: e.dma_start(out=KTf[:, q * 4096:(q + 1) * 4096], in_=C.kscr[:, q * 4096:(q + 1) * 4096]),
               writes=["K"], dma="kld")
    for q in range(4):
        pr.add("sp", lambda e, q=q: e.dma_start(out=Vf[:, q * 4096:(q + 1) * 4096], in_=C.vscr[:, q * 4096:(q + 1) * 4096]),
               writes=["V"], dma="vld")
    if not prefetched:
        attn_load_w(C, i)
    qcnt = 0
    for tg in range(NTG):
        cols = slice(tg * TG, (tg + 1) * TG)
        norm_tg(C, tg, gcol, lambda dc: UT[:, dc, :], lambda dc: ["ut%d" % dc], tmp, sq_eng="dve", st_bank=3)
        pr.add(AENG, lambda e: e.memset(QE[64:128, :, :], 0.0), writes=["qe%d" % c for c in range(NDC)])
        pr.add(AENG, lambda e: e.memset(QO[0:64, :, :], 0.0), writes=["qo%d" % c for c in range(NDC)])
        for c in range(NDC):
            b = qcnt % 3
            qcnt += 1
            ps = C.PS[b]
            for dc in range(NDC):
                pr.add("pe", lambda e, ps=ps, c=c, dc=dc: e.matmul(
                    ps[:], lhsT=WQ[:, dc, c * 128:(c + 1) * 128], rhs=UT[:, dc, :],
                    start=(dc == 0), stop=(dc == NDC - 1)),
                    reads=["wq", "ut%d" % dc], writes=["ps%d" % b])
            pr.add("dve", lambda e, ps=ps, c=c: e.tensor_scalar(
                out=QE[0:64, c, :], in0=ps[0:64, :], scalar1=0.125, scalar2=None, op0=ALU.mult),
                writes=["ps%d" % b, "qe%d" % c])
            pr.add("dve", lambda e, ps=ps, c=c: e.tensor_scalar(
                out=QO[64:128, c, :], in0=ps[64:128, :], scalar1=0.125, scalar2=None, op0=ALU.mult),
                writes=["ps%d" % b, "qo%d" % c])
        kmax = 4 * tg + 3
        items = []
        for c in range(NDC):
            for j2 in range(2):
                for kb in range(kmax, -1, -1):
                    items.append((c, j2, kb))
        n = len(items)

        def info(it):
            c, j2, kb = items[it]
            c0 = (kb - 4 * tg) * 128 if kb >= 4 * tg else 0
            return c, j2, kb, c0

        def z_pe(it):
            c, j2, kb, c0 = info(it)
            b = it % 2
            pz = C.PS[b]
            Q = QE if j2 == 0 else QO
            qk = ("qe%d" if j2 == 0 else "qo%d") % c
            pr.add("pe", lambda e: e.matmul(pz[:, c0:TG], lhsT=KT[:, c, kb * 128:(kb + 1) * 128], rhs=Q[:, c, c0:TG],
                                            start=True, stop=True),
                   reads=["K", qk], writes=["ps%d" % b])

        def exp_act(it):
            c, j2, kb, c0 = info(it)
            b = it % 2
            pz = C.PS[b]
            E = Eb[it % 3]
            pr.add("act", lambda e: e.activation(out=E[:, c0:TG], in_=pz[:, c0:TG], func=AF.Exp),
                   writes=["ps%d" % b, "E%d" % (it % 3)])

        def ln_act(it):
            c, j2, kb, c0 = info(it)
            L = Lb[it % 2]
            E = Eb[it % 3]
            pr.add("act", lambda e: e.activation(out=L[:, c0:TG], in_=E[:, c0:TG], func=AF.Ln, bias=1.0, scale=1.0),
                   reads=["E%d" % (it % 3)], writes=lk[it % 2])

        def maskl_dve(it):
            c, j2, kb, c0 = info(it)
            L = Lb[it % 2]
            if kb >= 4 * tg:
                pr.add("dve", lambda e: e.tensor_tensor(out=L[:, c0:c0 + 128], in0=L[:, c0:c0 + 128], in1=C.triu, op=ALU.mult),
                       reads=lk[it % 2] + ["cb"], writes=lk[it % 2])

        def r_pe(it):
            c, j2, kb, c0 = info(it)
            b = 2 + it % 2
            pb = C.PS[b]
            L = Lb[it % 2]
            first = (kb == 4 * tg + 3)
            SSt = SS2[j2]
            ssk = ssk2[j2]
            pr.add("pe", lambda e: e.matmul(pb[:, c0:TG], lhsT=C.tril, rhs=L[:, c0:TG], start=True, stop=first),
                   reads=lk[it % 2] + ["cb"], writes=["ps%d" % b])
            if not first:
                pr.add("pe", lambda e: e.matmul(pb[:, c0:TG], lhsT=C.ones1, rhs=SSt[:, c0:TG], start=False, stop=True),
                       reads=[ssk, "cb"], writes=["ps%d" % b])

        def ssum_dve(it):
            c, j2, kb, c0 = info(it)
            L = Lb[it % 2]
            first = (kb == 4 * tg + 3)
            SSt = SS2[j2]
            ssk = ssk2[j2]
            if first:
                pr.add("dve", lambda e: e.tensor_copy(out=SSt[:, c0:TG], in_=L[:, c0:TG]), reads=lk[it % 2], writes=[ssk])
                if c0 > 0:
                    pr.add("dve", lambda e: e.memset(SSt[:, 0:c0], 0.0), writes=[ssk])
            elif kb > 0:
                pr.add("dve", lambda e: e.tensor_tensor(out=SSt[:, c0:TG], in0=SSt[:, c0:TG], in1=L[:, c0:TG], op=ALU.add),
                       reads=[ssk] + lk[it % 2], writes=[ssk])

        def xr_act(it):
            c, j2, kb, c0 = info(it)
            b = 2 + it % 2
            pb = C.PS[b]
            XR = XRb[it % 2]
            pr.add("act", lambda e: e.activation(out=XR[:, c0:TG], in_=pb[:, c0:TG], func=AF.Exp, scale=-1.0),
                   writes=["ps%d" % b] + xrk[it % 2])

        def w_dve(it):
            c, j2, kb, c0 = info(it)
            E = Eb[it % 3]
            XR = XRb[it % 2]
            Wt = Wb[it % 3]
            wk = "wt%d" % (it % 3)
            pr.add("dve", lambda e: e.tensor_tensor(out=Wt[:, c0:TG], in0=E[:, c0:TG], in1=XR[:, c0:TG], op=ALU.mult),
                   reads=["E%d" % (it % 3)] + xrk[it % 2], writes=[wk])
            if kb >= 4 * tg:
                pr.add("dve", lambda e: e.tensor_tensor(out=Wt[:, c0:c0 + 128], in0=Wt[:, c0:c0 + 128], in1=C.triu, op=ALU.mult),
                       reads=[wk, "cb"], writes=[wk])
                if c0 > 0:
                    pr.add("dve", lambda e: e.memset(Wt[:, 0:c0], 0.0), writes=[wk])

        def pv_pe(it, tg_=tg):
            c, j2, kb, c0 = info(it)
            hp = slice(j2 * 64, (j2 + 1) * 64)
            b = 4 + 2 * (c % 2) + j2
            po = C.PS[b]
            Wt = Wb[it % 3]
            pv_first = (kb == 4 * tg_ + 3)
            pr.add("pe", lambda e: e.matmul(po[:], lhsT=V[:, kb, c * 128:(c + 1) * 128], rhs=Wt[:, :],
                                            start=pv_first, stop=(kb == 0)),
                   reads=["V", "wt%d" % (it % 3)], writes=["ps%d" % b])
            if kb == 0:
                pr.add("dve", lambda e: e.tensor_copy(out=OT[hp, c, :], in_=po[hp, :]),
                       writes=["ps%d" % b, "ot%d" % c])

        for t in range(n + 3):
            if t < n:
                for _j in range(NJUNK + 1):
                    z_pe(t)
                exp_act(t)
                ln_act(t)
                maskl_dve(t)
            if 0 <= t - 1 < n:
                r_pe(t - 1)
                ssum_dve(t - 1)
                xr_act(t - 1)
                w_dve(t - 1)
            if 0 <= t - 3 < n:
                pv_pe(t - 3)
        if C.dbg is not None and tg == 0:
            pr.barrier()
            pr.add("sp", lambda e: e.dma_start(out=C.dbg[0][:, 0:16384], in_=C.Xr[:, 0:16384]), dma="dbg")
            pr.add("sp", lambda e: e.dma_start(out=C.dbg[0][:, 16384:16384 + 3072], in_=C.AW[:, 100352 // 2:106496 // 2]),
                   dma="dbg")
            pr.add("sp", lambda e: e.dma_start(out=C.dbg[1][:, :], in_=Eb[0]), dma="dbg")
            pr.add("sp", lambda e: e.dma_start(out=C.dbg[2][:, :], in_=C.AW[:, 0:32768]), dma="dbg")
            pr.barrier()
        for dc in range(NDC):
            b = qcnt % 3
            qcnt += 1
            ps = C.PS[b]
            for c in range(NDC):
                pr.add("pe", lambda e, ps=ps, c=c, dc=dc: e.matmul(
                    ps[:], lhsT=WO[:, c, dc * 128:(dc + 1) * 128], rhs=OT[:, c, :],
                    start=(c == 0), stop=(c == NDC - 1)),
                    reads=["wo", "ot%d" % c], writes=["ps%d" % b])
            hh = C.H[:, dc, cols]
            pr.add("dve", lambda e, ps=ps, hh=hh: e.tensor_tensor(out=hh, in0=ps[:], in1=hh, op=ALU.add),
                   reads=[hk(dc, tg)], writes=["ps%d" % b, hk(dc, tg)])


def _fm(v):
    return np.ascontiguousarray(np.asarray(v, dtype=np.float32).reshape(8, 128).T)


def _host_tables(inp):
    vecs = np.zeros((128, NV), dtype=np.float32)
    for l in range(4):
        vecs[:, V_FFN1 + l * 8:V_FFN1 + l * 8 + 8] = _fm(inp["ffn1_norm"][l])
        vecs[:, V_MIX + l * 8:V_MIX + l * 8 + 8] = _fm(inp["mix_norm"][l])
        vecs[:, V_FFN2 + l * 8:V_FFN2 + l * 8 + 8] = _fm(inp["ffn2_norm"][l])
    vecs[:, V_KV:V_KV + 8] = _fm(inp["kv_norm"])
    vecs[:, V_FIN:V_FIN + 8] = _fm(inp["final_norm"])
    for i in range(2):
        vb = V_CONV + i * 296
        b1 = np.asarray(inp["conv_b_pw1"][i], dtype=np.float32)
        vecs[:, vb:vb + 8] = _fm(b1[:D])
        vecs[:, vb + 8:vb + 16] = _fm(b1[D:])
        vecs[:, vb + 16:vb + 24] = _fm(inp["conv_b_dw"][i])
        vecs[:, vb + 24:vb + 32] = _fm(inp["conv_ln_g"][i])
        vecs[:, vb + 32:vb + 40] = _fm(inp["conv_ln_b"][i])
        vecs[:, vb + 40:vb + 48] = _fm(inp["conv_b_pw2"][i])
        wdw = np.asarray(inp["conv_w_dw"][i], dtype=np.float32)
        vecs[:, vb + 48:vb + 296] = wdw.T.reshape(8, 128, CW).transpose(1, 0, 2).reshape(128, 8 * CW)
    ident = np.eye(128, dtype=np.float32)
    cb = np.zeros((128, 640), dtype=np.float32)
    cb[:, 0:128] = 1.0 / 1024.0
    cb[:, 128:256] = 1.0
    r = np.arange(128)
    cb[:, 256:384] = (r[:, None] >= r[None, :]).astype(np.float32)
    cb[:, 384:512] = (r[None, :] > r[:, None]).astype(np.float32)
    cb[:, 512:640] = np.eye(128, dtype=np.float32)
    return vecs, ident, cb


_NC_CACHE = {}
NPH = 12


def kernel(**inputs):
    inp = {k: np.asarray(v) for k, v in inputs.items()}
    vecs, ident, cb = _host_tables(inp)
    if NPH not in _NC_CACHE:
        _NC_CACHE[NPH] = _build(NPH)
    nc = _NC_CACHE[NPH]
    shared = {"vecs": vecs, "ident": ident, "cbf": cb}
    for nm in nc._used_w:
        shared[nm] = np.ascontiguousarray(inp[nm], dtype=np.float32)
    x = np.ascontiguousarray(inp["x"], dtype=np.float32)
    in_maps = []
    for b in range(NCORES):
        m = dict(shared)
        m["x"] = x[b]
        in_maps.append(m)
    res = run_bass_kernel_spmd(nc, in_maps, core_ids=list(range(NCORES)), **({"trace": True} if TRACE else {}))
    if TRACE:
        print("EXEC_TIME_NS", res.exec_time_ns)
    out = np.stack([np.asarray(res.results[b]["out"], dtype=np.float32) for b in range(NCORES)], axis=0)
    if NPH < 0:
        kernel.dbg = res.results
    return out
```
